# Optimizing a Trainium2 kernel written in Bass

```python
import math
import jax, jax.numpy as jnp
from jax import lax
import numpy as np

D_MODEL = 1024
BATCH = 8
SEQ = 2048
DEPTH = 4
DEC_BATCH = 128
DEC_SEQ = 4
PAST_LEN = 16384
PAGE_SIZE = 128

A_WIDTH = D_MODEL // 2
A_HEAD_DIM = 64
A_HEADS = A_WIDTH // A_HEAD_DIM
A_DECAY_LORA = 64
A_AAA_LORA = 64
A_MV_LORA = 32
A_GATE_LORA = 128
GN_EPS_A = 64e-5
B_WIDTH = D_MODEL - A_WIDTH
B_HEADS = 4
B_DV = B_WIDTH // B_HEADS
B_DK = B_DV // 2
B_KW = B_HEADS * B_DK
B_GATE_LORA = 16
GLA_LOGIT_NORM = 16.0
GLA_CHUNK = 64
RMS_EPS = 1e-5
D_FF = 2816
CONV_W = 3
ALPHA = (2 * DEPTH) ** 0.25
BETA = (8 * DEPTH) ** -0.25
LN_EPS = 1e-5
SHIFT_W = 3 * A_WIDTH + A_DECAY_LORA + A_AAA_LORA + A_GATE_LORA
GLA_IN_W = 2 * B_KW + 2 * B_WIDTH + B_GATE_LORA
IN_W = SHIFT_W + GLA_IN_W

kernel_name = 'hybrid_rwkv7_gla_convglu_step'


def split_cols(t, sizes):
    idx = np.cumsum(sizes)[:-1].tolist()
    return jnp.split(t, idx, axis=-1)


def layer_norm(x, g, b):
    xf = x.astype(jnp.float32)
    mu = jnp.mean(xf, -1, keepdims=True)
    var = jnp.mean(jnp.square(xf - mu), -1, keepdims=True)
    return ((xf - mu) * lax.rsqrt(var + LN_EPS) * g + b).astype(x.dtype)


def rwkv7_group(pa, pa_prev, S0, v_first, vmix, mu, w0, w_up, a0, a_up, g_up, k_k, k_a, r_k, lnx_g, lnx_b):
    f32 = jnp.float32
    pa = pa.astype(f32)
    B, T, _ = pa.shape
    prev = jnp.concatenate([pa_prev.astype(f32), pa[:, :-1]], axis=1)
    xs = pa + (prev - pa) * mu
    r, k, v, w_lo, a_lo, g_lo = split_cols(xs, [A_WIDTH, A_WIDTH, A_WIDTH, A_DECAY_LORA, A_AAA_LORA, A_GATE_LORA])
    w = -jax.nn.softplus(-(w0 + jnp.tanh(w_lo) @ w_up)) - 0.5
    decay = jnp.exp(-jnp.exp(w))
    a = jax.nn.sigmoid(a0 + a_lo @ a_up)
    g = jax.nn.sigmoid(g_lo) @ g_up
    if vmix is None:
        v_first = v
    else:
        v0, v1, v2 = vmix
        v = v + (v_first - v) * jax.nn.sigmoid(v0 + (v @ v1) @ v2)
    heads = lambda t: t.reshape(B, T, A_HEADS, A_HEAD_DIM)
    kk = heads(k * k_k)
    kk = kk * lax.rsqrt(jnp.maximum(jnp.sum(kk * kk, -1, keepdims=True), 1e-24))
    k = k * (1.0 + (a - 1.0) * k_a)
    rh, kh, vh, dh, ah = heads(r), heads(k), heads(v), heads(decay), heads(a)

    def step(S, inp):
        r_t, d_t, k_t, v_t, kk_t, a_t = inp
        sa = jnp.einsum('bhvk,bhk->bhv', S, -kk_t)
        S = (S * d_t[:, :, None, :] + sa[..., None] * (kk_t * a_t)[:, :, None, :]
             + v_t[..., None] * k_t[:, :, None, :])
        return S, jnp.einsum('bhvk,bhk->bhv', S, r_t)

    tm = lambda t: jnp.moveaxis(t, 1, 0)
    S, o = lax.scan(step, S0.astype(f32), (tm(rh), tm(dh), tm(kh), tm(vh), tm(kk), tm(ah)))
    o = jnp.moveaxis(o, 0, 1)
    m = jnp.mean(o, -1, keepdims=True)
    var = jnp.mean(jnp.square(o - m), -1, keepdims=True)
    o = ((o - m) * lax.rsqrt(var + GN_EPS_A)).reshape(B, T, A_WIDTH) * lnx_g + lnx_b
    bonus = (jnp.sum(rh * kh * r_k, -1, keepdims=True) * vh).reshape(B, T, A_WIDTH)
    return (o + bonus) * g, S, pa[:, -1:], v_first


def gla_chunked(q, k, v, gk, S0):
    B, T, H, K = q.shape
    V = v.shape[-1]
    C = math.gcd(T, GLA_CHUNK)
    NC = T // C
    chunks = lambda t: t.reshape(B, NC, C, H, t.shape[-1]).swapaxes(0, 1)
    mask = jnp.tril(jnp.ones((C, C), dtype=bool))

    def step(S, inp):
        qc, kc, vc, gc = inp
        b = jnp.cumsum(gc, axis=1)
        qt = qc * jnp.exp(b)
        kt = kc * jnp.exp(-b)
        A = jnp.where(mask, jnp.einsum('bthk,bshk->bhts', qt, kt), 0.0)
        o = jnp.einsum('bhts,bshv->bthv', A, vc) + jnp.einsum('bthk,bhkv->bthv', qt, S)
        b_last = b[:, -1]
        kd = kc * jnp.exp(b_last[:, None] - b)
        S = S * jnp.exp(b_last)[..., None] + jnp.einsum('bshk,bshv->bhkv', kd, vc)
        return S, o

    S, o = lax.scan(step, S0, (chunks(q), chunks(k), chunks(v), chunks(gk)))
    return o.swapaxes(0, 1).reshape(B, T, H, V), S


def gla_group(pg, S0, f_up, f_bias, norm_g):
    f32 = jnp.float32
    pg = pg.astype(f32)
    B, T, _ = pg.shape
    q, k, v, g, f_lo = split_cols(pg, [B_KW, B_KW, B_WIDTH, B_WIDTH, B_GATE_LORA])
    gk = jax.nn.log_sigmoid(f_lo @ f_up + f_bias) / GLA_LOGIT_NORM
    hk = lambda t: t.reshape(B, T, B_HEADS, B_DK)
    o, S = gla_chunked(hk(q) * B_DK ** -0.5, hk(k), v.reshape(B, T, B_HEADS, B_DV), hk(gk), S0.astype(f32))
    o = o * lax.rsqrt(jnp.mean(o * o, -1, keepdims=True) + RMS_EPS) * norm_g
    return o.reshape(B, T, B_WIDTH) * jax.nn.silu(g), S


def conv_glu_ffn(x, conv_past, w_up, conv_w, conv_b, w_down):
    T = x.shape[1]
    gate, val = split_cols(x @ w_up, [D_FF, D_FF])
    padded = jnp.concatenate([conv_past.astype(gate.dtype), gate], axis=1)
    acc = conv_b
    for j in range(CONV_W):
        acc = acc + padded[:, j:j + T] * conv_w[j]
    h = jax.nn.gelu(acc, approximate=False) * val
    return h @ w_down, padded[:, T:]


def trunk(x, st_rwkv, st_shift, st_gla, st_conv, P):
    new_rwkv, new_shift, new_gla, new_conv = [], [], [], []
    v_first = None
    for l in range(DEPTH):
        proj = x @ P['w_in'][l]
        vmix = None if l == 0 else (P['vres_bias'][l - 1], P['vres_down'][l - 1], P['vres_up'][l - 1])
        oa, s_a, sh, v_first = rwkv7_group(
            proj[..., :SHIFT_W], st_shift[l], st_rwkv[l], v_first, vmix, P['tok_mu'][l],
            P['w0'][l], P['w_lora_up'][l], P['a0'][l], P['a_lora_up'][l], P['g_lora_up'][l],
            P['k_k'][l], P['k_a'][l], P['r_k'][l], P['lnx_g'][l], P['lnx_b'][l])
        ob, s_b = gla_group(proj[..., SHIFT_W:], st_gla[l], P['gla_f_up'][l], P['gla_f_bias'][l], P['gla_norm_g'][l])
        mix = jnp.concatenate([oa, ob], axis=-1).astype(x.dtype) @ P['w_out'][l]
        x = layer_norm(ALPHA * x + mix, P['ln1_g'][l], P['ln1_b'][l])
        f, cv = conv_glu_ffn(x, st_conv[l], P['w_up'][l], P['conv_w'][l], P['conv_b'][l], P['w_down'][l])
        x = layer_norm(ALPHA * x + f, P['ln2_g'][l], P['ln2_b'][l])
        new_rwkv.append(s_a)
        new_shift.append(sh)
        new_gla.append(s_b)
        new_conv.append(cv)
    return (x, jnp.stack(new_rwkv).astype(st_rwkv.dtype), jnp.stack(new_shift).astype(st_shift.dtype),
            jnp.stack(new_gla).astype(st_gla.dtype), jnp.stack(new_conv).astype(st_conv.dtype))


def setup_inputs(seed: int = 0) -> dict:
    key = jax.random.key(seed)
    ks = iter(jax.random.split(key, 40))
    f32 = jnp.float32
    nrm = lambda shape, s: jax.random.normal(next(ks), shape, f32) * s
    L = DEPTH
    x_prompt = nrm((BATCH, SEQ, D_MODEL), 1.0)
    x_sample = nrm((DEC_BATCH, DEC_SEQ, D_MODEL), 1.0)
    state_rwkv = nrm((L, DEC_BATCH, A_HEADS, A_HEAD_DIM, A_HEAD_DIM), 0.3)
    state_shift = nrm((L, DEC_BATCH, 1, SHIFT_W), 1.0)
    state_gla = nrm((L, DEC_BATCH, B_HEADS, B_DK, B_DV), 0.3)
    state_conv = nrm((L, DEC_BATCH, CONV_W - 1, D_FF), BETA)
    col_scale = np.ones(IN_W, np.float32)
    col_scale[2 * A_WIDTH:3 * A_WIDTH] = BETA
    gv0 = SHIFT_W + 2 * B_KW
    col_scale[gv0:gv0 + B_WIDTH] = BETA
    w_in = nrm((L, D_MODEL, IN_W), D_MODEL ** -0.5) * jnp.asarray(col_scale)
    tok_mu = jax.random.uniform(next(ks), (L, SHIFT_W), f32)
    w0 = jax.random.uniform(next(ks), (L, A_WIDTH), f32, -6.5, -1.5)
    w_lora_up = nrm((L, A_DECAY_LORA, A_WIDTH), 0.1)
    a0 = nrm((L, A_WIDTH), 0.1)
    a_lora_up = nrm((L, A_AAA_LORA, A_WIDTH), 0.1)
    g_lora_up = nrm((L, A_GATE_LORA, A_WIDTH), A_GATE_LORA ** -0.5)
    k_k = 0.85 + nrm((L, A_WIDTH), 0.02)
    k_a = 1.0 + nrm((L, A_WIDTH), 0.02)
    r_k = nrm((L, A_HEADS, A_HEAD_DIM), 0.1)
    lnx_g = 1.0 + nrm((L, A_WIDTH), 0.02)
    lnx_b = nrm((L, A_WIDTH), 0.02)
    vres_bias = 1.0 + nrm((L - 1, A_WIDTH), 0.1)
    vres_down = nrm((L - 1, A_WIDTH, A_MV_LORA), A_WIDTH ** -0.5)
    vres_up = nrm((L - 1, A_MV_LORA, A_WIDTH), 0.1)
    gla_f_up = nrm((L, B_GATE_LORA, B_KW), B_GATE_LORA ** -0.5)
    gla_f_bias = nrm((L, B_KW), 0.5)
    gla_norm_g = 1.0 + nrm((L, B_DV), 0.02)
    w_out = nrm((L, D_MODEL, D_MODEL), D_MODEL ** -0.5 * BETA)
    ln1_g = 1.0 + nrm((L, D_MODEL), 0.02)
    ln1_b = nrm((L, D_MODEL), 0.02)
    w_up = nrm((L, D_MODEL, 2 * D_FF), D_MODEL ** -0.5 * BETA)
    conv_w = nrm((L, CONV_W, D_FF), CONV_W ** -0.5)
    conv_b = nrm((L, D_FF), 0.02)
    w_down = nrm((L, D_FF, D_MODEL), D_FF ** -0.5 * BETA)
    ln2_g = 1.0 + nrm((L, D_MODEL), 0.02)
    ln2_b = nrm((L, D_MODEL), 0.02)
    return {'x_prompt': x_prompt, 'x_sample': x_sample, 'state_rwkv': state_rwkv,
            'state_shift': state_shift, 'state_gla': state_gla, 'state_conv': state_conv,
            'w_in': w_in, 'tok_mu': tok_mu, 'w0': w0, 'w_lora_up': w_lora_up, 'a0': a0,
            'a_lora_up': a_lora_up, 'g_lora_up': g_lora_up, 'k_k': k_k, 'k_a': k_a, 'r_k': r_k,
            'lnx_g': lnx_g, 'lnx_b': lnx_b, 'vres_bias': vres_bias, 'vres_down': vres_down,
            'vres_up': vres_up, 'gla_f_up': gla_f_up, 'gla_f_bias': gla_f_bias,
            'gla_norm_g': gla_norm_g, 'w_out': w_out, 'ln1_g': ln1_g, 'ln1_b': ln1_b,
            'w_up': w_up, 'conv_w': conv_w, 'conv_b': conv_b, 'w_down': w_down,
            'ln2_g': ln2_g, 'ln2_b': ln2_b}


def reference(x_prompt, x_sample, state_rwkv, state_shift, state_gla, state_conv,
              w_in, tok_mu, w0, w_lora_up, a0, a_lora_up, g_lora_up, k_k, k_a, r_k,
              lnx_g, lnx_b, vres_bias, vres_down, vres_up, gla_f_up, gla_f_bias, gla_norm_g,
              w_out, ln1_g, ln1_b, w_up, conv_w, conv_b, w_down, ln2_g, ln2_b):
    P = dict(w_in=w_in, tok_mu=tok_mu, w0=w0, w_lora_up=w_lora_up, a0=a0, a_lora_up=a_lora_up,
             g_lora_up=g_lora_up, k_k=k_k, k_a=k_a, r_k=r_k, lnx_g=lnx_g, lnx_b=lnx_b,
             vres_bias=vres_bias, vres_down=vres_down, vres_up=vres_up, gla_f_up=gla_f_up,
             gla_f_bias=gla_f_bias, gla_norm_g=gla_norm_g, w_out=w_out, ln1_g=ln1_g, ln1_b=ln1_b,
             w_up=w_up, conv_w=conv_w, conv_b=conv_b, w_down=w_down, ln2_g=ln2_g, ln2_b=ln2_b)
    zeros = lambda s: jnp.zeros((DEPTH, BATCH) + s.shape[2:], s.dtype)
    y_prompt, rwkv_p, shift_p, gla_p, conv_p = trunk(
        x_prompt, zeros(state_rwkv), zeros(state_shift), zeros(state_gla), zeros(state_conv), P)
    y_sample, rwkv_s, shift_s, gla_s, conv_s = trunk(
        x_sample, state_rwkv, state_shift, state_gla, state_conv, P)
    return (y_prompt, y_sample, rwkv_p, rwkv_s, shift_p, shift_s, gla_p, gla_s, conv_p, conv_s)
```

```python
import math
import numpy as np
from contextlib import ExitStack
import concourse.bass as bass
import concourse.mybir as mybir
from concourse.bass_utils import run_bass_kernel_spmd

F32 = mybir.dt.float32
BF16 = mybir.dt.bfloat16
ALU = mybir.AluOpType
AF = mybir.ActivationFunctionType
ENGS = ("pe", "act", "dve", "pool", "sp")
USE_DRAIN = True


class Tile:
    __slots__ = ("ap", "keys")

    def __init__(self, ap, keys):
        self.ap = ap
        self.keys = keys if isinstance(keys, tuple) else (keys,)

    def __getitem__(self, idx):
        return Tile(self.ap[idx], self.keys)

    def v(self, ap):
        return Tile(ap, self.keys)

    def re(self, pattern_, **kw):
        return Tile(self.ap.rearrange(pattern_, **kw), self.keys)

    def bc(self, shape):
        return Tile(self.ap.broadcast_to(shape), self.keys)


class Op:
    __slots__ = ("eng", "fn", "is_dma", "deps", "inc", "ticket", "waits", "dsem", "dval", "idx", "prewait", "drain", "grp")

    def __init__(self, eng, fn, is_dma):
        self.eng = eng
        self.fn = fn
        self.is_dma = is_dma
        self.deps = set()
        self.inc = False
        self.ticket = 0
        self.waits = []
        self.dsem = None
        self.dval = 0
        self.prewait = None
        self.drain = False
        self.grp = None


def _keys(ts):
    out = []
    for t in ts:
        if isinstance(t, Tile):
            out.extend(t.keys)
    return out


def _ap(t):
    return t.ap if isinstance(t, Tile) else t


class Prog:
    skip_same = ("pe",)
    grp = None
    _gctr = 0

    def indep(self):
        prog = self

        class _G:
            def __enter__(self_):
                prog._gctr += 1
                self_.prev = prog.grp
                prog.grp = prog._gctr

            def __exit__(self_, *a):
                prog.grp = self_.prev
                return False
        return _G()

    def __init__(self, nc, n_dma_sems=10):
        self.nc = nc
        self.ops = []
        self.last_writer = {}
        self.readers = {}
        self.n_dma_sems = n_dma_sems

    def add(self, eng, fn, reads, writes, is_dma=False):
        op = Op(eng, fn, is_dma)
        op.idx = len(self.ops)
        op.grp = self.grp
        deps = op.deps
        lw, rd = self.last_writer, self.readers
        for k in reads:
            w = lw.get(k)
            if w is not None:
                deps.add(w)
        for k in writes:
            w = lw.get(k)
            if w is not None:
                deps.add(w)
            r = rd.get(k)
            if r:
                deps.update(r)
        for k in writes:
            lw[k] = op.idx
            rd[k] = []
        ws = set(writes)
        for k in reads:
            if k not in ws:
                rd.setdefault(k, []).append(op.idx)
        if self.bar_deps and eng not in self.bar_seen:
            deps.update(self.bar_deps)
            self.bar_seen.add(eng)
        deps.discard(op.idx)
        if eng in self.skip_same and not is_dma:
            for d in [d for d in deps if self.ops[d].eng == eng and not self.ops[d].is_dma]:
                deps.discard(d)
        self.ops.append(op)
        return op

    bar_deps = None
    bar_seen = None
    bar_start = 0

    def barrier(self):
        last = {}
        deps = set()
        for op in self.ops[self.bar_start:]:
            last[op.eng] = op.idx
            if op.is_dma:
                deps.add(op.idx)
        deps.update(last.values())
        if self.bar_deps and len(self.bar_seen) < len(ENGS):
            deps.update(self.bar_deps)
        self.bar_deps = deps
        self.bar_seen = set()
        self.bar_start = len(self.ops)
        self.last_writer = {}
        self.readers = {}

    def emit(self, stack):
        nc = self.nc
        ops = self.ops
        drained_upto = {e: -1 for e in ENGS}
        for op in ops:
            best = {}
            keep = set()
            same = []
            for d in op.deps:
                dop = ops[d]
                if dop.is_dma:
                    keep.add(d)
                elif dop.eng == op.eng and not op.is_dma and op.eng in ("act", "dve", "pool"):
                    same.append(d)
                elif best.get(dop.eng, -1) < d:
                    best[dop.eng] = d
            if same:
                need = [d for d in same if d >= drained_upto[op.eng]
                        and not (op.grp is not None and ops[d].grp == op.grp)]
                if need:
                    op.drain = True
                    drained_upto[op.eng] = op.idx
            keep.update(best.values())
            op.deps = keep
            for d in keep:
                ops[d].inc = True
        sems = {e: stack.enter_context(nc.semaphore("s_" + e)) for e in ENGS}
        dsems = {e: [stack.enter_context(nc.semaphore("d_%s_%d" % (e, i))) for i in range(self.n_dma_sems)]
                 for e in ENGS}
        cnt = {e: 0 for e in ENGS}
        dcnt = {e: [0] * self.n_dma_sems for e in ENGS}
        drr = {e: 0 for e in ENGS}
        for op in ops:
            if op.is_dma:
                i = drr[op.eng]
                drr[op.eng] = (i + 1) % self.n_dma_sems
                op.dsem = dsems[op.eng][i]
                if dcnt[op.eng][i] > 0:
                    op.prewait = (op.dsem, dcnt[op.eng][i])
                dcnt[op.eng][i] += 16
                op.dval = dcnt[op.eng][i]
            elif op.inc:
                cnt[op.eng] += 1
                op.ticket = cnt[op.eng]
        waited = {e: {} for e in ENGS}
        for op in ops:
            w = waited[op.eng]
            need = {}
            for d in op.deps:
                dop = ops[d]
                if dop.is_dma:
                    s, v = dop.dsem, dop.dval
                else:
                    s, v = sems[dop.eng], dop.ticket
                k = id(s)
                if w.get(k, 0) >= v:
                    continue
                if k not in need or need[k][1] < v:
                    need[k] = (s, v)
            if op.prewait is not None:
                s, v = op.prewait
                k = id(s)
                if w.get(k, 0) < v and (k not in need or need[k][1] < v):
                    need[k] = (s, v)
            for k, (s, v) in need.items():
                w[k] = v
            op.waits = list(need.values())
        per = {e: [op for op in ops if op.eng == e] for e in ENGS}
        self.stats = {e: (len(per[e]), cnt[e], sum(len(o.waits) for o in per[e])) for e in ENGS}

        def run(eng_name, eng):
            se = sems[eng_name]
            for op in per[eng_name]:
                for (s, v) in op.waits:
                    eng.wait_ge(s, v)
                if op.drain and USE_DRAIN:
                    eng.drain()
                ins = op.fn(eng)
                if op.is_dma:
                    ins.then_inc(op.dsem, 16)
                elif op.inc:
                    ins.then_inc(se, 1)

        with nc.Block() as block:
            @block.tensor
            def _(eng):
                run("pe", eng)

            @block.scalar
            def _(eng):
                run("act", eng)

            @block.vector
            def _(eng):
                run("dve", eng)

            @block.gpsimd
            def _(eng):
                run("pool", eng)

            @block.sync
            def _(eng):
                run("sp", eng)

    def mm(self, out, lhsT, rhs, start=True, stop=True):
        o, l, r = out.ap, lhsT.ap, rhs.ap
        return self.add("pe", lambda e: e.matmul(o, l, r, start=start, stop=stop),
                        _keys([lhsT, rhs]), _keys([out]))

    def transpose(self, out, in_, ident):
        o, i, d = out.ap, in_.ap, ident.ap
        return self.add("pe", lambda e: e.transpose(o, i, d), _keys([in_, ident]), _keys([out]))

    def act(self, out, in_, func, bias=None, scale=None):
        o, i = out.ap, in_.ap
        kw = {}
        if bias is not None:
            kw["bias"] = _ap(bias)
        if scale is not None:
            kw["scale"] = _ap(scale)
        return self.add("act", lambda e: e.activation(o, i, func, **kw), _keys([in_, bias, scale]), _keys([out]))

    def tt(self, eng, out, in0, in1, op):
        o, a, b = out.ap, in0.ap, in1.ap
        return self.add(eng, lambda e: e.tensor_tensor(o, a, b, op), _keys([in0, in1]), _keys([out]))

    def ts(self, eng, out, in0, s1, op0, s2=None, op1=None):
        o, a = out.ap, in0.ap
        a1, a2 = _ap(s1), _ap(s2)
        if op1 is None:
            return self.add(eng, lambda e: e.tensor_scalar(o, a, a1, None, op0), _keys([in0, s1]), _keys([out]))
        return self.add(eng, lambda e: e.tensor_scalar(o, a, a1, a2, op0, op1), _keys([in0, s1, s2]), _keys([out]))

    def stt(self, out, in0, scalar, in1, op0, op1):
        o, a, b = out.ap, in0.ap, in1.ap
        s = _ap(scalar)
        return self.add("dve", lambda e: e.scalar_tensor_tensor(o, a, s, b, op0, op1),
                        _keys([in0, in1, scalar]), _keys([out]))

    def copy(self, eng, out, in_):
        o, i = out.ap, in_.ap
        if eng == "act":
            return self.add("act", lambda e: e.copy(o, i), _keys([in_]), _keys([out]))
        return self.add(eng, lambda e: e.tensor_copy(o, i), _keys([in_]), _keys([out]))

    def recip(self, out, in_):
        o, i = out.ap, in_.ap
        return self.add("dve", lambda e: e.reciprocal(o, i), _keys([in_]), _keys([out]))

    def memset(self, eng, out, val):
        o = out.ap
        return self.add(eng, lambda e: e.memset(o, val), [], _keys([out]))

    def scan(self, out, d0, d1):
        o, a, b = out.ap, d0.ap, d1.ap
        return self.add("dve", lambda e: e.tensor_tensor_scan(o, a, b, 0.0, ALU.mult, ALU.add),
                        _keys([d0, d1]), _keys([out]))

    def dma(self, eng, out, in_, **kw):
        o, i = _ap(out), _ap(in_)
        return self.add(eng, lambda e: e.dma_start(out=o, in_=i, **kw), _keys([in_]), _keys([out]), is_dma=True)

    def finish(self, eng="sp"):
        deps = [op.idx for op in self.ops if op.is_dma]
        op = self.add(eng, lambda e: e.nop(), [], [])
        op.deps.update(deps)
        for e in ENGS:
            last = [o.idx for o in self.ops if o.eng == e and not o.is_dma and o.idx != op.idx]
            if last:
                op.deps.add(last[-1])
        return op


D = 1024
KC = 8
AW = 512
SHIFT_W = 1792
IN_W = 3344
DFF = 2816
FC = 22
ALPHA = 8 ** 0.25
GN_EPS = 64e-5
RMS_EPS = 1e-5
LN_EPS = 1e-5
KDEC = math.exp(-0.5)
DEC_SEQ = 4


class _Stop(Exception):
    pass


def _round_robin(gens, lag=1):
    live = [True] * len(gens)
    step = 0
    while any(live):
        for gi in range(len(gens)):
            if gi * lag > step or not live[gi]:
                continue
            try:
                next(gens[gi])
            except StopIteration:
                live[gi] = False
        step += 1


def build_program(L, SEQ, NSQ, TB, debug=False, stop=99):
    def stage(n):
        if n > stop:
            raise _Stop()
    nc = bass.Bass("TRN2", target_bir_lowering=False)
    P = Prog(nc)
    NTMAX = max(TB, NSQ * DEC_SEQ)
    NTILE_MAX = (NTMAX + 127) // 128
    NCHMAX = max(TB // 64, NSQ)
    NS4 = NSQ * DEC_SEQ

    def din(name, shape):
        return nc.dram_tensor(name, list(shape), F32, kind="ExternalInput").ap()

    def dout(name, shape):
        return nc.dram_tensor(name, list(shape), F32, kind="ExternalOutput").ap()

    x_p = din("x_p", [SEQ, D])
    x_s = din("x_s", [NS4, D])
    st_rwkv = din("st_rwkv", [L, NSQ, 8, 64, 64])
    st_shift = din("st_shift", [L, NSQ, SHIFT_W])
    st_gla = din("st_gla", [L, NSQ, 4, 64, 128])
    st_conv = din("st_conv", [L, NSQ, 2, DFF])
    w_in = din("w_in", [L, D, IN_W])
    tok_mu = din("tok_mu", [L, SHIFT_W])
    w0 = din("w0", [L, AW])
    w_lora_up = din("w_lora_up", [L, 64, AW])
    a0 = din("a0", [L, AW])
    a_lora_up = din("a_lora_up", [L, 64, AW])
    g_lora_up = din("g_lora_up", [L, 128, AW])
    k_k = din("k_k", [L, AW])
    k_a = din("k_a", [L, AW])
    r_k = din("r_k", [L, AW])
    lnx_g = din("lnx_g", [L, AW])
    lnx_b = din("lnx_b", [L, AW])
    LV = max(L - 1, 1)
    vres_bias = din("vres_bias", [LV, AW])
    vres_down = din("vres_down", [LV, AW, 32])
    vres_up = din("vres_up", [LV, 32, AW])
    gla_f_up = din("gla_f_up", [L, 16, 256])
    gla_f_bias = din("gla_f_bias", [L, 256])
    gla_norm_g = din("gla_norm_g", [L, 128])
    w_out = din("w_out", [L, D, D])
    ln1_g = din("ln1_g", [L, D])
    ln1_b = din("ln1_b", [L, D])
    w_up = din("w_up", [L, D, 2 * DFF])
    conv_w = din("conv_w", [L, 3, DFF])
    conv_b = din("conv_b", [L, DFF])
    w_down = din("w_down", [L, DFF, D])
    ln2_g = din("ln2_g", [L, D])
    ln2_b = din("ln2_b", [L, D])

    y_p = dout("y_p", [SEQ, D])
    y_s = dout("y_s", [NS4, D])
    o_rwkv_p = dout("o_rwkv_p", [L, 8, 64, 64])
    o_rwkv_s = dout("o_rwkv_s", [L, NSQ, 8, 64, 64])
    o_shift_p = dout("o_shift_p", [L, SHIFT_W])
    o_shift_s = dout("o_shift_s", [L, NSQ, SHIFT_W])
    o_gla_p = dout("o_gla_p", [L, 4, 64, 128])
    o_gla_s = dout("o_gla_s", [L, NSQ, 4, 64, 128])
    o_conv_p = dout("o_conv_p", [L, 2, DFF])
    o_conv_s = dout("o_conv_s", [L, NSQ, 2, DFF])
    dbg = {}

    st = ExitStack()
    st.enter_context(nc.allow_non_contiguous_dma(reason="small strided parameter/state transfers"))
    st.enter_context(nc.allow_low_precision(reason="bf16 matmul operands, fp32 accumulation"))

    def sb(name, shape, dt, nkeys=None):
        t = st.enter_context(nc.sbuf_tensor(name, list(shape), dt))
        return Tile(t.ap(), name)

    def sbk(name, shape, dt, n):
        t = st.enter_context(nc.sbuf_tensor(name, list(shape), dt))
        ap = t.ap()
        return [Tile(ap[:, i], "%s.%d" % (name, i)) for i in range(n)]

    xres = sbk("xres", [128, NTILE_MAX, D], F32, NTILE_MAX)
    xT = sb("xT", [128, KC, NTMAX], BF16)
    vfirst = sbk("vfirst", [128, 4, NTMAX], F32, 4)
    mixT = sbk("mixT", [128, KC, NTMAX], BF16, KC)
    Hm = sbk("Hm", [128, L, 4, 128], F32, L)
    Sg = sbk("Sg", [128, L, 2, 128], F32, L)
    shp = sbk("shp", [128, L, 14], F32, L)
    cst = sbk("cst", [128, L, FC, 2], F32, L)
    mu_t = sb("mu_t", [128, L, 14], F32)
    pv = {}
    for nm, src in (("w0", w0), ("a0", a0), ("kk", k_k), ("ka", k_a), ("rk", r_k), ("lg", lnx_g), ("lb", lnx_b)):
        pv[nm] = sb("pv_" + nm, [128, L, 4], F32)
        P.dma("sp", pv[nm], src.rearrange("l (c p) -> p l c", p=128))
    pv["v0"] = sb("pv_v0", [128, LV, 4], F32)
    P.dma("sp", pv["v0"], vres_bias.rearrange("l (c p) -> p l c", p=128))
    omka = sb("omka", [128, L, 4], F32)
    P.ts("dve", omka, pv["ka"], -1.0, ALU.mult, 1.0, ALU.add)
    P.dma("sp", mu_t, tok_mu.rearrange("l (c p) -> p l c", p=128))
    omu = sb("omu", [128, L, 14], F32)
    P.ts("dve", omu, mu_t, -1.0, ALU.mult, 1.0, ALU.add)
    fb_t = sb("fb_t", [128, L, 2], F32)
    P.dma("sp", fb_t, gla_f_bias.rearrange("l (c p) -> p l c", p=128))
    ng_t = sb("ng_t", [128, L], F32)
    P.dma("sp", ng_t, gla_norm_g.rearrange("l p -> p l"))
    cw_t = sb("cw_t", [128, L, 3, FC], F32)
    P.dma("sp", cw_t, conv_w.rearrange("l j (c p) -> p l j c", p=128))
    cb_t = sb("cb_t", [128, L, FC], F32)
    P.dma("sp", cb_t, conv_b.rearrange("l (c p) -> p l c", p=128))
    wa_up = sb("wa_up", [128, L, AW], BF16)
    P.dma("pool", wa_up[0:64], w_lora_up.rearrange("l k c -> k l c"))
    P.dma("pool", wa_up[64:128], a_lora_up.rearrange("l k c -> k l c"))
    g_up = sb("g_up", [128, L, AW], BF16)
    P.dma("pool", g_up, g_lora_up.rearrange("l k c -> k l c"))
    vdn = sb("vdn", [128, LV, 4, 32], BF16)
    P.dma("pool", vdn, vres_down.rearrange("l (c p) r -> p l c r", p=128))
    vup = sb("vup", [32, LV, AW], BF16)
    P.dma("pool", vup, vres_up.rearrange("l r c -> r l c"))
    fup = sb("fup", [16, L, 256], BF16)
    P.dma("pool", fup, gla_f_up.rearrange("l r c -> r l c"))

    identb = sb("identb", [128, 128], BF16)
    identf = sb("identf", [128, 128], F32)
    for t in (identb, identf):
        P.memset("dve", t, 0.0)
        tap = t.ap
        P.add("pool", (lambda a: (lambda e: e.affine_select(a, a, [[-1, 128]], ALU.not_equal, 1.0, base=0,
                                                            channel_multiplier=1)))(tap), list(t.keys), list(t.keys))
    imask = sb("imask", [64, 64], F32)
    smask = sb("smask", [64, 64], F32)
    smaskT = sb("smaskT", [64, 64], F32)
    for t, pat, cm, base in ((imask, 1, -1, 0), (smask, 1, -1, -1), (smaskT, -1, 1, -1)):
        P.memset("dve", t, 1.0)
        tap = t.ap
        P.add("pool", (lambda a, pat, cm, base: (lambda e: e.affine_select(
            a, a, [[pat, 64]], ALU.is_ge, 0.0, base=base, channel_multiplier=cm)))(tap, pat, cm, base),
            list(t.keys), list(t.keys))
    bdones = sb("bdones", [128, 128], F32)
    P.memset("dve", bdones, 0.0)
    P.memset("dve", bdones[0:64, 0:64], 1.0)
    P.memset("dve", bdones[64:128, 64:128], 1.0)
    allones = sb("allones", [128, 128], F32)
    P.memset("dve", allones, 1.0)
    rmask_p = sb("rmask_p", [128, TB], BF16)
    P.memset("dve", rmask_p, 1.0)
    P.memset("dve", rmask_p.re("p (n c) -> p n c", c=64)[:, :, 0:1], 0.0)
    rmask_s = sb("rmask_s", [128, NS4], BF16)
    P.memset("dve", rmask_s, 1.0)
    P.memset("dve", rmask_s.re("p (n c) -> p n c", c=4)[:, :, 0:1], 0.0)
    for l in range(L):
        P.memset("dve", Hm[l], 0.0)
        P.memset("dve", Sg[l], 0.0)
        P.memset("dve", shp[l], 0.0)
        P.memset("dve", cst[l], 0.0)

    NSLOT = 3
    wslot = [sb("wslot%d" % i, [128, KC, 512], BF16) for i in range(NSLOT)]
    wctr = [0]

    NGRP = 28 * L
    wscr = nc.dram_tensor("wscr", [NGRP, 128, KC * 512], BF16, kind="Internal").ap()
    gctr = [0]
    first_blk = [True]

    def wload(src_ap, nk, ncol):
        s = wslot[wctr[0] % NSLOT]
        wctr[0] += 1
        gid = gctr[0]
        gctr[0] += 1
        assert gid < NGRP
        dst = s[:, 0:nk, 0:ncol]
        scr = Tile(wscr[gid], "wscr.%d" % gid)
        flat = s.re("p k c -> p (k c)")
        if first_blk[0]:
            P.dma("pool", dst, src_ap)
            P.dma("sp", scr, flat)
        else:
            P.dma("pool", flat, scr)
        return dst

    pst = st.enter_context(nc.psum_tensor("psum", [128, 8, 512], F32))
    psap = pst.ap()
    PSB = [Tile(psap[:, i], "ps%d" % i) for i in range(8)]

    def ps_multi(b0, nb):
        return Tile(psap[:, b0:b0 + nb].rearrange("p b c -> p (b c)"), tuple("ps%d" % i for i in range(b0, b0 + nb)))

    def ps_bf(bank):
        return Tile(psap[:, bank].bitcast(BF16), "ps%d" % bank)

    ARENA_F32 = 0
    mixer_defs = []
    ffn_defs = []

    NT_ = NTMAX
    PA_W = max(TB + 1, NSQ * (DEC_SEQ + 1))
    GT_W = max(TB + 2, NSQ * (DEC_SEQ + 2))
    m_off = [0]

    def m_alloc(words):
        o = m_off[0]
        m_off[0] += (words + 7) // 8 * 8
        return o

    f_off = [0]

    def f_alloc(words):
        o = f_off[0]
        f_off[0] += (words + 7) // 8 * 8
        return o

    layout_m = {}
    layout_m["pa"] = m_alloc(14 * PA_W)
    NTMP = 9
    for i in range(NTMP):
        layout_m["t%d" % i] = m_alloc(NT_)
    layout_m["lor"] = m_alloc(NT_ // 2)
    layout_m["sgl"] = m_alloc(NT_ // 2)
    layout_m["vb"] = m_alloc(4 * NT_ // 2)
    layout_m["vdb"] = m_alloc(NT_ // 2)
    layout_m["osb"] = m_alloc(4 * NT_)
    layout_m["bon"] = m_alloc(4 * NT_ // 2)
    layout_m["gst"] = m_alloc(4 * NT_ // 2)
    layout_m["AK"] = m_alloc(4 * NT_ * 2 // 2)
    layout_m["BR"] = m_alloc(4 * NT_ * 2 // 2)
    layout_m["PC"] = m_alloc(4 * NCHMAX)
    for nm, w in (("XA", 8 * 128 // 2), ("XB", 8 * 128 // 2), ("YA", 8 * 64 // 2), ("YB", 8 * 64 // 2),
                  ("MkT", 8 * 64 // 2), ("QT", 2 * 8 * 64 // 2), ("Wsb", 8 * 128 // 2), ("Upad", 8 * 128 // 2),
                  ("Upair", 4 * 128 // 2), ("Vpad", 8 * 128 // 2), ("Vpair", 4 * 128 // 2), ("AKt", 2 * 4 * 128 // 2),
                  ("Hbd", 4 * 128 // 2), ("sttmp", 8 * 64), ("Sgb", 2 * 128 // 2), ("ATsb", 4 * 64 // 2),
                  ("Vt", 4 * 128 // 2), ("kdt", 2 * 128 // 2), ("stout", 4 * 64)):
        layout_m[nm] = m_alloc(w)
    layout_f = {}
    layout_f["zbuf"] = f_alloc(NTILE_MAX * D)
    layout_f["lng"] = f_alloc(D)
    layout_f["lnb"] = f_alloc(D)
    layout_f["hT"] = f_alloc(FC * NT_ // 2)
    layout_f["gt0"] = f_alloc(GT_W)
    layout_f["gt1"] = f_alloc(GT_W)
    layout_f["acc0"] = f_alloc(NT_)
    layout_f["acc1"] = f_alloc(NT_)
    layout_f["xb"] = f_alloc(D // 2)
    layout_f["xb1"] = f_alloc(D // 2)
    layout_f["stat"] = f_alloc(32 * NTILE_MAX)
    layout_f["cpast"] = f_alloc(FC * NSQ * 2)
    layout_f["cout"] = f_alloc(FC * NSQ * 2)
    layout_f["ctok"] = f_alloc(DFF)
    ARENA_F32 = max(m_off[0], f_off[0])
    arena_t = st.enter_context(nc.sbuf_tensor("arena", [128, ARENA_F32], F32))
    arena = arena_t.ap()
    MKEYS = tuple("m." + k for k in layout_m)

    def mtile(name, words, dt, shape_str=None, **kw):
        o = layout_m[name]
        ap = arena[:, o:o + words]
        if dt == BF16:
            ap = ap.bitcast(BF16)
        if shape_str:
            ap = ap.rearrange(shape_str, **kw)
        return Tile(ap, "m." + name)

    def ftile(name, words, dt, shape_str=None, **kw):
        o = layout_f[name]
        ap = arena[:, o:o + words]
        if dt == BF16:
            ap = ap.bitcast(BF16)
        if shape_str:
            ap = ap.rearrange(shape_str, **kw)
        return Tile(ap, "f." + name)

    def dump(name, tile, shape):
        if not debug:
            return
        d = dout("dbg_" + name, shape)
        dbg[name] = shape
        P.dma("sp", d, tile)

    def load_x_block(kind, b, NT):
        ntile = (NT + 127) // 128
        for t in range(ntile):
            rows = min(128, NT - t * 128)
            src = x_p[b * TB + t * 128: b * TB + t * 128 + rows, :] if kind == "p" else x_s[t * 128: t * 128 + rows, :]
            P.dma("sp", xres[t][0:rows], src)

    def make_xT(NT, which_ps=(6, 7)):
        ntile = (NT + 127) // 128
        for t in range(ntile):
            rows = min(128, NT - t * 128)
            xb = ftile("xb" if t % 2 == 0 else "xb1", D // 2, BF16)
            P.copy("act", xb[0:rows], xres[t][0:rows])
            pb = ps_bf(which_ps[t % 2])
            for k in range(KC):
                P.transpose(pb[:, k * 128:k * 128 + rows], xb[0:rows, k * 128:(k + 1) * 128], identb[0:rows, 0:rows])
            P.copy("dve", xT[:, :, t * 128:t * 128 + rows],
                   pb[:, 0:KC * 128].re("p (k c) -> p k c", c=128)[:, :, 0:rows])

    def layer_norm_tile(t, rows, zt, g_dram, b_dram, lng, lnb):
        st_all = ftile("stat", 32 * NTILE_MAX, F32)
        stat = Tile(st_all.ap[:, 32 * t:32 * (t + 1)], "f.stat.%d" % t)
        s6 = stat[:, 0:12].re("p (a b) -> p a b", b=6)
        for hf in range(2):
            za, sa = zt[0:rows, hf * 512:(hf + 1) * 512].ap, s6[0:rows, hf].ap
            P.add("dve", (lambda o, i: (lambda e: e.bn_stats(o, i)))(sa, za), _keys([zt]), _keys([stat]))
        yield
        mv = stat[:, 12:14]
        P.add("dve", (lambda o, i: (lambda e: e.bn_aggr(o, i)))(mv[0:rows].ap, stat[0:rows, 0:12].ap),
              _keys([stat]), _keys([stat]))
        yield
        rs = stat[:, 14:15]
        P.act(rs[0:rows], mv[0:rows, 1:2], AF.Sqrt, bias=LN_EPS, scale=1.0)
        yield
        P.recip(rs[0:rows], rs[0:rows])
        yield
        nb = stat[:, 15:16]
        P.ts("dve", nb[0:rows], mv[0:rows, 0:1], rs[0:rows], ALU.mult, -1.0, ALU.mult)
        yield
        P.act(zt[0:rows], zt[0:rows], AF.Identity, bias=nb[0:rows], scale=rs[0:rows])
        yield
        P.tt("dve", zt[0:rows], zt[0:rows], lng[0:rows], ALU.mult)
        yield
        P.tt("dve", xres[t][0:rows], zt[0:rows], lnb[0:rows], ALU.add)
        yield

    def proj_token_major(NT, lhs_chunks, nk, w_dram_kview, g_dram, b_dram):
        ntile = (NT + 127) // 128
        zb_all = ftile("zbuf", NTILE_MAX * D, F32, "p (t d) -> p t d", d=D)
        zb = [Tile(zb_all.ap[:, t], "f.zbuf.%d" % t) for t in range(NTILE_MAX)]
        lng = ftile("lng", D, F32)
        lnb = ftile("lnb", D, F32)
        P.dma("sp", lng, g_dram.partition_broadcast(128))
        P.dma("sp", lnb, b_dram.partition_broadcast(128))
        for hf in range(2):
            k0 = 0
            while k0 < nk:
                nk_ = min(KC, nk - k0)
                ws = wload(w_dram_kview(k0, nk_, hf), nk_, 512)
                for kk in range(nk_):
                    k = k0 + kk
                    for t in range(ntile):
                        rows = min(128, NT - t * 128)
                        P.mm(PSB[t][0:rows], lhs_chunks(k)[:, t * 128:t * 128 + rows], ws[:, kk, :],
                             start=(k == 0), stop=(k == nk - 1))
                k0 += nk_
            for t in range(ntile):
                rows = min(128, NT - t * 128)
                P.stt(zb[t][0:rows, hf * 512:(hf + 1) * 512], xres[t][0:rows, hf * 512:(hf + 1) * 512], ALPHA,
                      PSB[t][0:rows], ALU.mult, ALU.add)
        _round_robin([layer_norm_tile(t, min(128, NT - t * 128), zb[t], g_dram, b_dram, lng, lnb)
                      for t in range(ntile)], lag=1)

    pctr = [0]

    def proj_feature_major(NT, w_dram, col0, ncols, consume):
        wv = w_dram.rearrange("(k p) c -> p k c", p=128)
        c = 0
        j = 0
        while c < ncols:
            wcols = min(512, ncols - c)
            ws = wload(wv[:, :, col0 + c: col0 + c + wcols], KC, wcols)
            cc = 0
            while cc < wcols:
                m = min(128, wcols - cc)
                pt = PSB[pctr[0] % 4]
                pctr[0] += 1
                for k in range(KC):
                    P.mm(pt[0:m, 0:NT], ws[:, k, cc:cc + m], xT[:, k, 0:NT], start=(k == 0), stop=(k == KC - 1))
                consume(j, pt, m)
                j += 1
                cc += m
            c += wcols

    blocks = [("p", b) for b in range(SEQ // TB)] + [("s", 0)]
    try:
      stage(1)
      for (kind, b) in blocks:
          is_p = kind == "p"
          NT = TB if is_p else NS4
          C = 64 if is_p else DEC_SEQ
          nch = NT // C
          nseq = 1 if is_p else NSQ
          Tq = TB if is_p else DEC_SEQ
          nlev = int(round(math.log2(C))) - 1
          last_p = is_p and b == SEQ // TB - 1
          rmask = rmask_p if is_p else rmask_s

          gctr[0] = 0
          first_blk[0] = (kind, b) == blocks[0]
          load_x_block(kind, b, NT)
          stage(2)
          make_xT(NT)
          stage(3)
          P.barrier()

          for l in range(L):
              pa_ap = mtile("pa", 14 * PA_W, F32, "p (j w) -> p j w", w=PA_W).ap
              PAKEYS = tuple("m.pa.%d" % j for j in range(14))

              class _PA:
                  def __getitem__(self, idx):
                      j = idx[1]
                      if isinstance(j, int):
                          return Tile(pa_ap[idx], PAKEYS[j])
                      return Tile(pa_ap[idx], PAKEYS)
              pa = _PA()

              def pa_cur(j):
                  return pa[:, j, 0:nseq * (Tq + 1)].re("p (s t) -> p s t", t=Tq + 1)[:, :, 1:]

              def pa_prev(j):
                  return pa[:, j, 0:nseq * (Tq + 1)].re("p (s t) -> p s t", t=Tq + 1)[:, :, 0:Tq]

              def pa_cv(j):
                  if is_p:
                      return pa[:, j, 1:1 + NT].re("p (n c) -> p n c", c=C)
                  return pa_cur(j)

              def pa_flat(j):
                  return pa[:, j, 0:NT]

              tmp = [mtile("t%d" % i, NT_, F32)[:, 0:NT] for i in range(NTMP)]

              def cv(t):
                  return t.re("p (n c) -> p n c", c=C)

              def sv(t):
                  return t.re("p (s t) -> p s t", t=Tq)

              lor = mtile("lor", NT_ // 2, BF16)[:, 0:NT]
              sgl = mtile("sgl", NT_ // 2, BF16)[:, 0:NT]
              vb = mtile("vb", 4 * NT_ // 2, BF16, "p (c n) -> p c n", c=4)[:, :, 0:NT]
              vdb = mtile("vdb", NT_ // 2, BF16)[:, 0:NT]
              osb = mtile("osb", 4 * NT_, F32, "p (c n) -> p c n", c=4)[:, :, 0:NT]
              bon = mtile("bon", 4 * NT_ // 2, BF16, "p (c n) -> p c n", c=4)[:, :, 0:NT]
              gst = mtile("gst", 4 * NT_ // 2, BF16, "p (c n) -> p c n", c=4)[:, :, 0:NT]
              AK = mtile("AK", 4 * NT_, BF16, "p (c n) -> p c n", c=4)[:, :, 0:2 * NT].re("p c (n k s) -> p c n k s", k=2, s=C)
              BR = mtile("BR", 4 * NT_, BF16, "p (c n) -> p c n", c=4)[:, :, 0:2 * NT].re("p c (n k s) -> p c n k s", k=2, s=C)
              PC = mtile("PC", 4 * NCHMAX, F32, "p (c n) -> p c n", c=4)[:, :, 0:nch]

              def consume_pa(j, pt, m):
                  P.copy("act", pa_cur(j), pt[:, 0:NT].re("p (s t) -> p s t", t=Tq))
              proj_feature_major(NT, w_in[l], 0, SHIFT_W, consume_pa)
              stage(4)
              if is_p:
                  P.copy("dve", pa[:, :, 0], shp[l])
                  P.copy("dve", shp[l], pa[:, :, Tq])
                  if last_p:
                      P.dma("sp", o_shift_p[l].rearrange("(j p) -> p j", p=128), shp[l])
              else:
                  shs = mtile("sttmp", 8 * 64, F32)[:, 0:14 * NSQ].re("p (j s) -> p j s", s=NSQ)
                  fast_sh = NT_ >= 448 and NSQ <= 32
                  if fast_sh:
                      o2 = layout_m["t2"]
                      stok = Tile(arena[:, o2:o2 + SHIFT_W], ("m.t2", "m.t3", "m.t4", "m.t5"))
                      P.dma("sp", stok[0:NSQ], st_shift[l])
                      pin = PSB[6]
                      for j in range(14):
                          P.mm(pin[:, j * NSQ:(j + 1) * NSQ], stok[0:NSQ, j * 128:(j + 1) * 128], identf[0:NSQ, 0:NSQ])
                      P.copy("act", pa[:, :, 0:nseq * (Tq + 1)].re("p j (s t) -> p j s t", t=Tq + 1)[:, :, :, 0],
                             pin[:, 0:14 * NSQ].re("p (j s) -> p j s", s=NSQ))
                  else:
                      for j in range(14):
                          P.dma("sp", pa[:, j, 0:nseq * (Tq + 1)].re("p (s t) -> p s t", t=Tq + 1)[:, :, 0],
                                st_shift[l, :, j * 128:(j + 1) * 128].rearrange("s p -> p s"))
                  for j in range(14):
                      P.copy("dve", shs[:, j], pa[:, j, 0:nseq * (Tq + 1)].re("p (s t) -> p s t", t=Tq + 1)[:, :, Tq])
                  if fast_sh:
                      for j in range(14):
                          bk, cc = divmod(j, 4)
                          P.mm(PSB[4 + bk][0:NSQ, cc * 128:(cc + 1) * 128], shs[:, j, :], identf)
                      for bk in range(4):
                          w_ = min(512, SHIFT_W - bk * 512)
                          P.copy("act", stok[0:NSQ, bk * 512:bk * 512 + w_], PSB[4 + bk][0:NSQ, 0:w_])
                      P.dma("sp", o_shift_s[l], stok[0:NSQ])
                  else:
                      for j in range(14):
                          P.dma("sp", o_shift_s[l, :, j * 128:(j + 1) * 128].rearrange("s p -> p s"), shs[:, j])
              stage(5)
              for j in range(14):
                  tw_ = sv(tmp[j % 2])
                  P.act(tw_, pa_prev(j), AF.Copy, scale=mu_t[:, l, j:j + 1])
                  P.stt(pa_cur(j), pa_cur(j), omu[:, l, j:j + 1], tw_, ALU.mult, ALU.add)
              stage(6)
              P.act(cv(lor)[0:64], pa_cv(12)[0:64], AF.Tanh)
              P.copy("act", cv(lor)[64:128], pa_cv(12)[64:128])
              P.act(cv(sgl), pa_cv(13), AF.Sigmoid)
              if l == 0:
                  for c in range(4):
                      P.copy("act", cv(vfirst[c][:, 0:NT]), pa_cv(8 + c))
              else:
                  for c in range(4):
                      P.copy("act", cv(vb[:, c]), pa_cv(8 + c))
                  pt = PSB[4]
                  for c in range(4):
                      P.mm(pt[0:32, 0:NT], vdn[:, l - 1, c, :], vb[:, c], start=(c == 0), stop=(c == 3))
                  P.copy("act", vdb[0:32], pt[0:32, 0:NT])

              stage(7)
              T_s1, T_cs, T_eP, T_eM, T_a, T_kk, T_km, T_1, T_2 = tmp
              T_csx = T_eX = T_s1
              T_kn = T_kk
              two_chain = NT_ >= 512

              def host(names):
                  o_ = layout_m[names[0]]
                  return Tile(arena[:, o_:o_ + NT_], tuple("m." + n_ for n_ in names))[:, 0:NT]
              if two_chain:
                  tmpB = [host(["osb"]), host(["XA"]), host(["XB"]), host(["Wsb"]), host(["QT"]), host(["AKt"]),
                          host(["Upad"]), host(["Vpad"]), host(["YA", "YB"])]
              else:
                  tmpB = tmp

              def pair_gen(c, TT, BK):
                  t_s1, t_cs, t_eP, t_eM, t_a, t_kk, t_km, t_1, t_2 = TT
                  t_csx = t_eX = t_s1
                  t_kn = t_kk
                  cs_ = slice(c * 128, (c + 1) * 128)
                  pw, pa_, pg, pvv = PSB[BK], PSB[BK + 1], PSB[BK + 2], PSB[BK + 3]
                  P.mm(pw[:, 0:NT], wa_up[0:64, l, cs_], lor[0:64])
                  P.mm(pa_[:, 0:NT], wa_up[64:128, l, cs_], lor[64:128])
                  P.mm(pg[:, 0:NT], g_up[:, l, cs_], sgl)
                  if l > 0:
                      P.mm(pvv[:, 0:NT], vup[0:32, l - 1, cs_], vdb[0:32])
                  yield
                  P.act(t_s1, pw[:, 0:NT], AF.Sigmoid, bias=pv["w0"][:, l, c:c + 1], scale=1.0)
                  P.act(t_a, pa_[:, 0:NT], AF.Sigmoid, bias=pv["a0"][:, l, c:c + 1], scale=1.0)
                  if l > 0:
                      P.act(t_1, pvv[:, 0:NT], AF.Sigmoid, bias=pv["v0"][:, l - 1, c:c + 1], scale=1.0)
                  P.copy("act", gst[:, c], pg[:, 0:NT])
                  yield
                  P.scan(t_cs, rmask[:, 0:NT], t_s1)
                  yield
                  P.tt("dve", t_csx, t_cs, t_s1, ALU.subtract)
                  P.act(t_eP, t_cs, AF.Exp, scale=-KDEC)
                  P.act(t_eM, t_cs, AF.Exp, scale=KDEC)
                  yield
                  P.act(t_eX, t_csx, AF.Exp, scale=-KDEC)
                  P.copy("act", PC[:, c], cv(t_eP)[:, :, C - 1])
                  xr, xk, xv = pa_cv(c), pa_cv(4 + c), pa_cv(8 + c)
                  if l > 0:
                      P.tt("dve", cv(t_2), cv(vfirst[c][:, 0:NT]), xv, ALU.subtract)
                      yield
                      P.tt("dve", t_2, t_2, t_1, ALU.mult)
                      yield
                      P.tt("dve", xv, xv, cv(t_2), ALU.add)
                      yield
                  P.copy("act", cv(vb[:, c]), xv)
                  P.act(cv(t_kk), xk, AF.Copy, scale=pv["kk"][:, l, c:c + 1])
                  yield
                  P.act(t_1, t_kk, AF.Square)
                  P.mm(pw[:, 0:NT], bdones, t_1)
                  yield
                  P.ts("dve", t_1, pw[:, 0:NT], 1e-24, ALU.max)
                  P.act(t_1, t_1, AF.Sqrt)
                  yield
                  P.recip(t_1, t_1)
                  yield
                  P.tt("dve", t_kn, t_kk, t_1, ALU.mult)
                  yield
                  P.act(t_2, t_a, AF.Identity, bias=omka[:, l, c:c + 1], scale=pv["ka"][:, l, c:c + 1])
                  yield
                  P.tt("dve", cv(t_km), xk, cv(t_2), ALU.mult)
                  yield
                  P.tt("dve", t_1, t_a, t_kn, ALU.mult)
                  yield
                  P.tt("dve", AK[:, c, :, 0, :], cv(t_1), cv(t_eM), ALU.mult)
                  yield
                  P.tt("dve", AK[:, c, :, 1, :], cv(t_km), cv(t_eM), ALU.mult)
                  yield
                  P.stt(BR[:, c, :, 0, :], cv(t_kn), -1.0, cv(t_eX), ALU.mult, ALU.mult)
                  yield
                  P.tt("dve", BR[:, c, :, 1, :], xr, cv(t_eP), ALU.mult)
                  yield
                  P.tt("dve", cv(t_2), xr, cv(t_km), ALU.mult)
                  yield
                  P.act(t_2, t_2, AF.Copy, scale=pv["rk"][:, l, c:c + 1])
                  P.mm(pa_[:, 0:NT], bdones, t_2)
                  yield
                  P.tt("dve", cv(bon[:, c]), cv(pa_[:, 0:NT]), xv, ALU.mult)
                  yield

              for c0 in (0, 2):
                  gens = [pair_gen(c0, tmp, 0), pair_gen(c0 + 1, tmpB, 4)]
                  live = [True, True]
                  started = 0
                  step = 0
                  while any(live):
                      for gi in range(2):
                          if gi == 1 and step < 3:
                              continue
                          if live[gi]:
                              try:
                                  next(gens[gi])
                              except StopIteration:
                                  live[gi] = False
                      step += 1

              stage(8)
              XAB = [mtile(n, 8 * 128 // 2, BF16, "p (h k s) -> p h k s", h=8, k=2) for n in ("XA", "XB")]
              YAB = [mtile(n, 8 * 64 // 2, BF16, "p (h s) -> p h s", h=8) for n in ("YA", "YB")]
              MkT = mtile("MkT", 8 * 64 // 2, BF16, "p (h s) -> p h s", h=8)
              QT = mtile("QT", 2 * 8 * 64 // 2, BF16, "p (k h s) -> p k h s", k=2, h=8)
              Wsb = mtile("Wsb", 8 * 128 // 2, BF16, "p (h s) -> p h s", h=8)
              Upad = mtile("Upad", 8 * 128 // 2, BF16, "p (h s) -> p h s", h=8)
              Upair = mtile("Upair", 4 * 128 // 2, BF16, "p (h s) -> p h s", h=4)
              Vpad = mtile("Vpad", 8 * 128 // 2, BF16, "p (h s) -> p h s", h=8)
              Vpair = mtile("Vpair", 4 * 128 // 2, BF16, "p (h s) -> p h s", h=4)
              AKt = mtile("AKt", 2 * 4 * 128 // 2, BF16, "p (k c s) -> p k c s", k=2, c=4)
              Hbd = mtile("Hbd", 4 * 128 // 2, BF16, "p (c s) -> p c s", c=4)
              P.memset("dve", Vpad, 0.0)
              P.memset("dve", Upad, 0.0)
              if is_p:
                  Hcur = Hm[l]
              else:
                  Hs = mtile("sttmp", 8 * 64, F32, "p (c s) -> p c s", c=4)
                  Hcur = Hs
                  P.memset("dve", Hs, 0.0)
              stout = mtile("stout", 4 * 64, F32, "p (c s) -> p c s", c=4)

              def padview(t8):
                  a = t8.ap
                  parts = a.ap
                  return Tile(bass.AP(tensor=a.tensor, offset=a.offset, ap=[list(parts[0]), [256, 4], [192, 2], [1, 64]]),
                              t8.keys)

              dbl = (not is_p) and NT_ >= 320 and NT <= 64
              if dbl:
                  def _tail(i_, key):
                      o_ = layout_m["t%d" % i_] + 64
                      return Tile(arena[:, o_:o_ + 256].rearrange("p (c s) -> p c s", c=4), key)
                  sin_sets = [[_tail(6, "m.sinA0"), _tail(7, "m.sinA1")], [_tail(8, "m.sinB0"), _tail(0, "m.sinB1")]]
                  stouts = [stout, _tail(1, "m.stoutB")]

                  def load_sin(n_):
                      for hf in range(2):
                          P.dma("sp", sin_sets[n_ % 2][hf][0:64], st_rwkv[l, n_, 4 * hf:4 * hf + 4].rearrange("h v k -> v h k"))
                  load_sin(0)

              def store_state(Hc, dst_h, stout=stout):
                  pt = PSB[7]
                  for c in range(4):
                      P.transpose(pt[:, c * 128:(c + 1) * 128], Hc[:, c, :], identf)
                  ptv = pt.re("p (c s) -> p c s", c=4)
                  for p_ in range(2):
                      ph = slice(p_ * 64, p_ * 64 + 64)
                      P.copy("dve", stout[ph], ptv[ph, :, p_ * 64:p_ * 64 + 64])
                      P.dma("sp", dst_h.rearrange("(c p) v k -> p v c k", p=2)[p_], stout[ph])

              pipe = PA_W >= 512
              X0, X1 = XAB
              Y0, Y1 = YAB
              smk = smask[0:C, 0:C]
              imk = imask[0:C, 0:C]
              smt = smaskT[0:C, 0:C]
              idn = identb[0:C, 0:C]
              bc4 = lambda m_: Tile(m_.ap.unsqueeze(1).broadcast_to([C, 4, C]), m_.keys)

              def parow(j_, words, pattern, **kw):
                  return Tile(pa_ap[:, j_, 0:words].bitcast(BF16).rearrange(pattern, **kw), PAKEYS[j_])
              if pipe:
                  sets = [dict(Vpair=Vpair, Vpad=Vpad, AKt=AKt, MkT=MkT, QT=QT,
                               Yfin=parow(4, 256, "p (h s) -> p h s", h=8)),
                          dict(Vpair=parow(3, 256, "p (h s) -> p h s", h=4),
                               Vpad=parow(0, 512, "p (h s) -> p h s", h=8),
                               AKt=parow(1, 512, "p (k c s) -> p k c s", k=2, c=4),
                               MkT=parow(5, 256, "p (h s) -> p h s", h=8),
                               QT=parow(2, 512, "p (k h s) -> p k h s", k=2, h=8),
                               Yfin=parow(6, 256, "p (h s) -> p h s", h=8))]
                  P.memset("dve", sets[1]["Vpad"], 0.0)
              else:
                  sets = [dict(Vpair=Vpair, Vpad=Vpad, AKt=AKt, MkT=MkT, QT=QT, Yfin=None)] * 2

              def P1_stages(n, S):
                  Vpair_, Vpad_, AKt_, MkT_, QT_ = S["Vpair"], S["Vpad"], S["AKt"], S["MkT"], S["QT"]

                  def stA():
                      pvt = PSB[3]
                      for c in range(4):
                          P.mm(pvt[0:C, c * 128:(c + 1) * 128], vb[:, c, n * C:(n + 1) * C], identb)
                      pvv4 = pvt[0:C, 0:512].re("p (c s) -> p c s", c=4)
                      with P.indep():
                          P.copy("act", Vpair_[0:C], pvv4)
                          for q_ in range(2):
                              P.copy("act", Vpad_[0:C].re("p (c q) s -> p c q s", q=2)[:, :, q_, q_ * 64:(q_ + 1) * 64],
                                     pvv4[:, :, q_ * 64:(q_ + 1) * 64])
                      for k_ in range(2):
                          for c in range(4):
                              P.mm(PSB[4 + k_][0:C, c * 128:(c + 1) * 128], AK[:, c, n, k_, :], identb)
                      for k_ in range(2):
                          P.copy("act" if k_ == 0 else "dve", AKt_[0:C, k_],
                                 PSB[4 + k_][0:C, 0:512].re("p (c s) -> p c s", c=4))

                  def stB():
                      Gq = [[PSB[2 * q_ + k_][0:C, 0:4 * 2 * C].re("p (c s) -> p c s", c=4) for k_ in range(2)]
                            for q_ in range(2)]
                      PXq = [PSB[4 + q_][0:C, 0:4 * C].re("p (c s) -> p c s", c=4) for q_ in range(2)]
                      for h in range(8):
                          c, p_ = h // 2, h % 2
                          ph = slice(p_ * 64, p_ * 64 + 64)
                          for k_ in range(2):
                              P.mm(Gq[p_][k_][:, c, :], AK[ph, c, n, k_, :], BR[ph, c, n, :, :].re("p k s -> p (k s)"))
                          P.mm(PXq[p_][:, c, :], BR[ph, c, n, 0, :], AK[ph, c, n, 0, :])
                      X0q = X0[0:C].re("p (c q) k s -> p c q k s", q=2)
                      MkTq = MkT_[0:C].re("p (c q) s -> p c q s", q=2)
                      with P.indep():
                          for q_ in range(2):
                              P.tt("dve", X0q[:, :, q_, 1, 0:C], Gq[q_][0][:, :, 0:C], bc4(smk), ALU.mult)
                              P.tt("dve", MkTq[:, :, q_, 0:C], Gq[q_][1][:, :, 0:C], bc4(smk), ALU.mult)
                              for k_ in range(2):
                                  P.tt("dve", QT_[0:C, k_].re("p (c q) s -> p c q s", q=2)[:, :, q_, 0:C],
                                       Gq[q_][k_][:, :, C:2 * C], bc4(imk), ALU.mult)
                              P.tt("dve", X0q[:, :, q_, 0, 0:C], PXq[q_], bc4(smt), ALU.mult)
                      P.tt("dve", Y0[0:C, :, 0:C], X0[0:C, :, 1, 0:C],
                           Tile(idn.ap.unsqueeze(1).broadcast_to([C, 8, C]), idn.keys), ALU.add)

                  def mk_level(lev):
                      def stC():
                          lastlev = lev == nlev - 1
                          Xc, Xn = (X0, X1) if lev % 2 == 0 else (X1, X0)
                          Yc, Yn = (Y0, Y1) if lev % 2 == 0 else (Y1, Y0)
                          if lastlev and S["Yfin"] is not None:
                              Yn = S["Yfin"]
                          PL = ps_multi(0, 2)[0:C, 0:8 * 2 * C].re("p (h k s) -> p h k s", h=8, k=2)
                          for h in range(8):
                              P.mm(PL[:, h, 0, :], Xc[0:C, h, 1, 0:C], Xc[0:C, h, 0, 0:C])
                              if not lastlev:
                                  P.mm(PL[:, h, 1, :], Xc[0:C, h, 0, 0:C], Xc[0:C, h, 1, 0:C])
                          for hh in range(2):
                              hs = slice(4 * hh, 4 * hh + 4)
                              ce = "act" if hh == 0 else "dve"
                              if lastlev:
                                  P.copy(ce, Xn[0:C, hs, 0, 0:C], PL[:, hs, 0, :])
                              else:
                                  P.copy(ce, Xn[0:C, hs, :, 0:C], PL[:, hs])
                          PY = PSB[2][0:C, 0:8 * C].re("p (h s) -> p h s", h=8)
                          for h in range(8):
                              P.mm(PY[:, h, :], Xn[0:C, h, 0, 0:C], Yc[0:C, h, 0:C])
                          P.tt("dve", Yn[0:C, :, 0:C], PY, Yc[0:C, :, 0:C], ALU.add)
                      return stC
                  return [stA, stB] + [mk_level(lev) for lev in range(nlev)]

              def P2_stages(n, S):
                  Vpair_, Vpad_, AKt_, MkT_, QT_ = S["Vpair"], S["Vpad"], S["AKt"], S["MkT"], S["QT"]
                  Yf = S["Yfin"] if S["Yfin"] is not None else ((Y0, Y1)[nlev % 2])

                  def stD0():
                      if not is_p:
                          pt = PSB[7]
                          if dbl:
                              if n + 1 < nch:
                                  load_sin(n + 1)
                              for c in range(4):
                                  sh_ = sin_sets[n % 2][c // 2]
                                  P.mm(pt[:, c * 64:(c + 1) * 64],
                                       sh_[0:64, 2 * (c % 2):2 * (c % 2) + 2, :].re("p h k -> p (h k)"), identf[0:64, 0:64])
                          else:
                              sin = mtile("Wsb", 8 * 128 // 2, F32)[0:64, 0:512].re("p (h k) -> p h k", h=8)
                              P.dma("sp", sin, st_rwkv[l, n].rearrange("h v k -> v h k"))
                              for c in range(4):
                                  P.mm(pt[:, c * 64:(c + 1) * 64], sin[:, 2 * c:2 * c + 2, :].re("p h k -> p (h k)"),
                                       identf[0:64, 0:64])
                          ptv = pt[:, 0:256].re("p (c s) -> p c s", c=4)
                          for p_ in range(2):
                              ph = slice(p_ * 64, p_ * 64 + 64)
                              P.copy("dve", Hs[ph, :, p_ * 64:p_ * 64 + 64], ptv[ph])
                      P.copy("act", Hbd, Hcur)

                  def stD1():
                      PW = PSB[6][0:C, 0:512].re("p (c s) -> p c s", c=4)
                      for c in range(4):
                          P.mm(PW[:, c, :], BR[:, c, n, 0, :], Hbd[:, c, :], start=True, stop=False)
                          for p_ in range(2):
                              h = 2 * c + p_
                              P.mm(PW[:, c, :], MkT_[0:C, h, 0:C], Vpad_[0:C, h, :], start=False, stop=(p_ == 1))
                      P.copy("act", Wsb[0:C, 0:4], PW)

                  def stE():
                      PU = PSB[7][0:C, 0:512].re("p (c s) -> p c s", c=4)
                      for h in range(8):
                          c, p_ = h // 2, h % 2
                          vs = slice(p_ * 64, p_ * 64 + 64)
                          P.mm(PU[:, c, vs], Yf[0:C, h, 0:C], Wsb[0:C, c, vs])
                      with P.indep():
                          P.copy("act", Upair[0:C], PU)
                          for q_ in range(2):
                              P.copy("act", Upad[0:C].re("p (c q) s -> p c q s", q=2)[:, :, q_, q_ * 64:(q_ + 1) * 64],
                                     PU[:, :, q_ * 64:(q_ + 1) * 64])

                  def stF():
                      PO = PSB[6][:, 0:4 * C].re("p (c s) -> p c s", c=4)
                      for c in range(4):
                          P.mm(PO[:, c, :], Hbd[:, c, :], BR[:, c, n, 1, :], start=True, stop=False)
                          for p_ in range(2):
                              h = 2 * c + p_
                              P.mm(PO[:, c, :], Upad[0:C, h, :], QT_[0:C, 0, h, 0:C], start=False, stop=False)
                              P.mm(PO[:, c, :], Vpad_[0:C, h, :], QT_[0:C, 1, h, 0:C], start=False, stop=(p_ == 1))
                      P.copy("act", osb[:, :, n * C:(n + 1) * C], PO)

                  def stG():
                      PH = PSB[7].re("p (c s) -> p c s", c=4)
                      for c in range(4):
                          P.mm(PH[:, c, :], AKt_[0:C, 0, c, :], Upair[0:C, c, :], start=True, stop=False)
                          P.mm(PH[:, c, :], AKt_[0:C, 1, c, :], Vpair_[0:C, c, :], start=False, stop=True)
                      with P.indep():
                          for p_ in range(2):
                              ph = slice(p_ * 64, p_ * 64 + 64)
                              P.tt("dve", Hcur[ph, :, ph], Hcur[ph, :, ph], PH[ph, :, ph], ALU.add)
                      with P.indep():
                          for p_ in range(2):
                              ph = slice(p_ * 64, p_ * 64 + 64)
                              pcb = PC[ph, :, n:n + 1]
                              P.tt("dve", Hcur[ph, :, ph], Hcur[ph, :, ph],
                                   Tile(pcb.ap.broadcast_to([64, 4, 64]), pcb.keys), ALU.mult)
                      if not is_p:
                          store_state(Hs, o_rwkv_s[l, n], stouts[n % 2] if dbl else stout)
                  return [stD0, stD1, stE, stF, stG]

              for st_ in P1_stages(0, sets[0]):
                  st_()
              for n in range(nch):
                  p2 = P2_stages(n, sets[n % 2])
                  p1 = P1_stages(n + 1, sets[(n + 1) % 2]) if n + 1 < nch else []
                  seq = []
                  i1_, i2_ = 0, 0
                  while i1_ < len(p1) or i2_ < len(p2):
                      if i2_ < len(p2):
                          seq.append(p2[i2_]); i2_ += 1
                      if i1_ < len(p1):
                          seq.append(p1[i1_]); i1_ += 1
                  for st_ in seq:
                      st_()
              if last_p:
                  store_state(Hm[l], o_rwkv_p[l])

              stage(9)
              def run2(gens, lag=2):
                  live = [True] * len(gens)
                  step = 0
                  while any(live):
                      for gi in range(len(gens)):
                          if gi * lag > step or not live[gi]:
                              continue
                          try:
                              next(gens[gi])
                          except StopIteration:
                              live[gi] = False
                      step += 1

              def rpost_gen(c, t1, t2, ta, b0):
                  oc = osb[:, c]
                  p1, p2 = PSB[b0], PSB[b0 + 1]
                  P.act(t1, oc, AF.Square)
                  P.mm(p1[:, 0:NT], bdones, oc)
                  P.mm(p2[:, 0:NT], bdones, t1)
                  yield
                  P.act(t2, p1[:, 0:NT], AF.Copy, scale=1.0 / 64)
                  yield
                  P.act(ta, t2, AF.Square)
                  yield
                  P.stt(ta, p2[:, 0:NT], 1.0 / 64, ta, ALU.mult, ALU.subtract)
                  yield
                  P.act(ta, ta, AF.Sqrt, bias=GN_EPS, scale=1.0)
                  yield
                  P.recip(ta, ta)
                  yield
                  P.tt("dve", t2, oc, t2, ALU.subtract)
                  yield
                  P.tt("dve", t2, t2, ta, ALU.mult)
                  yield
                  P.act(t2, t2, AF.Identity, bias=pv["lb"][:, l, c:c + 1], scale=pv["lg"][:, l, c:c + 1])
                  yield
                  P.tt("dve", t2, t2, bon[:, c], ALU.add)
                  yield
                  P.tt("dve", mixT[c][:, 0:NT], t2, gst[:, c], ALU.mult)
                  yield
              for c0 in (0, 2):
                  run2([rpost_gen(c0, T_1, T_2, T_a, 0), rpost_gen(c0 + 1, T_kk, T_km, T_eM, 2)])

              stage(10)
              def consume_gla(j, pt, m):
                  P.copy("act", pa_flat(j)[0:m], pt[0:m, 0:NT])
              proj_feature_major(NT, w_in[l], SHIFT_W, IN_W - SHIFT_W, consume_gla)
              flo = lor
              P.copy("act", flo[0:16], pa_flat(12)[0:16])
              AKflat = mtile("AK", 4 * NT_, BF16)
              qtz = AKflat[:, 0:4 * NT_].re("p (c q n) -> p c q n", c=2, q=2)[:, :, :, 0:NT]
              ktb = AKflat[:, 4 * NT_:6 * NT_].re("p (c n) -> p c n", c=2)[:, :, 0:NT]
              P.memset("dve", AKflat[:, 0:4 * NT_], 0.0)
              kdb = mtile("BR", 4 * NT_, BF16, "p (c n) -> p c n", c=4)[:, 0:2, 0:NT]
              vgb = mtile("vb", 4 * NT_ // 2, BF16, "p (c n) -> p c n", c=4)[:, :, 0:NT]
              ogs = osb
              PCg = PC
              for c in range(2):
                  pz = PSB[c]
                  P.mm(pz[:, 0:NT], fup[0:16, l, c * 128:(c + 1) * 128], flo[0:16])
                  P.act(T_s1, pz[:, 0:NT], AF.Sigmoid, bias=fb_t[:, l, c:c + 1], scale=1.0)
                  P.act(T_s1, T_s1, AF.Ln)
                  P.scan(T_cs, rmask[:, 0:NT], T_s1)
                  P.act(T_eP, T_cs, AF.Exp, scale=1.0 / 16)
                  P.act(T_eM, T_cs, AF.Exp, scale=-1.0 / 16)
                  P.copy("act", PCg[:, c], cv(T_eP)[:, :, C - 1])
                  for q_ in range(2):
                      ph = slice(q_ * 64, q_ * 64 + 64)
                      P.stt(qtz[ph, c, q_], pa_flat(c)[ph], 0.125, T_eP[ph], ALU.mult, ALU.mult)
                  P.tt("dve", T_1, pa_flat(2 + c), T_eM, ALU.mult)
                  P.copy("act", ktb[:, c], T_1)
                  pcb = PCg[:, c, :]
                  P.tt("dve", cv(kdb[:, c]), cv(T_1), Tile(pcb.ap.unsqueeze(2).broadcast_to([128, nch, C]), pcb.keys), ALU.mult)
              for h in range(4):
                  P.copy("act", vgb[:, h], pa_flat(4 + h))
              Sgb = mtile("Sgb", 2 * 128 // 2, BF16, "p (c s) -> p c s", c=2)
              ATsb = mtile("ATsb", 4 * 64 // 2, BF16, "p (h s) -> p h s", h=4)
              Vt = mtile("Vt", 4 * 128 // 2, BF16, "p (h s) -> p h s", h=4)
              kdt = mtile("kdt", 2 * 128 // 2, BF16, "p (c s) -> p c s", c=2)
              pre_s = (not is_p) and PA_W >= 336 and NT_ >= 320 and NT <= 64 and nch <= 16
              if is_p:
                  Scur = Sg[l]
              elif pre_s:
                  osb_full = mtile("osb", 4 * NT_, F32, "p (c n) -> p c n", c=4).ap
                  Sbufs = []
                  for i_ in range(nch):
                      apb = pa_ap[:, i_, 80:336] if i_ < 14 else osb_full[:, i_ - 14, 64:320]
                      Sbufs.append(Tile(apb.rearrange("p (c s) -> p c s", c=2), "m.Sb%d" % i_))
                  for n in range(nch):
                      for c in range(2):
                          for p_ in range(2):
                              P.dma("sp", Sbufs[n][p_ * 64:(p_ + 1) * 64, c, :], st_gla[l, n, 2 * c + p_])
              else:
                  Scur = mtile("sttmp", 8 * 64, F32)[:, 0:256].re("p (c s) -> p c s", c=2)
              for n in range(nch):
                  if pre_s:
                      Scur = Sbufs[n]
                  elif not is_p:
                      for c in range(2):
                          for p_ in range(2):
                              P.dma("sp", Scur[p_ * 64:(p_ + 1) * 64, c, :], st_gla[l, n, 2 * c + p_])
                  P.copy("act", Sgb, Scur)
                  pvt = PSB[6]
                  for h in range(4):
                      P.mm(pvt[0:C, h * 128:(h + 1) * 128], vgb[:, h, n * C:(n + 1) * C], identb)
                  P.copy("act", Vt[0:C], pvt[0:C, 0:512].re("p (h s) -> p h s", h=4))
                  pkt = PSB[5]
                  for c in range(2):
                      P.mm(pkt[0:C, c * 128:(c + 1) * 128], kdb[:, c, n * C:(n + 1) * C], identb)
                  P.copy("act", kdt[0:C], pkt[0:C, 0:256].re("p (c s) -> p c s", c=2))
                  PA_ = PSB[0][0:C, 0:4 * C].re("p (h s) -> p h s", h=4)
                  for h in range(4):
                      c, p_ = h // 2, h % 2
                      ph = slice(p_ * 64, p_ * 64 + 64)
                      P.mm(PA_[:, h, :], ktb[:, c, n * C:(n + 1) * C], qtz[:, c, p_, n * C:(n + 1) * C])
                  imk = imask[0:C, 0:C]
                  P.tt("dve", ATsb[0:C, :, 0:C], PA_, Tile(imk.ap.unsqueeze(1).broadcast_to([C, 4, C]), imk.keys), ALU.mult)
                  PO = PSB[1][:, 0:4 * C].re("p (h s) -> p h s", h=4)
                  for h in range(4):
                      c, p_ = h // 2, h % 2
                      ph = slice(p_ * 64, p_ * 64 + 64)
                      P.mm(PO[:, h, :], Vt[0:C, h, :], ATsb[0:C, h, 0:C], start=True, stop=False)
                      P.mm(PO[:, h, :], Sgb[:, c, :], qtz[:, c, p_, n * C:(n + 1) * C], start=False, stop=True)
                  P.copy("act", ogs[:, :, n * C:(n + 1) * C], PO)
                  PS_ = PSB[2].re("p (h s) -> p h s", h=4)
                  for h in range(4):
                      c = h // 2
                      P.mm(PS_[:, h, :], kdt[0:C, c, :], Vt[0:C, h, :])
                  with P.indep():
                      for h in range(4):
                          c, p_ = h // 2, h % 2
                          ph = slice(p_ * 64, p_ * 64 + 64)
                          P.stt(Scur[ph, c, :], Scur[ph, c, :], PCg[ph, c, n:n + 1], PS_[ph, h, :], ALU.mult, ALU.add)
                  if not is_p:
                      for c in range(2):
                          for p_ in range(2):
                              P.dma("sp", o_gla_s[l, n, 2 * c + p_], Scur[p_ * 64:(p_ + 1) * 64, c, :])
              if last_p:
                  for c in range(2):
                      for p_ in range(2):
                          P.dma("sp", o_gla_p[l, 2 * c + p_], Sg[l][p_ * 64:(p_ + 1) * 64, c, :])
              def gpost_gen(h, t1, t2, ta, b0):
                  oc = ogs[:, h]
                  p1 = PSB[b0]
                  P.act(t1, oc, AF.Square)
                  P.mm(p1[:, 0:NT], allones, t1)
                  yield
                  P.act(t2, p1[:, 0:NT], AF.Sqrt, bias=RMS_EPS, scale=1.0 / 128)
                  yield
                  P.recip(t2, t2)
                  yield
                  P.tt("dve", t2, oc, t2, ALU.mult)
                  P.act(ta, pa_flat(8 + h), AF.Silu)
                  yield
                  P.stt(mixT[4 + h][:, 0:NT], t2, ng_t[:, l:l + 1], ta, ALU.mult, ALU.mult)
                  yield
              for h0 in (0, 2):
                  run2([gpost_gen(h0, T_1, T_2, T_a, 0), gpost_gen(h0 + 1, T_kk, T_km, T_eM, 1)])

              stage(11)
              P.barrier()
              wo_v = w_out[l].rearrange("(k p) c -> p k c", p=128)
              proj_token_major(NT, lambda k: mixT[k], KC,
                               lambda k0, nk_, hf: wo_v[:, k0:k0 + nk_, hf * 512:(hf + 1) * 512],
                               ln1_g[l], ln1_b[l])
              make_xT(NT)

              stage(12)
              hT = ftile("hT", FC * NT_ // 2, BF16, "p (j n) -> p j n", j=FC)[:, :, 0:NT]
              gts = [ftile("gt%d" % i, GT_W, F32)[:, 0:nseq * (Tq + 2)].re("p (s t) -> p s t", t=Tq + 2) for i in range(2)]
              accs = [ftile("acc%d" % i, NT_, F32)[:, 0:NT].re("p (s t) -> p s t", t=Tq) for i in range(2)]
              wu_v = w_up[l].rearrange("(k p) c -> p k c", p=128)
              gate_ps = {}
              if not is_p:
                  cpast = ftile("cpast", FC * NSQ * 2, F32, "p (j s r) -> p j s r", j=FC, r=2)
                  cout = ftile("cout", FC * NSQ * 2, F32, "p (j s r) -> p j s r", j=FC, r=2)
                  R2 = NSQ * 2
                  ctok = ftile("ctok", DFF, F32)
                  if R2 <= 32:
                      P.dma("sp", ctok[0:R2], st_conv[l].rearrange("s r w -> (s r) w"))
                      cpf = cpast.re("p j s r -> p (j s r)")
                      for j in range(FC):
                          bk, cc = divmod(j, 16)
                          P.mm(PSB[4 + bk][:, cc * R2:(cc + 1) * R2], ctok[0:R2, j * 128:(j + 1) * 128], identf[0:R2, 0:R2])
                      P.copy("act", cpf[:, 0:16 * R2], PSB[4][:, 0:16 * R2])
                      P.copy("act", cpf[:, 16 * R2:FC * R2], PSB[5][:, 0:(FC - 16) * R2])
                  else:
                      for j in range(FC):
                          for r_ in range(2):
                              P.dma("sp", cpast[:, j, :, r_], st_conv[l, :, r_, j * 128:(j + 1) * 128].rearrange("s p -> p s"))
              j0 = 0
              while j0 < FC:
                  nj = min(4, FC - j0)
                  wg = wload(wu_v[:, :, j0 * 128:(j0 + nj) * 128], KC, nj * 128)
                  wvv = wload(wu_v[:, :, DFF + j0 * 128: DFF + (j0 + nj) * 128], KC, nj * 128)
                  for jj in range(nj):
                      j = j0 + jj
                      pg_, pv_ = PSB[(2 * j) % 4], PSB[(2 * j + 1) % 4]
                      for k in range(KC):
                          P.mm(pg_[:, 0:NT], wg[:, k, jj * 128:(jj + 1) * 128], xT[:, k, 0:NT], start=(k == 0), stop=(k == KC - 1))
                      for k in range(KC):
                          P.mm(pv_[:, 0:NT], wvv[:, k, jj * 128:(jj + 1) * 128], xT[:, k, 0:NT], start=(k == 0), stop=(k == KC - 1))
                      gt = gts[j % 2]
                      acc = accs[j % 2]
                      P.copy("act", gt[:, :, 2:], pg_[:, 0:NT].re("p (s t) -> p s t", t=Tq))
                      if is_p:
                          P.copy("act", gt[:, 0, 0:2], cst[l][:, j, :])
                          P.copy("act", cst[l][:, j, :], gt[:, 0, Tq:Tq + 2])
                          if last_p:
                              P.dma("sp", o_conv_p[l, :, j * 128:(j + 1) * 128].rearrange("r p -> p r"), cst[l][:, j, :])
                      else:
                          P.copy("act", gt[:, :, 0:2], cpast[:, j])
                          P.copy("act", cout[:, j], gt[:, :, Tq:Tq + 2])
                      P.act(acc, gt[:, :, 2:], AF.Identity, bias=cb_t[:, l, j:j + 1], scale=cw_t[:, l, 2, j:j + 1])
                      P.stt(acc, gt[:, :, 1:Tq + 1], cw_t[:, l, 1, j:j + 1], acc, ALU.mult, ALU.add)
                      P.stt(acc, gt[:, :, 0:Tq], cw_t[:, l, 0, j:j + 1], acc, ALU.mult, ALU.add)
                      P.act(acc, acc, AF.Gelu)
                      P.tt("dve", hT[:, j].re("p (s t) -> p s t", t=Tq), acc, pv_[:, 0:NT].re("p (s t) -> p s t", t=Tq), ALU.mult)
                  j0 += nj
              if not is_p:
                  if R2 <= 32:
                      cof = cout.re("p j s r -> p j (s r)")
                      for j in range(FC):
                          bk, cc = divmod(j, 4)
                          P.mm(PSB[bk][0:R2, cc * 128:(cc + 1) * 128], cof[:, j, :], identf)
                      for bk in range((FC + 3) // 4):
                          w_ = min(512, DFF - bk * 512)
                          P.copy("act", ctok[0:R2, bk * 512:bk * 512 + w_], PSB[bk][0:R2, 0:w_])
                      P.dma("sp", o_conv_s[l].rearrange("s r w -> (s r) w"), ctok[0:R2])
                  else:
                      for j in range(FC):
                          for r_ in range(2):
                              P.dma("sp", o_conv_s[l, :, r_, j * 128:(j + 1) * 128].rearrange("s p -> p s"), cout[:, j, :, r_])

              stage(13)
              wd_v = w_down[l].rearrange("(k p) c -> p k c", p=128)
              proj_token_major(NT, lambda k: hT[:, k], FC,
                               lambda k0, nk_, hf: wd_v[:, k0:k0 + nk_, hf * 512:(hf + 1) * 512],
                               ln2_g[l], ln2_b[l])
              if l < L - 1:
                  make_xT(NT)
              P.barrier()

          ntile = (NT + 127) // 128
          for t in range(ntile):
              rows = min(128, NT - t * 128)
              dst = y_p[b * TB + t * 128: b * TB + t * 128 + rows, :] if is_p else y_s[t * 128:t * 128 + rows, :]
              P.dma("sp", dst, xres[t][0:rows])

    except _Stop:
        pass
    P.finish()
    P.emit(st)
    st.close()
    return nc, P, dbg


N_CORES = 8
_CACHE = {}


def _core_inputs(inp, i, NSQ):
    sl = slice(i * NSQ, (i + 1) * NSQ)
    L = inp["w_in"].shape[0]
    c = np.ascontiguousarray
    m = {
        "x_p": c(inp["x_prompt"][i]),
        "x_s": c(inp["x_sample"][sl].reshape(NSQ * DEC_SEQ, D)),
        "st_rwkv": c(inp["state_rwkv"][:, sl]),
        "st_shift": c(inp["state_shift"][:, sl, 0, :]),
        "st_gla": c(inp["state_gla"][:, sl]),
        "st_conv": c(inp["state_conv"][:, sl]),
        "r_k": c(inp["r_k"].reshape(L, AW)),
    }
    for k in ("w_in", "tok_mu", "w0", "w_lora_up", "a0", "a_lora_up", "g_lora_up", "k_k", "k_a", "lnx_g", "lnx_b",
              "vres_bias", "vres_down", "vres_up", "gla_f_up", "gla_f_bias", "gla_norm_g", "w_out", "ln1_g",
              "ln1_b", "w_up", "conv_w", "conv_b", "w_down", "ln2_g", "ln2_b"):
        m[k] = c(inp[k])
    return m


def run(inp, TB=512, debug=False, n_cores=N_CORES, trace=False, stop=99):
    inp = {k: np.asarray(v, dtype=np.float32) for k, v in inp.items()}
    L = inp["w_in"].shape[0]
    B, SEQ, _ = inp["x_prompt"].shape
    DB = inp["x_sample"].shape[0]
    assert B == n_cores
    NSQ = DB // n_cores
    TB = min(TB, SEQ)
    key = (L, SEQ, NSQ, TB, debug, stop)
    if key not in _CACHE:
        _CACHE[key] = build_program(L, SEQ, NSQ, TB, debug, stop)
    nc, P, dbg = _CACHE[key]
    in_maps = [_core_inputs(inp, i, NSQ) for i in range(n_cores)]
    res = run_bass_kernel_spmd(nc, in_maps, core_ids=list(range(n_cores)), **({"trace": True} if trace else {}))
    R = res.results
    cat = lambda name, axis: np.concatenate([r[name] for r in R], axis=axis)
    stk = lambda name: np.stack([r[name] for r in R], axis=1)
    outs = (
        np.stack([r["y_p"] for r in R], axis=0),
        cat("y_s", 0).reshape(DB, DEC_SEQ, D),
        stk("o_rwkv_p"),
        cat("o_rwkv_s", 1),
        stk("o_shift_p")[:, :, None, :],
        cat("o_shift_s", 1)[:, :, None, :],
        stk("o_gla_p"),
        cat("o_gla_s", 1),
        stk("o_conv_p"),
        cat("o_conv_s", 1),
    )
    outs = tuple(np.ascontiguousarray(o, dtype=np.float32) for o in outs)
    if debug:
        return outs, R, res
    return outs


def kernel(**inputs):
    return run(inputs)
```

```python
import math
import numpy as np
from contextlib import ExitStack
import concourse.bass as bass
import concourse.mybir as mybir
from concourse.bass_utils import run_bass_kernel_spmd

F32 = mybir.dt.float32
BF16 = mybir.dt.bfloat16
ALU = mybir.AluOpType
AF = mybir.ActivationFunctionType
ENGS = ("pe", "act", "dve", "pool", "sp")
USE_DRAIN = True


class Tile:
    __slots__ = ("ap", "keys")

    def __init__(self, ap, keys):
        self.ap = ap
        self.keys = keys if isinstance(keys, tuple) else (keys,)

    def __getitem__(self, idx):
        return Tile(self.ap[idx], self.keys)

    def v(self, ap):
        return Tile(ap, self.keys)

    def re(self, pattern_, **kw):
        return Tile(self.ap.rearrange(pattern_, **kw), self.keys)

    def bc(self, shape):
        return Tile(self.ap.broadcast_to(shape), self.keys)


class Op:
    __slots__ = ("eng", "fn", "is_dma", "deps", "inc", "ticket", "waits", "dsem", "dval", "idx", "prewait", "drain", "grp")

    def __init__(self, eng, fn, is_dma):
        self.eng = eng
        self.fn = fn
        self.is_dma = is_dma
        self.deps = set()
        self.inc = False
        self.ticket = 0
        self.waits = []
        self.dsem = None
        self.dval = 0
        self.prewait = None
        self.drain = False
        self.grp = None


def _keys(ts):
    out = []
    for t in ts:
        if isinstance(t, Tile):
            out.extend(t.keys)
    return out


def _ap(t):
    return t.ap if isinstance(t, Tile) else t


class Prog:
    skip_same = ("pe",)
    grp = None
    _gctr = 0

    def indep(self):
        prog = self

        class _G:
            def __enter__(self_):
                prog._gctr += 1
                self_.prev = prog.grp
                prog.grp = prog._gctr

            def __exit__(self_, *a):
                prog.grp = self_.prev
                return False
        return _G()

    def __init__(self, nc, n_dma_sems=10):
        self.nc = nc
        self.ops = []
        self.last_writer = {}
        self.readers = {}
        self.n_dma_sems = n_dma_sems

    def add(self, eng, fn, reads, writes, is_dma=False):
        op = Op(eng, fn, is_dma)
        op.idx = len(self.ops)
        op.grp = self.grp
        deps = op.deps
        lw, rd = self.last_writer, self.readers
        for k in reads:
            w = lw.get(k)
            if w is not None:
                deps.add(w)
        for k in writes:
            w = lw.get(k)
            if w is not None:
                deps.add(w)
            r = rd.get(k)
            if r:
                deps.update(r)
        for k in writes:
            lw[k] = op.idx
            rd[k] = []
        ws = set(writes)
        for k in reads:
            if k not in ws:
                rd.setdefault(k, []).append(op.idx)
        if self.bar_deps and eng not in self.bar_seen:
            deps.update(self.bar_deps)
            self.bar_seen.add(eng)
        deps.discard(op.idx)
        if eng in self.skip_same and not is_dma:
            for d in [d for d in deps if self.ops[d].eng == eng and not self.ops[d].is_dma]:
                deps.discard(d)
        self.ops.append(op)
        return op

    bar_deps = None
    bar_seen = None
    bar_start = 0

    def barrier(self):
        last = {}
        deps = set()
        for op in self.ops[self.bar_start:]:
            last[op.eng] = op.idx
            if op.is_dma:
                deps.add(op.idx)
        deps.update(last.values())
        if self.bar_deps and len(self.bar_seen) < len(ENGS):
            deps.update(self.bar_deps)
        self.bar_deps = deps
        self.bar_seen = set()
        self.bar_start = len(self.ops)
        self.last_writer = {}
        self.readers = {}

    def emit(self, stack):
        nc = self.nc
        ops = self.ops
        drained_upto = {e: -1 for e in ENGS}
        for op in ops:
            best = {}
            keep = set()
            same = []
            for d in op.deps:
                dop = ops[d]
                if dop.is_dma:
                    keep.add(d)
                elif dop.eng == op.eng and not op.is_dma and op.eng in ("act", "dve", "pool"):
                    same.append(d)
                elif best.get(dop.eng, -1) < d:
                    best[dop.eng] = d
            if same:
                need = [d for d in same if d >= drained_upto[op.eng]
                        and not (op.grp is not None and ops[d].grp == op.grp)]
                if need:
                    op.drain = True
                    drained_upto[op.eng] = op.idx
            keep.update(best.values())
            op.deps = keep
            for d in keep:
                ops[d].inc = True
        sems = {e: stack.enter_context(nc.semaphore("s_" + e)) for e in ENGS}
        dsems = {e: [stack.enter_context(nc.semaphore("d_%s_%d" % (e, i))) for i in range(self.n_dma_sems)]
                 for e in ENGS}
        cnt = {e: 0 for e in ENGS}
        dcnt = {e: [0] * self.n_dma_sems for e in ENGS}
        drr = {e: 0 for e in ENGS}
        for op in ops:
            if op.is_dma:
                i = drr[op.eng]
                drr[op.eng] = (i + 1) % self.n_dma_sems
                op.dsem = dsems[op.eng][i]
                if dcnt[op.eng][i] > 0:
                    op.prewait = (op.dsem, dcnt[op.eng][i])
                dcnt[op.eng][i] += 16
                op.dval = dcnt[op.eng][i]
            elif op.inc:
                cnt[op.eng] += 1
                op.ticket = cnt[op.eng]
        waited = {e: {} for e in ENGS}
        for op in ops:
            w = waited[op.eng]
            need = {}
            for d in op.deps:
                dop = ops[d]
                if dop.is_dma:
                    s, v = dop.dsem, dop.dval
                else:
                    s, v = sems[dop.eng], dop.ticket
                k = id(s)
                if w.get(k, 0) >= v:
                    continue
                if k not in need or need[k][1] < v:
                    need[k] = (s, v)
            if op.prewait is not None:
                s, v = op.prewait
                k = id(s)
                if w.get(k, 0) < v and (k not in need or need[k][1] < v):
                    need[k] = (s, v)
            for k, (s, v) in need.items():
                w[k] = v
            op.waits = list(need.values())
        per = {e: [op for op in ops if op.eng == e] for e in ENGS}
        self.stats = {e: (len(per[e]), cnt[e], sum(len(o.waits) for o in per[e])) for e in ENGS}

        def run(eng_name, eng):
            se = sems[eng_name]
            for op in per[eng_name]:
                for (s, v) in op.waits:
                    eng.wait_ge(s, v)
                if op.drain and USE_DRAIN:
                    eng.drain()
                ins = op.fn(eng)
                if op.is_dma:
                    ins.then_inc(op.dsem, 16)
                elif op.inc:
                    ins.then_inc(se, 1)

        with nc.Block() as block:
            @block.tensor
            def _(eng):
                run("pe", eng)

            @block.scalar
            def _(eng):
                run("act", eng)

            @block.vector
            def _(eng):
                run("dve", eng)

            @block.gpsimd
            def _(eng):
                run("pool", eng)

            @block.sync
            def _(eng):
                run("sp", eng)

    def mm(self, out, lhsT, rhs, start=True, stop=True):
        o, l, r = out.ap, lhsT.ap, rhs.ap
        return self.add("pe", lambda e: e.matmul(o, l, r, start=start, stop=stop),
                        _keys([lhsT, rhs]), _keys([out]))

    def transpose(self, out, in_, ident):
        o, i, d = out.ap, in_.ap, ident.ap
        return self.add("pe", lambda e: e.transpose(o, i, d), _keys([in_, ident]), _keys([out]))

    def act(self, out, in_, func, bias=None, scale=None):
        o, i = out.ap, in_.ap
        kw = {}
        if bias is not None:
            kw["bias"] = _ap(bias)
        if scale is not None:
            kw["scale"] = _ap(scale)
        return self.add("act", lambda e: e.activation(o, i, func, **kw), _keys([in_, bias, scale]), _keys([out]))

    def tt(self, eng, out, in0, in1, op):
        o, a, b = out.ap, in0.ap, in1.ap
        return self.add(eng, lambda e: e.tensor_tensor(o, a, b, op), _keys([in0, in1]), _keys([out]))

    def ts(self, eng, out, in0, s1, op0, s2=None, op1=None):
        o, a = out.ap, in0.ap
        a1, a2 = _ap(s1), _ap(s2)
        if op1 is None:
            return self.add(eng, lambda e: e.tensor_scalar(o, a, a1, None, op0), _keys([in0, s1]), _keys([out]))
        return self.add(eng, lambda e: e.tensor_scalar(o, a, a1, a2, op0, op1), _keys([in0, s1, s2]), _keys([out]))

    def stt(self, out, in0, scalar, in1, op0, op1):
        o, a, b = out.ap, in0.ap, in1.ap
        s = _ap(scalar)
        return self.add("dve", lambda e: e.scalar_tensor_tensor(o, a, s, b, op0, op1),
                        _keys([in0, in1, scalar]), _keys([out]))

    def copy(self, eng, out, in_):
        o, i = out.ap, in_.ap
        if eng == "act":
            return self.add("act", lambda e: e.copy(o, i), _keys([in_]), _keys([out]))
        return self.add(eng, lambda e: e.tensor_copy(o, i), _keys([in_]), _keys([out]))

    def recip(self, out, in_):
        o, i = out.ap, in_.ap
        return self.add("dve", lambda e: e.reciprocal(o, i), _keys([in_]), _keys([out]))

    def memset(self, eng, out, val):
        o = out.ap
        return self.add(eng, lambda e: e.memset(o, val), [], _keys([out]))

    def scan(self, out, d0, d1):
        o, a, b = out.ap, d0.ap, d1.ap
        return self.add("dve", lambda e: e.tensor_tensor_scan(o, a, b, 0.0, ALU.mult, ALU.add),
                        _keys([d0, d1]), _keys([out]))

    def dma(self, eng, out, in_, **kw):
        o, i = _ap(out), _ap(in_)
        return self.add(eng, lambda e: e.dma_start(out=o, in_=i, **kw), _keys([in_]), _keys([out]), is_dma=True)

    def finish(self, eng="sp"):
        deps = [op.idx for op in self.ops if op.is_dma]
        op = self.add(eng, lambda e: e.nop(), [], [])
        op.deps.update(deps)
        for e in ENGS:
            last = [o.idx for o in self.ops if o.eng == e and not o.is_dma and o.idx != op.idx]
            if last:
                op.deps.add(last[-1])
        return op


D = 1024
KC = 8
AW = 512
SHIFT_W = 1792
IN_W = 3344
DFF = 2816
FC = 22
ALPHA = 8 ** 0.25
GN_EPS = 64e-5
RMS_EPS = 1e-5
LN_EPS = 1e-5
KDEC = math.exp(-0.5)
DEC_SEQ = 4


class _Stop(Exception):
    pass


def _round_robin(gens, lag=1):
    live = [True] * len(gens)
    step = 0
    while any(live):
        for gi in range(len(gens)):
            if gi * lag > step or not live[gi]:
                continue
            try:
                next(gens[gi])
            except StopIteration:
                live[gi] = False
        step += 1


def build_program(L, SEQ, NSQ, TB, debug=False, stop=99):
    def stage(n):
        if n > stop:
            raise _Stop()
    nc = bass.Bass("TRN2", target_bir_lowering=False)
    P = Prog(nc)
    NTMAX = max(TB, NSQ * DEC_SEQ)
    NTILE_MAX = (NTMAX + 127) // 128
    NCHMAX = max(TB // 64, NSQ)
    NS4 = NSQ * DEC_SEQ

    def din(name, shape):
        return nc.dram_tensor(name, list(shape), F32, kind="ExternalInput").ap()

    def dout(name, shape):
        return nc.dram_tensor(name, list(shape), F32, kind="ExternalOutput").ap()

    x_p = din("x_p", [SEQ, D])
    x_s = din("x_s", [NS4, D])
    st_rwkv = din("st_rwkv", [L, NSQ, 8, 64, 64])
    st_shift = din("st_shift", [L, NSQ, SHIFT_W])
    st_gla = din("st_gla", [L, NSQ, 4, 64, 128])
    st_conv = din("st_conv", [L, NSQ, 2, DFF])
    w_in = din("w_in", [L, D, IN_W])
    tok_mu = din("tok_mu", [L, SHIFT_W])
    w0 = din("w0", [L, AW])
    w_lora_up = din("w_lora_up", [L, 64, AW])
    a0 = din("a0", [L, AW])
    a_lora_up = din("a_lora_up", [L, 64, AW])
    g_lora_up = din("g_lora_up", [L, 128, AW])
    k_k = din("k_k", [L, AW])
    k_a = din("k_a", [L, AW])
    r_k = din("r_k", [L, AW])
    lnx_g = din("lnx_g", [L, AW])
    lnx_b = din("lnx_b", [L, AW])
    LV = max(L - 1, 1)
    vres_bias = din("vres_bias", [LV, AW])
    vres_down = din("vres_down", [LV, AW, 32])
    vres_up = din("vres_up", [LV, 32, AW])
    gla_f_up = din("gla_f_up", [L, 16, 256])
    gla_f_bias = din("gla_f_bias", [L, 256])
    gla_norm_g = din("gla_norm_g", [L, 128])
    w_out = din("w_out", [L, D, D])
    ln1_g = din("ln1_g", [L, D])
    ln1_b = din("ln1_b", [L, D])
    w_up = din("w_up", [L, D, 2 * DFF])
    conv_w = din("conv_w", [L, 3, DFF])
    conv_b = din("conv_b", [L, DFF])
    w_down = din("w_down", [L, DFF, D])
    ln2_g = din("ln2_g", [L, D])
    ln2_b = din("ln2_b", [L, D])

    y_p = dout("y_p", [SEQ, D])
    y_s = dout("y_s", [NS4, D])
    o_rwkv_p = dout("o_rwkv_p", [L, 8, 64, 64])
    o_rwkv_s = dout("o_rwkv_s", [L, NSQ, 8, 64, 64])
    o_shift_p = dout("o_shift_p", [L, SHIFT_W])
    o_shift_s = dout("o_shift_s", [L, NSQ, SHIFT_W])
    o_gla_p = dout("o_gla_p", [L, 4, 64, 128])
    o_gla_s = dout("o_gla_s", [L, NSQ, 4, 64, 128])
    o_conv_p = dout("o_conv_p", [L, 2, DFF])
    o_conv_s = dout("o_conv_s", [L, NSQ, 2, DFF])
    dbg = {}

    st = ExitStack()
    st.enter_context(nc.allow_non_contiguous_dma(reason="small strided parameter/state transfers"))
    st.enter_context(nc.allow_low_precision(reason="bf16 matmul operands, fp32 accumulation"))

    def sb(name, shape, dt, nkeys=None):
        t = st.enter_context(nc.sbuf_tensor(name, list(shape), dt))
        return Tile(t.ap(), name)

    def sbk(name, shape, dt, n):
        t = st.enter_context(nc.sbuf_tensor(name, list(shape), dt))
        ap = t.ap()
        return [Tile(ap[:, i], "%s.%d" % (name, i)) for i in range(n)]

    xres = sbk("xres", [128, NTILE_MAX, D], F32, NTILE_MAX)
    xT = sb("xT", [128, KC, NTMAX], BF16)
    vfirst = sbk("vfirst", [128, 4, NTMAX], F32, 4)
    mixT = sbk("mixT", [128, KC, NTMAX], BF16, KC)
    Hm = sbk("Hm", [128, L, 4, 128], F32, L)
    Sg = sbk("Sg", [128, L, 2, 128], F32, L)
    shp = sbk("shp", [128, L, 14], F32, L)
    cst = sbk("cst", [128, L, FC, 2], F32, L)
    mu_t = sb("mu_t", [128, L, 14], F32)
    pv = {}
    for nm, src in (("w0", w0), ("a0", a0), ("kk", k_k), ("ka", k_a), ("rk", r_k), ("lg", lnx_g), ("lb", lnx_b)):
        pv[nm] = sb("pv_" + nm, [128, L, 4], F32)
        P.dma("sp", pv[nm], src.rearrange("l (c p) -> p l c", p=128))
    pv["v0"] = sb("pv_v0", [128, LV, 4], F32)
    P.dma("sp", pv["v0"], vres_bias.rearrange("l (c p) -> p l c", p=128))
    omka = sb("omka", [128, L, 4], F32)
    P.ts("dve", omka, pv["ka"], -1.0, ALU.mult, 1.0, ALU.add)
    P.dma("sp", mu_t, tok_mu.rearrange("l (c p) -> p l c", p=128))
    omu = sb("omu", [128, L, 14], F32)
    P.ts("dve", omu, mu_t, -1.0, ALU.mult, 1.0, ALU.add)
    fb_t = sb("fb_t", [128, L, 2], F32)
    P.dma("sp", fb_t, gla_f_bias.rearrange("l (c p) -> p l c", p=128))
    ng_t = sb("ng_t", [128, L], F32)
    P.dma("sp", ng_t, gla_norm_g.rearrange("l p -> p l"))
    cw_t = sb("cw_t", [128, L, 3, FC], F32)
    P.dma("sp", cw_t, conv_w.rearrange("l j (c p) -> p l j c", p=128))
    cb_t = sb("cb_t", [128, L, FC], F32)
    P.dma("sp", cb_t, conv_b.rearrange("l (c p) -> p l c", p=128))
    wa_up = sb("wa_up", [128, L, AW], BF16)
    P.dma("pool", wa_up[0:64], w_lora_up.rearrange("l k c -> k l c"))
    P.dma("pool", wa_up[64:128], a_lora_up.rearrange("l k c -> k l c"))
    g_up = sb("g_up", [128, L, AW], BF16)
    P.dma("pool", g_up, g_lora_up.rearrange("l k c -> k l c"))
    vdn = sb("vdn", [128, LV, 4, 32], BF16)
    P.dma("pool", vdn, vres_down.rearrange("l (c p) r -> p l c r", p=128))
    vup = sb("vup", [32, LV, AW], BF16)
    P.dma("pool", vup, vres_up.rearrange("l r c -> r l c"))
    fup = sb("fup", [16, L, 256], BF16)
    P.dma("pool", fup, gla_f_up.rearrange("l r c -> r l c"))

    identb = sb("identb", [128, 128], BF16)
    identf = sb("identf", [128, 128], F32)
    for t in (identb, identf):
        P.memset("dve", t, 0.0)
        tap = t.ap
        P.add("pool", (lambda a: (lambda e: e.affine_select(a, a, [[-1, 128]], ALU.not_equal, 1.0, base=0,
                                                            channel_multiplier=1)))(tap), list(t.keys), list(t.keys))
    imask = sb("imask", [64, 64], F32)
    smask = sb("smask", [64, 64], F32)
    smaskT = sb("smaskT", [64, 64], F32)
    for t, pat, cm, base in ((imask, 1, -1, 0), (smask, 1, -1, -1), (smaskT, -1, 1, -1)):
        P.memset("dve", t, 1.0)
        tap = t.ap
        P.add("pool", (lambda a, pat, cm, base: (lambda e: e.affine_select(
            a, a, [[pat, 64]], ALU.is_ge, 0.0, base=base, channel_multiplier=cm)))(tap, pat, cm, base),
            list(t.keys), list(t.keys))
    bdones = sb("bdones", [128, 128], F32)
    P.memset("dve", bdones, 0.0)
    P.memset("dve", bdones[0:64, 0:64], 1.0)
    P.memset("dve", bdones[64:128, 64:128], 1.0)
    allones = sb("allones", [128, 128], F32)
    P.memset("dve", allones, 1.0)
    rmask_p = sb("rmask_p", [128, TB], BF16)
    P.memset("dve", rmask_p, 1.0)
    P.memset("dve", rmask_p.re("p (n c) -> p n c", c=64)[:, :, 0:1], 0.0)
    rmask_s = sb("rmask_s", [128, NS4], BF16)
    P.memset("dve", rmask_s, 1.0)
    P.memset("dve", rmask_s.re("p (n c) -> p n c", c=4)[:, :, 0:1], 0.0)
    for l in range(L):
        P.memset("dve", Hm[l], 0.0)
        P.memset("dve", Sg[l], 0.0)
        P.memset("dve", shp[l], 0.0)
        P.memset("dve", cst[l], 0.0)

    NSLOT = 3
    wslot = [sb("wslot%d" % i, [128, KC, 512], BF16) for i in range(NSLOT)]
    wctr = [0]

    NGRP = 28 * L
    wscr = nc.dram_tensor("wscr", [NGRP, 128, KC * 512], BF16, kind="Internal").ap()
    gctr = [0]
    first_blk = [True]

    def wload(src_ap, nk, ncol):
        s = wslot[wctr[0] % NSLOT]
        wctr[0] += 1
        gid = gctr[0]
        gctr[0] += 1
        assert gid < NGRP
        dst = s[:, 0:nk, 0:ncol]
        scr = Tile(wscr[gid], "wscr.%d" % gid)
        flat = s.re("p k c -> p (k c)")
        if first_blk[0]:
            P.dma("pool", dst, src_ap)
            P.dma("sp", scr, flat)
        else:
            P.dma("pool", flat, scr)
        return dst

    pst = st.enter_context(nc.psum_tensor("psum", [128, 8, 512], F32))
    psap = pst.ap()
    PSB = [Tile(psap[:, i], "ps%d" % i) for i in range(8)]

    def ps_multi(b0, nb):
        return Tile(psap[:, b0:b0 + nb].rearrange("p b c -> p (b c)"), tuple("ps%d" % i for i in range(b0, b0 + nb)))

    def ps_bf(bank):
        return Tile(psap[:, bank].bitcast(BF16), "ps%d" % bank)

    ARENA_F32 = 0
    mixer_defs = []
    ffn_defs = []

    NT_ = NTMAX
    PA_W = max(TB + 1, NSQ * (DEC_SEQ + 1))
    GT_W = max(TB + 2, NSQ * (DEC_SEQ + 2))
    m_off = [0]

    def m_alloc(words):
        o = m_off[0]
        m_off[0] += (words + 7) // 8 * 8
        return o

    f_off = [0]

    def f_alloc(words):
        o = f_off[0]
        f_off[0] += (words + 7) // 8 * 8
        return o

    layout_m = {}
    layout_m["pa"] = m_alloc(14 * PA_W)
    NTMP = 9
    for i in range(NTMP):
        layout_m["t%d" % i] = m_alloc(NT_)
    layout_m["lor"] = m_alloc(NT_ // 2)
    layout_m["sgl"] = m_alloc(NT_ // 2)
    layout_m["vb"] = m_alloc(4 * NT_ // 2)
    layout_m["vdb"] = m_alloc(NT_ // 2)
    layout_m["osb"] = m_alloc(4 * NT_)
    layout_m["bon"] = m_alloc(4 * NT_ // 2)
    layout_m["gst"] = m_alloc(4 * NT_ // 2)
    layout_m["AK"] = m_alloc(4 * NT_ * 2 // 2)
    layout_m["BR"] = m_alloc(4 * NT_ * 2 // 2)
    layout_m["PC"] = m_alloc(4 * NCHMAX)
    for nm, w in (("XA", 8 * 128 // 2), ("XB", 8 * 128 // 2), ("YA", 8 * 64 // 2), ("YB", 8 * 64 // 2),
                  ("MkT", 8 * 64 // 2), ("QT", 2 * 8 * 64 // 2), ("Wsb", 8 * 128 // 2), ("Upad", 8 * 128 // 2),
                  ("Upair", 4 * 128 // 2), ("Vpad", 8 * 128 // 2), ("Vpair", 4 * 128 // 2), ("AKt", 2 * 4 * 128 // 2),
                  ("Hbd", 4 * 128 // 2), ("sttmp", 8 * 64), ("Sgb", 2 * 128 // 2), ("ATsb", 4 * 64 // 2),
                  ("Vt", 4 * 128 // 2), ("kdt", 2 * 128 // 2), ("stout", 4 * 64)):
        layout_m[nm] = m_alloc(w)
    layout_f = {}
    layout_f["zbuf"] = f_alloc(NTILE_MAX * D)
    layout_f["lng"] = f_alloc(D)
    layout_f["lnb"] = f_alloc(D)
    layout_f["hT"] = f_alloc(FC * NT_ // 2)
    layout_f["gt0"] = f_alloc(GT_W)
    layout_f["gt1"] = f_alloc(GT_W)
    layout_f["acc0"] = f_alloc(NT_)
    layout_f["acc1"] = f_alloc(NT_)
    layout_f["xb"] = f_alloc(D // 2)
    layout_f["xb1"] = f_alloc(D // 2)
    layout_f["stat"] = f_alloc(32 * NTILE_MAX)
    layout_f["cpast"] = f_alloc(FC * NSQ * 2)
    layout_f["cout"] = f_alloc(FC * NSQ * 2)
    layout_f["ctok"] = f_alloc(DFF)
    ARENA_F32 = max(m_off[0], f_off[0])
    arena_t = st.enter_context(nc.sbuf_tensor("arena", [128, ARENA_F32], F32))
    arena = arena_t.ap()
    MKEYS = tuple("m." + k for k in layout_m)

    def mtile(name, words, dt, shape_str=None, **kw):
        o = layout_m[name]
        ap = arena[:, o:o + words]
        if dt == BF16:
            ap = ap.bitcast(BF16)
        if shape_str:
            ap = ap.rearrange(shape_str, **kw)
        return Tile(ap, "m." + name)

    def ftile(name, words, dt, shape_str=None, **kw):
        o = layout_f[name]
        ap = arena[:, o:o + words]
        if dt == BF16:
            ap = ap.bitcast(BF16)
        if shape_str:
            ap = ap.rearrange(shape_str, **kw)
        return Tile(ap, "f." + name)

    def dump(name, tile, shape):
        if not debug:
            return
        d = dout("dbg_" + name, shape)
        dbg[name] = shape
        P.dma("sp", d, tile)

    def load_x_block(kind, b, NT):
        ntile = (NT + 127) // 128
        for t in range(ntile):
            rows = min(128, NT - t * 128)
            src = x_p[b * TB + t * 128: b * TB + t * 128 + rows, :] if kind == "p" else x_s[t * 128: t * 128 + rows, :]
            P.dma("sp", xres[t][0:rows], src)

    def make_xT(NT, which_ps=(6, 7)):
        ntile = (NT + 127) // 128
        for t in range(ntile):
            rows = min(128, NT - t * 128)
            xb = ftile("xb" if t % 2 == 0 else "xb1", D // 2, BF16)
            P.copy("act", xb[0:rows], xres[t][0:rows])
            pb = ps_bf(which_ps[t % 2])
            for k in range(KC):
                P.transpose(pb[:, k * 128:k * 128 + rows], xb[0:rows, k * 128:(k + 1) * 128], identb[0:rows, 0:rows])
            P.copy("dve", xT[:, :, t * 128:t * 128 + rows],
                   pb[:, 0:KC * 128].re("p (k c) -> p k c", c=128)[:, :, 0:rows])

    def layer_norm_tile(t, rows, zt, g_dram, b_dram, lng, lnb):
        st_all = ftile("stat", 32 * NTILE_MAX, F32)
        stat = Tile(st_all.ap[:, 32 * t:32 * (t + 1)], "f.stat.%d" % t)
        s6 = stat[:, 0:12].re("p (a b) -> p a b", b=6)
        for hf in range(2):
            za, sa = zt[0:rows, hf * 512:(hf + 1) * 512].ap, s6[0:rows, hf].ap
            P.add("dve", (lambda o, i: (lambda e: e.bn_stats(o, i)))(sa, za), _keys([zt]), _keys([stat]))
        yield
        mv = stat[:, 12:14]
        P.add("dve", (lambda o, i: (lambda e: e.bn_aggr(o, i)))(mv[0:rows].ap, stat[0:rows, 0:12].ap),
              _keys([stat]), _keys([stat]))
        yield
        rs = stat[:, 14:15]
        P.act(rs[0:rows], mv[0:rows, 1:2], AF.Ln, bias=LN_EPS, scale=1.0)
        yield
        P.act(rs[0:rows], rs[0:rows], AF.Exp, scale=-0.5)
        yield
        nb = stat[:, 15:16]
        P.ts("dve", nb[0:rows], mv[0:rows, 0:1], rs[0:rows], ALU.mult, -1.0, ALU.mult)
        yield
        P.act(zt[0:rows], zt[0:rows], AF.Identity, bias=nb[0:rows], scale=rs[0:rows])
        yield
        P.tt("dve", zt[0:rows], zt[0:rows], lng[0:rows], ALU.mult)
        yield
        P.tt("dve", xres[t][0:rows], zt[0:rows], lnb[0:rows], ALU.add)
        yield

    def proj_token_major(NT, lhs_chunks, nk, w_dram_kview, g_dram, b_dram):
        ntile = (NT + 127) // 128
        zb_all = ftile("zbuf", NTILE_MAX * D, F32, "p (t d) -> p t d", d=D)
        zb = [Tile(zb_all.ap[:, t], "f.zbuf.%d" % t) for t in range(NTILE_MAX)]
        lng = ftile("lng", D, F32)
        lnb = ftile("lnb", D, F32)
        P.dma("sp", lng, g_dram.partition_broadcast(128))
        P.dma("sp", lnb, b_dram.partition_broadcast(128))
        for hf in range(2):
            k0 = 0
            while k0 < nk:
                nk_ = min(KC, nk - k0)
                ws = wload(w_dram_kview(k0, nk_, hf), nk_, 512)
                for kk in range(nk_):
                    k = k0 + kk
                    for t in range(ntile):
                        rows = min(128, NT - t * 128)
                        P.mm(PSB[t][0:rows], lhs_chunks(k)[:, t * 128:t * 128 + rows], ws[:, kk, :],
                             start=(k == 0), stop=(k == nk - 1))
                k0 += nk_
            for t in range(ntile):
                rows = min(128, NT - t * 128)
                P.stt(zb[t][0:rows, hf * 512:(hf + 1) * 512], xres[t][0:rows, hf * 512:(hf + 1) * 512], ALPHA,
                      PSB[t][0:rows], ALU.mult, ALU.add)
        _round_robin([layer_norm_tile(t, min(128, NT - t * 128), zb[t], g_dram, b_dram, lng, lnb)
                      for t in range(ntile)], lag=1)

    pctr = [0]

    def proj_feature_major(NT, w_dram, col0, ncols, consume):
        wv = w_dram.rearrange("(k p) c -> p k c", p=128)
        c = 0
        j = 0
        while c < ncols:
            wcols = min(512, ncols - c)
            ws = wload(wv[:, :, col0 + c: col0 + c + wcols], KC, wcols)
            cc = 0
            while cc < wcols:
                m = min(128, wcols - cc)
                pt = PSB[pctr[0] % 4]
                pctr[0] += 1
                for k in range(KC):
                    P.mm(pt[0:m, 0:NT], ws[:, k, cc:cc + m], xT[:, k, 0:NT], start=(k == 0), stop=(k == KC - 1))
                consume(j, pt, m)
                j += 1
                cc += m
            c += wcols

    blocks = [("p", b) for b in range(SEQ // TB)] + [("s", 0)]
    try:
      stage(1)
      for (kind, b) in blocks:
          is_p = kind == "p"
          NT = TB if is_p else NS4
          C = 64 if is_p else DEC_SEQ
          nch = NT // C
          nseq = 1 if is_p else NSQ
          Tq = TB if is_p else DEC_SEQ
          nlev = int(round(math.log2(C))) - 1
          last_p = is_p and b == SEQ // TB - 1
          rmask = rmask_p if is_p else rmask_s

          gctr[0] = 0
          first_blk[0] = (kind, b) == blocks[0]
          load_x_block(kind, b, NT)
          stage(2)
          make_xT(NT)
          stage(3)
          P.barrier()

          for l in range(L):
              pa_ap = mtile("pa", 14 * PA_W, F32, "p (j w) -> p j w", w=PA_W).ap
              PAKEYS = tuple("m.pa.%d" % j for j in range(14))

              class _PA:
                  def __getitem__(self, idx):
                      j = idx[1]
                      if isinstance(j, int):
                          return Tile(pa_ap[idx], PAKEYS[j])
                      return Tile(pa_ap[idx], PAKEYS)
              pa = _PA()

              def pa_cur(j):
                  return pa[:, j, 0:nseq * (Tq + 1)].re("p (s t) -> p s t", t=Tq + 1)[:, :, 1:]

              def pa_prev(j):
                  return pa[:, j, 0:nseq * (Tq + 1)].re("p (s t) -> p s t", t=Tq + 1)[:, :, 0:Tq]

              def pa_cv(j):
                  if is_p:
                      return pa[:, j, 1:1 + NT].re("p (n c) -> p n c", c=C)
                  return pa_cur(j)

              def pa_flat(j):
                  return pa[:, j, 0:NT]

              tmp = [mtile("t%d" % i, NT_, F32)[:, 0:NT] for i in range(NTMP)]

              def cv(t):
                  return t.re("p (n c) -> p n c", c=C)

              def sv(t):
                  return t.re("p (s t) -> p s t", t=Tq)

              lor = mtile("lor", NT_ // 2, BF16)[:, 0:NT]
              sgl = mtile("sgl", NT_ // 2, BF16)[:, 0:NT]
              vb = mtile("vb", 4 * NT_ // 2, BF16, "p (c n) -> p c n", c=4)[:, :, 0:NT]
              vdb = mtile("vdb", NT_ // 2, BF16)[:, 0:NT]
              osb = mtile("osb", 4 * NT_, F32, "p (c n) -> p c n", c=4)[:, :, 0:NT]
              bon = mtile("bon", 4 * NT_ // 2, BF16, "p (c n) -> p c n", c=4)[:, :, 0:NT]
              gst = mtile("gst", 4 * NT_ // 2, BF16, "p (c n) -> p c n", c=4)[:, :, 0:NT]
              AK = mtile("AK", 4 * NT_, BF16, "p (c n) -> p c n", c=4)[:, :, 0:2 * NT].re("p c (n k s) -> p c n k s", k=2, s=C)
              BR = mtile("BR", 4 * NT_, BF16, "p (c n) -> p c n", c=4)[:, :, 0:2 * NT].re("p c (n k s) -> p c n k s", k=2, s=C)
              PC = mtile("PC", 4 * NCHMAX, F32, "p (c n) -> p c n", c=4)[:, :, 0:nch]

              def consume_pa(j, pt, m):
                  P.copy("act", pa_cur(j), pt[:, 0:NT].re("p (s t) -> p s t", t=Tq))
              proj_feature_major(NT, w_in[l], 0, SHIFT_W, consume_pa)
              stage(4)
              if is_p:
                  P.copy("dve", pa[:, :, 0], shp[l])
                  P.copy("dve", shp[l], pa[:, :, Tq])
                  if last_p:
                      P.dma("sp", o_shift_p[l].rearrange("(j p) -> p j", p=128), shp[l])
              else:
                  shs = mtile("sttmp", 8 * 64, F32)[:, 0:14 * NSQ].re("p (j s) -> p j s", s=NSQ)
                  fast_sh = NT_ >= 448 and NSQ <= 32
                  if fast_sh:
                      o2 = layout_m["t2"]
                      stok = Tile(arena[:, o2:o2 + SHIFT_W], ("m.t2", "m.t3", "m.t4", "m.t5"))
                      P.dma("sp", stok[0:NSQ], st_shift[l])
                      pin = PSB[6]
                      for j in range(14):
                          P.mm(pin[:, j * NSQ:(j + 1) * NSQ], stok[0:NSQ, j * 128:(j + 1) * 128], identf[0:NSQ, 0:NSQ])
                      P.copy("act", pa[:, :, 0:nseq * (Tq + 1)].re("p j (s t) -> p j s t", t=Tq + 1)[:, :, :, 0],
                             pin[:, 0:14 * NSQ].re("p (j s) -> p j s", s=NSQ))
                  else:
                      for j in range(14):
                          P.dma("sp", pa[:, j, 0:nseq * (Tq + 1)].re("p (s t) -> p s t", t=Tq + 1)[:, :, 0],
                                st_shift[l, :, j * 128:(j + 1) * 128].rearrange("s p -> p s"))
                  for j in range(14):
                      P.copy("dve", shs[:, j], pa[:, j, 0:nseq * (Tq + 1)].re("p (s t) -> p s t", t=Tq + 1)[:, :, Tq])
                  if fast_sh:
                      for j in range(14):
                          bk, cc = divmod(j, 4)
                          P.mm(PSB[4 + bk][0:NSQ, cc * 128:(cc + 1) * 128], shs[:, j, :], identf)
                      for bk in range(4):
                          w_ = min(512, SHIFT_W - bk * 512)
                          P.copy("act", stok[0:NSQ, bk * 512:bk * 512 + w_], PSB[4 + bk][0:NSQ, 0:w_])
                      P.dma("sp", o_shift_s[l], stok[0:NSQ])
                  else:
                      for j in range(14):
                          P.dma("sp", o_shift_s[l, :, j * 128:(j + 1) * 128].rearrange("s p -> p s"), shs[:, j])
              stage(5)
              for j in range(14):
                  tw_ = sv(tmp[j % 2])
                  P.act(tw_, pa_prev(j), AF.Copy, scale=mu_t[:, l, j:j + 1])
                  P.stt(pa_cur(j), pa_cur(j), omu[:, l, j:j + 1], tw_, ALU.mult, ALU.add)
              stage(6)
              P.act(cv(lor)[0:64], pa_cv(12)[0:64], AF.Tanh)
              P.copy("act", cv(lor)[64:128], pa_cv(12)[64:128])
              P.act(cv(sgl), pa_cv(13), AF.Sigmoid)
              if l == 0:
                  for c in range(4):
                      P.copy("act", cv(vfirst[c][:, 0:NT]), pa_cv(8 + c))
              else:
                  for c in range(4):
                      P.copy("act", cv(vb[:, c]), pa_cv(8 + c))
                  pt = PSB[4]
                  for c in range(4):
                      P.mm(pt[0:32, 0:NT], vdn[:, l - 1, c, :], vb[:, c], start=(c == 0), stop=(c == 3))
                  P.copy("act", vdb[0:32], pt[0:32, 0:NT])

              stage(7)
              T_s1, T_cs, T_eP, T_eM, T_a, T_kk, T_km, T_1, T_2 = tmp
              T_csx = T_eX = T_s1
              T_kn = T_kk
              two_chain = NT_ >= 512

              def host(names):
                  o_ = layout_m[names[0]]
                  return Tile(arena[:, o_:o_ + NT_], tuple("m." + n_ for n_ in names))[:, 0:NT]
              if two_chain:
                  tmpB = [host(["osb"]), host(["XA"]), host(["XB"]), host(["Wsb"]), host(["QT"]), host(["AKt"]),
                          host(["Upad"]), host(["Vpad"]), host(["YA", "YB"])]
              else:
                  tmpB = tmp

              def pair_gen(c, TT, BK):
                  t_s1, t_cs, t_eP, t_eM, t_a, t_kk, t_km, t_1, t_2 = TT
                  t_csx = t_eX = t_s1
                  t_kn = t_kk
                  cs_ = slice(c * 128, (c + 1) * 128)
                  pw, pa_, pg, pvv = PSB[BK], PSB[BK + 1], PSB[BK + 2], PSB[BK + 3]
                  P.mm(pw[:, 0:NT], wa_up[0:64, l, cs_], lor[0:64])
                  P.mm(pa_[:, 0:NT], wa_up[64:128, l, cs_], lor[64:128])
                  P.mm(pg[:, 0:NT], g_up[:, l, cs_], sgl)
                  if l > 0:
                      P.mm(pvv[:, 0:NT], vup[0:32, l - 1, cs_], vdb[0:32])
                  yield
                  P.act(t_s1, pw[:, 0:NT], AF.Sigmoid, bias=pv["w0"][:, l, c:c + 1], scale=1.0)
                  P.act(t_a, pa_[:, 0:NT], AF.Sigmoid, bias=pv["a0"][:, l, c:c + 1], scale=1.0)
                  if l > 0:
                      P.act(t_1, pvv[:, 0:NT], AF.Sigmoid, bias=pv["v0"][:, l - 1, c:c + 1], scale=1.0)
                  P.copy("act", gst[:, c], pg[:, 0:NT])
                  yield
                  P.scan(t_cs, rmask[:, 0:NT], t_s1)
                  yield
                  P.tt("dve", t_csx, t_cs, t_s1, ALU.subtract)
                  P.act(t_eP, t_cs, AF.Exp, scale=-KDEC)
                  P.act(t_eM, t_cs, AF.Exp, scale=KDEC)
                  yield
                  P.act(t_eX, t_csx, AF.Exp, scale=-KDEC)
                  P.copy("act", PC[:, c], cv(t_eP)[:, :, C - 1])
                  xr, xk, xv = pa_cv(c), pa_cv(4 + c), pa_cv(8 + c)
                  if l > 0:
                      P.tt("dve", cv(t_2), cv(vfirst[c][:, 0:NT]), xv, ALU.subtract)
                      yield
                      P.tt("dve", t_2, t_2, t_1, ALU.mult)
                      yield
                      P.tt("dve", xv, xv, cv(t_2), ALU.add)
                      yield
                  P.copy("act", cv(vb[:, c]), xv)
                  P.act(cv(t_kk), xk, AF.Copy, scale=pv["kk"][:, l, c:c + 1])
                  yield
                  P.act(t_1, t_kk, AF.Square)
                  P.mm(pw[:, 0:NT], bdones, t_1)
                  yield
                  P.ts("dve", t_1, pw[:, 0:NT], 1e-24, ALU.max)
                  P.act(t_1, t_1, AF.Sqrt)
                  yield
                  P.recip(t_1, t_1)
                  yield
                  P.tt("dve", t_kn, t_kk, t_1, ALU.mult)
                  yield
                  P.act(t_2, t_a, AF.Identity, bias=omka[:, l, c:c + 1], scale=pv["ka"][:, l, c:c + 1])
                  yield
                  P.tt("dve", cv(t_km), xk, cv(t_2), ALU.mult)
                  yield
                  P.tt("dve", t_1, t_a, t_kn, ALU.mult)
                  yield
                  P.tt("dve", AK[:, c, :, 0, :], cv(t_1), cv(t_eM), ALU.mult)
                  yield
                  P.tt("dve", AK[:, c, :, 1, :], cv(t_km), cv(t_eM), ALU.mult)
                  yield
                  P.stt(BR[:, c, :, 0, :], cv(t_kn), -1.0, cv(t_eX), ALU.mult, ALU.mult)
                  yield
                  P.tt("dve", BR[:, c, :, 1, :], xr, cv(t_eP), ALU.mult)
                  yield
                  P.tt("dve", cv(t_2), xr, cv(t_km), ALU.mult)
                  yield
                  P.act(t_2, t_2, AF.Copy, scale=pv["rk"][:, l, c:c + 1])
                  P.mm(pa_[:, 0:NT], bdones, t_2)
                  yield
                  P.tt("dve", cv(bon[:, c]), cv(pa_[:, 0:NT]), xv, ALU.mult)
                  yield

              for c0 in (0, 2):
                  gens = [pair_gen(c0, tmp, 0), pair_gen(c0 + 1, tmpB, 4)]
                  live = [True, True]
                  started = 0
                  step = 0
                  while any(live):
                      for gi in range(2):
                          if gi == 1 and step < 3:
                              continue
                          if live[gi]:
                              try:
                                  next(gens[gi])
                              except StopIteration:
                                  live[gi] = False
                      step += 1

              stage(8)
              XAB = [mtile(n, 8 * 128 // 2, BF16, "p (h k s) -> p h k s", h=8, k=2) for n in ("XA", "XB")]
              YAB = [mtile(n, 8 * 64 // 2, BF16, "p (h s) -> p h s", h=8) for n in ("YA", "YB")]
              MkT = mtile("MkT", 8 * 64 // 2, BF16, "p (h s) -> p h s", h=8)
              QT = mtile("QT", 2 * 8 * 64 // 2, BF16, "p (k h s) -> p k h s", k=2, h=8)
              Wsb = mtile("Wsb", 8 * 128 // 2, BF16, "p (h s) -> p h s", h=8)
              Upad = mtile("Upad", 8 * 128 // 2, BF16, "p (h s) -> p h s", h=8)
              Upair = mtile("Upair", 4 * 128 // 2, BF16, "p (h s) -> p h s", h=4)
              Vpad = mtile("Vpad", 8 * 128 // 2, BF16, "p (h s) -> p h s", h=8)
              Vpair = mtile("Vpair", 4 * 128 // 2, BF16, "p (h s) -> p h s", h=4)
              AKt = mtile("AKt", 2 * 4 * 128 // 2, BF16, "p (k c s) -> p k c s", k=2, c=4)
              Hbd = mtile("Hbd", 4 * 128 // 2, BF16, "p (c s) -> p c s", c=4)
              P.memset("dve", Vpad, 0.0)
              P.memset("dve", Upad, 0.0)
              if is_p:
                  Hcur = Hm[l]
              else:
                  Hs = mtile("sttmp", 8 * 64, F32, "p (c s) -> p c s", c=4)
                  Hcur = Hs
                  P.memset("dve", Hs, 0.0)
              stout = mtile("stout", 4 * 64, F32, "p (c s) -> p c s", c=4)

              def padview(t8):
                  a = t8.ap
                  parts = a.ap
                  return Tile(bass.AP(tensor=a.tensor, offset=a.offset, ap=[list(parts[0]), [256, 4], [192, 2], [1, 64]]),
                              t8.keys)

              dbl = (not is_p) and NT_ >= 320 and NT <= 64
              if dbl:
                  def _tail(i_, key):
                      o_ = layout_m["t%d" % i_] + 64
                      return Tile(arena[:, o_:o_ + 256].rearrange("p (c s) -> p c s", c=4), key)
                  sin_sets = [[_tail(6, "m.sinA0"), _tail(7, "m.sinA1")], [_tail(8, "m.sinB0"), _tail(0, "m.sinB1")]]
                  stouts = [stout, _tail(1, "m.stoutB")]

                  def load_sin(n_):
                      for hf in range(2):
                          P.dma("sp", sin_sets[n_ % 2][hf][0:64], st_rwkv[l, n_, 4 * hf:4 * hf + 4].rearrange("h v k -> v h k"))
                  load_sin(0)

              def store_state(Hc, dst_h, stout=stout):
                  pt = PSB[7]
                  for c in range(4):
                      P.transpose(pt[:, c * 128:(c + 1) * 128], Hc[:, c, :], identf)
                  ptv = pt.re("p (c s) -> p c s", c=4)
                  for p_ in range(2):
                      ph = slice(p_ * 64, p_ * 64 + 64)
                      P.copy("dve", stout[ph], ptv[ph, :, p_ * 64:p_ * 64 + 64])
                      P.dma("sp", dst_h.rearrange("(c p) v k -> p v c k", p=2)[p_], stout[ph])

              pipe = PA_W >= 512
              X0, X1 = XAB
              Y0, Y1 = YAB
              smk = smask[0:C, 0:C]
              imk = imask[0:C, 0:C]
              smt = smaskT[0:C, 0:C]
              idn = identb[0:C, 0:C]
              bc4 = lambda m_: Tile(m_.ap.unsqueeze(1).broadcast_to([C, 4, C]), m_.keys)

              def parow(j_, words, pattern, **kw):
                  return Tile(pa_ap[:, j_, 0:words].bitcast(BF16).rearrange(pattern, **kw), PAKEYS[j_])
              if pipe:
                  sets = [dict(Vpair=Vpair, Vpad=Vpad, AKt=AKt, MkT=MkT, QT=QT,
                               Yfin=parow(4, 256, "p (h s) -> p h s", h=8)),
                          dict(Vpair=parow(3, 256, "p (h s) -> p h s", h=4),
                               Vpad=parow(0, 512, "p (h s) -> p h s", h=8),
                               AKt=parow(1, 512, "p (k c s) -> p k c s", k=2, c=4),
                               MkT=parow(5, 256, "p (h s) -> p h s", h=8),
                               QT=parow(2, 512, "p (k h s) -> p k h s", k=2, h=8),
                               Yfin=parow(6, 256, "p (h s) -> p h s", h=8))]
                  P.memset("dve", sets[1]["Vpad"], 0.0)
              else:
                  sets = [dict(Vpair=Vpair, Vpad=Vpad, AKt=AKt, MkT=MkT, QT=QT, Yfin=None)] * 2

              def P1_stages(n, S):
                  Vpair_, Vpad_, AKt_, MkT_, QT_ = S["Vpair"], S["Vpad"], S["AKt"], S["MkT"], S["QT"]

                  def stA():
                      pvt = PSB[3]
                      for c in range(4):
                          P.mm(pvt[0:C, c * 128:(c + 1) * 128], vb[:, c, n * C:(n + 1) * C], identb)
                      pvv4 = pvt[0:C, 0:512].re("p (c s) -> p c s", c=4)
                      with P.indep():
                          P.copy("act", Vpair_[0:C], pvv4)
                          for q_ in range(2):
                              P.copy("act", Vpad_[0:C].re("p (c q) s -> p c q s", q=2)[:, :, q_, q_ * 64:(q_ + 1) * 64],
                                     pvv4[:, :, q_ * 64:(q_ + 1) * 64])
                      for k_ in range(2):
                          for c in range(4):
                              P.mm(PSB[4 + k_][0:C, c * 128:(c + 1) * 128], AK[:, c, n, k_, :], identb)
                      for k_ in range(2):
                          P.copy("act" if k_ == 0 else "dve", AKt_[0:C, k_],
                                 PSB[4 + k_][0:C, 0:512].re("p (c s) -> p c s", c=4))

                  def stB():
                      Gq = [[PSB[2 * q_ + k_][0:C, 0:4 * 2 * C].re("p (c s) -> p c s", c=4) for k_ in range(2)]
                            for q_ in range(2)]
                      PXq = [PSB[4 + q_][0:C, 0:4 * C].re("p (c s) -> p c s", c=4) for q_ in range(2)]
                      for h in range(8):
                          c, p_ = h // 2, h % 2
                          ph = slice(p_ * 64, p_ * 64 + 64)
                          for k_ in range(2):
                              P.mm(Gq[p_][k_][:, c, :], AK[ph, c, n, k_, :], BR[ph, c, n, :, :].re("p k s -> p (k s)"))
                          P.mm(PXq[p_][:, c, :], BR[ph, c, n, 0, :], AK[ph, c, n, 0, :])
                      X0q = X0[0:C].re("p (c q) k s -> p c q k s", q=2)
                      MkTq = MkT_[0:C].re("p (c q) s -> p c q s", q=2)
                      with P.indep():
                          for q_ in range(2):
                              P.tt("dve", X0q[:, :, q_, 1, 0:C], Gq[q_][0][:, :, 0:C], bc4(smk), ALU.mult)
                              P.tt("dve", MkTq[:, :, q_, 0:C], Gq[q_][1][:, :, 0:C], bc4(smk), ALU.mult)
                              for k_ in range(2):
                                  P.tt("dve", QT_[0:C, k_].re("p (c q) s -> p c q s", q=2)[:, :, q_, 0:C],
                                       Gq[q_][k_][:, :, C:2 * C], bc4(imk), ALU.mult)
                              P.tt("dve", X0q[:, :, q_, 0, 0:C], PXq[q_], bc4(smt), ALU.mult)
                      P.tt("dve", Y0[0:C, :, 0:C], X0[0:C, :, 1, 0:C],
                           Tile(idn.ap.unsqueeze(1).broadcast_to([C, 8, C]), idn.keys), ALU.add)

                  def mk_level(lev):
                      def stC():
                          lastlev = lev == nlev - 1
                          Xc, Xn = (X0, X1) if lev % 2 == 0 else (X1, X0)
                          Yc, Yn = (Y0, Y1) if lev % 2 == 0 else (Y1, Y0)
                          if lastlev and S["Yfin"] is not None:
                              Yn = S["Yfin"]
                          PL = ps_multi(0, 2)[0:C, 0:8 * 2 * C].re("p (h k s) -> p h k s", h=8, k=2)
                          for h in range(8):
                              P.mm(PL[:, h, 0, :], Xc[0:C, h, 1, 0:C], Xc[0:C, h, 0, 0:C])
                              if not lastlev:
                                  P.mm(PL[:, h, 1, :], Xc[0:C, h, 0, 0:C], Xc[0:C, h, 1, 0:C])
                          for hh in range(2):
                              hs = slice(4 * hh, 4 * hh + 4)
                              ce = "act" if hh == 0 else "dve"
                              if lastlev:
                                  P.copy(ce, Xn[0:C, hs, 0, 0:C], PL[:, hs, 0, :])
                              else:
                                  P.copy(ce, Xn[0:C, hs, :, 0:C], PL[:, hs])
                          PY = PSB[2][0:C, 0:8 * C].re("p (h s) -> p h s", h=8)
                          for h in range(8):
                              P.mm(PY[:, h, :], Xn[0:C, h, 0, 0:C], Yc[0:C, h, 0:C])
                          P.tt("dve", Yn[0:C, :, 0:C], PY, Yc[0:C, :, 0:C], ALU.add)
                      return stC
                  return [stA, stB] + [mk_level(lev) for lev in range(nlev)]

              def P2_stages(n, S):
                  Vpair_, Vpad_, AKt_, MkT_, QT_ = S["Vpair"], S["Vpad"], S["AKt"], S["MkT"], S["QT"]
                  Yf = S["Yfin"] if S["Yfin"] is not None else ((Y0, Y1)[nlev % 2])

                  def stD0():
                      if not is_p:
                          pt = PSB[7]
                          if dbl:
                              if n + 1 < nch:
                                  load_sin(n + 1)
                              for c in range(4):
                                  sh_ = sin_sets[n % 2][c // 2]
                                  P.mm(pt[:, c * 64:(c + 1) * 64],
                                       sh_[0:64, 2 * (c % 2):2 * (c % 2) + 2, :].re("p h k -> p (h k)"), identf[0:64, 0:64])
                          else:
                              sin = mtile("Wsb", 8 * 128 // 2, F32)[0:64, 0:512].re("p (h k) -> p h k", h=8)
                              P.dma("sp", sin, st_rwkv[l, n].rearrange("h v k -> v h k"))
                              for c in range(4):
                                  P.mm(pt[:, c * 64:(c + 1) * 64], sin[:, 2 * c:2 * c + 2, :].re("p h k -> p (h k)"),
                                       identf[0:64, 0:64])
                          ptv = pt[:, 0:256].re("p (c s) -> p c s", c=4)
                          for p_ in range(2):
                              ph = slice(p_ * 64, p_ * 64 + 64)
                              P.copy("dve", Hs[ph, :, p_ * 64:p_ * 64 + 64], ptv[ph])
                      P.copy("act", Hbd, Hcur)

                  def stD1():
                      PW = PSB[6][0:C, 0:512].re("p (c s) -> p c s", c=4)
                      for c in range(4):
                          P.mm(PW[:, c, :], BR[:, c, n, 0, :], Hbd[:, c, :], start=True, stop=False)
                          for p_ in range(2):
                              h = 2 * c + p_
                              P.mm(PW[:, c, :], MkT_[0:C, h, 0:C], Vpad_[0:C, h, :], start=False, stop=(p_ == 1))
                      P.copy("act", Wsb[0:C, 0:4], PW)

                  def stE():
                      PU = PSB[7][0:C, 0:512].re("p (c s) -> p c s", c=4)
                      for h in range(8):
                          c, p_ = h // 2, h % 2
                          vs = slice(p_ * 64, p_ * 64 + 64)
                          P.mm(PU[:, c, vs], Yf[0:C, h, 0:C], Wsb[0:C, c, vs])
                      with P.indep():
                          P.copy("act", Upair[0:C], PU)
                          for q_ in range(2):
                              P.copy("act", Upad[0:C].re("p (c q) s -> p c q s", q=2)[:, :, q_, q_ * 64:(q_ + 1) * 64],
                                     PU[:, :, q_ * 64:(q_ + 1) * 64])

                  def stF():
                      PO = PSB[6][:, 0:4 * C].re("p (c s) -> p c s", c=4)
                      for c in range(4):
                          P.mm(PO[:, c, :], Hbd[:, c, :], BR[:, c, n, 1, :], start=True, stop=False)
                          for p_ in range(2):
                              h = 2 * c + p_
                              P.mm(PO[:, c, :], Upad[0:C, h, :], QT_[0:C, 0, h, 0:C], start=False, stop=False)
                              P.mm(PO[:, c, :], Vpad_[0:C, h, :], QT_[0:C, 1, h, 0:C], start=False, stop=(p_ == 1))
                      P.copy("act", osb[:, :, n * C:(n + 1) * C], PO)

                  def stG():
                      PH = PSB[7].re("p (c s) -> p c s", c=4)
                      for c in range(4):
                          P.mm(PH[:, c, :], AKt_[0:C, 0, c, :], Upair[0:C, c, :], start=True, stop=False)
                          P.mm(PH[:, c, :], AKt_[0:C, 1, c, :], Vpair_[0:C, c, :], start=False, stop=True)
                      with P.indep():
                          for p_ in range(2):
                              ph = slice(p_ * 64, p_ * 64 + 64)
                              P.tt("dve", Hcur[ph, :, ph], Hcur[ph, :, ph], PH[ph, :, ph], ALU.add)
                      with P.indep():
                          for p_ in range(2):
                              ph = slice(p_ * 64, p_ * 64 + 64)
                              pcb = PC[ph, :, n:n + 1]
                              P.tt("dve", Hcur[ph, :, ph], Hcur[ph, :, ph],
                                   Tile(pcb.ap.broadcast_to([64, 4, 64]), pcb.keys), ALU.mult)
                      if not is_p:
                          store_state(Hs, o_rwkv_s[l, n], stouts[n % 2] if dbl else stout)
                  return [stD0, stD1, stE, stF, stG]

              for st_ in P1_stages(0, sets[0]):
                  st_()
              for n in range(nch):
                  p2 = P2_stages(n, sets[n % 2])
                  p1 = P1_stages(n + 1, sets[(n + 1) % 2]) if n + 1 < nch else []
                  seq = []
                  i1_, i2_ = 0, 0
                  while i1_ < len(p1) or i2_ < len(p2):
                      if i2_ < len(p2):
                          seq.append(p2[i2_]); i2_ += 1
                      if i1_ < len(p1):
                          seq.append(p1[i1_]); i1_ += 1
                  for st_ in seq:
                      st_()
              if last_p:
                  store_state(Hm[l], o_rwkv_p[l])

              stage(9)
              def run2(gens, lag=2):
                  live = [True] * len(gens)
                  step = 0
                  while any(live):
                      for gi in range(len(gens)):
                          if gi * lag > step or not live[gi]:
                              continue
                          try:
                              next(gens[gi])
                          except StopIteration:
                              live[gi] = False
                      step += 1

              def rpost_gen(c, t1, t2, ta, b0):
                  oc = osb[:, c]
                  p1, p2 = PSB[b0], PSB[b0 + 1]
                  P.act(t1, oc, AF.Square)
                  P.mm(p1[:, 0:NT], bdones, oc)
                  P.mm(p2[:, 0:NT], bdones, t1)
                  yield
                  P.act(t2, p1[:, 0:NT], AF.Copy, scale=1.0 / 64)
                  yield
                  P.act(ta, t2, AF.Square)
                  yield
                  P.stt(ta, p2[:, 0:NT], 1.0 / 64, ta, ALU.mult, ALU.subtract)
                  yield
                  P.act(ta, ta, AF.Ln, bias=GN_EPS, scale=1.0)
                  yield
                  P.act(ta, ta, AF.Exp, scale=-0.5)
                  yield
                  P.tt("dve", t2, oc, t2, ALU.subtract)
                  yield
                  P.tt("dve", t2, t2, ta, ALU.mult)
                  yield
                  P.act(t2, t2, AF.Identity, bias=pv["lb"][:, l, c:c + 1], scale=pv["lg"][:, l, c:c + 1])
                  yield
                  P.tt("dve", t2, t2, bon[:, c], ALU.add)
                  yield
                  P.tt("dve", mixT[c][:, 0:NT], t2, gst[:, c], ALU.mult)
                  yield
              for c0 in (0, 2):
                  run2([rpost_gen(c0, T_1, T_2, T_a, 0), rpost_gen(c0 + 1, T_kk, T_km, T_eM, 2)])

              stage(10)
              def consume_gla(j, pt, m):
                  P.copy("act", pa_flat(j)[0:m], pt[0:m, 0:NT])
              proj_feature_major(NT, w_in[l], SHIFT_W, IN_W - SHIFT_W, consume_gla)
              flo = lor
              P.copy("act", flo[0:16], pa_flat(12)[0:16])
              AKflat = mtile("AK", 4 * NT_, BF16)
              qtz = AKflat[:, 0:4 * NT_].re("p (c q n) -> p c q n", c=2, q=2)[:, :, :, 0:NT]
              ktb = AKflat[:, 4 * NT_:6 * NT_].re("p (c n) -> p c n", c=2)[:, :, 0:NT]
              P.memset("dve", AKflat[:, 0:4 * NT_], 0.0)
              kdb = mtile("BR", 4 * NT_, BF16, "p (c n) -> p c n", c=4)[:, 0:2, 0:NT]
              vgb = mtile("vb", 4 * NT_ // 2, BF16, "p (c n) -> p c n", c=4)[:, :, 0:NT]
              ogs = osb
              PCg = PC
              for c in range(2):
                  pz = PSB[c]
                  P.mm(pz[:, 0:NT], fup[0:16, l, c * 128:(c + 1) * 128], flo[0:16])
                  P.act(T_s1, pz[:, 0:NT], AF.Sigmoid, bias=fb_t[:, l, c:c + 1], scale=1.0)
                  P.act(T_s1, T_s1, AF.Ln)
                  P.scan(T_cs, rmask[:, 0:NT], T_s1)
                  P.act(T_eP, T_cs, AF.Exp, scale=1.0 / 16)
                  P.act(T_eM, T_cs, AF.Exp, scale=-1.0 / 16)
                  P.copy("act", PCg[:, c], cv(T_eP)[:, :, C - 1])
                  for q_ in range(2):
                      ph = slice(q_ * 64, q_ * 64 + 64)
                      P.stt(qtz[ph, c, q_], pa_flat(c)[ph], 0.125, T_eP[ph], ALU.mult, ALU.mult)
                  P.tt("dve", T_1, pa_flat(2 + c), T_eM, ALU.mult)
                  P.copy("act", ktb[:, c], T_1)
                  pcb = PCg[:, c, :]
                  P.tt("dve", cv(kdb[:, c]), cv(T_1), Tile(pcb.ap.unsqueeze(2).broadcast_to([128, nch, C]), pcb.keys), ALU.mult)
              for h in range(4):
                  P.copy("act", vgb[:, h], pa_flat(4 + h))
              Sgb = mtile("Sgb", 2 * 128 // 2, BF16, "p (c s) -> p c s", c=2)
              ATsb = mtile("ATsb", 4 * 64 // 2, BF16, "p (h s) -> p h s", h=4)
              Vt = mtile("Vt", 4 * 128 // 2, BF16, "p (h s) -> p h s", h=4)
              kdt = mtile("kdt", 2 * 128 // 2, BF16, "p (c s) -> p c s", c=2)
              pre_s = (not is_p) and PA_W >= 336 and NT_ >= 320 and NT <= 64 and nch <= 16
              if is_p:
                  Scur = Sg[l]
              elif pre_s:
                  osb_full = mtile("osb", 4 * NT_, F32, "p (c n) -> p c n", c=4).ap
                  Sbufs = []
                  for i_ in range(nch):
                      apb = pa_ap[:, i_, 80:336] if i_ < 14 else osb_full[:, i_ - 14, 64:320]
                      Sbufs.append(Tile(apb.rearrange("p (c s) -> p c s", c=2), "m.Sb%d" % i_))
                  for n in range(nch):
                      for c in range(2):
                          for p_ in range(2):
                              P.dma("sp", Sbufs[n][p_ * 64:(p_ + 1) * 64, c, :], st_gla[l, n, 2 * c + p_])
              else:
                  Scur = mtile("sttmp", 8 * 64, F32)[:, 0:256].re("p (c s) -> p c s", c=2)
              for n in range(nch):
                  if pre_s:
                      Scur = Sbufs[n]
                  elif not is_p:
                      for c in range(2):
                          for p_ in range(2):
                              P.dma("sp", Scur[p_ * 64:(p_ + 1) * 64, c, :], st_gla[l, n, 2 * c + p_])
                  P.copy("act", Sgb, Scur)
                  pvt = PSB[6]
                  for h in range(4):
                      P.mm(pvt[0:C, h * 128:(h + 1) * 128], vgb[:, h, n * C:(n + 1) * C], identb)
                  P.copy("act", Vt[0:C], pvt[0:C, 0:512].re("p (h s) -> p h s", h=4))
                  pkt = PSB[5]
                  for c in range(2):
                      P.mm(pkt[0:C, c * 128:(c + 1) * 128], kdb[:, c, n * C:(n + 1) * C], identb)
                  P.copy("act", kdt[0:C], pkt[0:C, 0:256].re("p (c s) -> p c s", c=2))
                  PA_ = PSB[0][0:C, 0:4 * C].re("p (h s) -> p h s", h=4)
                  for h in range(4):
                      c, p_ = h // 2, h % 2
                      ph = slice(p_ * 64, p_ * 64 + 64)
                      P.mm(PA_[:, h, :], ktb[:, c, n * C:(n + 1) * C], qtz[:, c, p_, n * C:(n + 1) * C])
                  imk = imask[0:C, 0:C]
                  P.tt("dve", ATsb[0:C, :, 0:C], PA_, Tile(imk.ap.unsqueeze(1).broadcast_to([C, 4, C]), imk.keys), ALU.mult)
                  PO = PSB[1][:, 0:4 * C].re("p (h s) -> p h s", h=4)
                  for h in range(4):
                      c, p_ = h // 2, h % 2
                      ph = slice(p_ * 64, p_ * 64 + 64)
                      P.mm(PO[:, h, :], Vt[0:C, h, :], ATsb[0:C, h, 0:C], start=True, stop=False)
                      P.mm(PO[:, h, :], Sgb[:, c, :], qtz[:, c, p_, n * C:(n + 1) * C], start=False, stop=True)
                  P.copy("act", ogs[:, :, n * C:(n + 1) * C], PO)
                  PS_ = PSB[2].re("p (h s) -> p h s", h=4)
                  for h in range(4):
                      c = h // 2
                      P.mm(PS_[:, h, :], kdt[0:C, c, :], Vt[0:C, h, :])
                  with P.indep():
                      for h in range(4):
                          c, p_ = h // 2, h % 2
                          ph = slice(p_ * 64, p_ * 64 + 64)
                          P.stt(Scur[ph, c, :], Scur[ph, c, :], PCg[ph, c, n:n + 1], PS_[ph, h, :], ALU.mult, ALU.add)
                  if not is_p:
                      for c in range(2):
                          for p_ in range(2):
                              P.dma("sp", o_gla_s[l, n, 2 * c + p_], Scur[p_ * 64:(p_ + 1) * 64, c, :])
              if last_p:
                  for c in range(2):
                      for p_ in range(2):
                          P.dma("sp", o_gla_p[l, 2 * c + p_], Sg[l][p_ * 64:(p_ + 1) * 64, c, :])
              def gpost_gen(h, t1, t2, ta, b0):
                  oc = ogs[:, h]
                  p1 = PSB[b0]
                  P.act(t1, oc, AF.Square)
                  P.mm(p1[:, 0:NT], allones, t1)
                  yield
                  P.act(t2, p1[:, 0:NT], AF.Ln, bias=RMS_EPS, scale=1.0 / 128)
                  yield
                  P.act(t2, t2, AF.Exp, scale=-0.5)
                  yield
                  P.tt("dve", t2, oc, t2, ALU.mult)
                  P.act(ta, pa_flat(8 + h), AF.Silu)
                  yield
                  P.stt(mixT[4 + h][:, 0:NT], t2, ng_t[:, l:l + 1], ta, ALU.mult, ALU.mult)
                  yield
              for h0 in (0, 2):
                  run2([gpost_gen(h0, T_1, T_2, T_a, 0), gpost_gen(h0 + 1, T_kk, T_km, T_eM, 1)])

              stage(11)
              P.barrier()
              wo_v = w_out[l].rearrange("(k p) c -> p k c", p=128)
              proj_token_major(NT, lambda k: mixT[k], KC,
                               lambda k0, nk_, hf: wo_v[:, k0:k0 + nk_, hf * 512:(hf + 1) * 512],
                               ln1_g[l], ln1_b[l])
              make_xT(NT)

              stage(12)
              hT = ftile("hT", FC * NT_ // 2, BF16, "p (j n) -> p j n", j=FC)[:, :, 0:NT]
              gts = [ftile("gt%d" % i, GT_W, F32)[:, 0:nseq * (Tq + 2)].re("p (s t) -> p s t", t=Tq + 2) for i in range(2)]
              accs = [ftile("acc%d" % i, NT_, F32)[:, 0:NT].re("p (s t) -> p s t", t=Tq) for i in range(2)]
              wu_v = w_up[l].rearrange("(k p) c -> p k c", p=128)
              gate_ps = {}
              if not is_p:
                  cpast = ftile("cpast", FC * NSQ * 2, F32, "p (j s r) -> p j s r", j=FC, r=2)
                  cout = ftile("cout", FC * NSQ * 2, F32, "p (j s r) -> p j s r", j=FC, r=2)
                  R2 = NSQ * 2
                  ctok = ftile("ctok", DFF, F32)
                  if R2 <= 32:
                      P.dma("sp", ctok[0:R2], st_conv[l].rearrange("s r w -> (s r) w"))
                      cpf = cpast.re("p j s r -> p (j s r)")
                      for j in range(FC):
                          bk, cc = divmod(j, 16)
                          P.mm(PSB[4 + bk][:, cc * R2:(cc + 1) * R2], ctok[0:R2, j * 128:(j + 1) * 128], identf[0:R2, 0:R2])
                      P.copy("act", cpf[:, 0:16 * R2], PSB[4][:, 0:16 * R2])
                      P.copy("act", cpf[:, 16 * R2:FC * R2], PSB[5][:, 0:(FC - 16) * R2])
                  else:
                      for j in range(FC):
                          for r_ in range(2):
                              P.dma("sp", cpast[:, j, :, r_], st_conv[l, :, r_, j * 128:(j + 1) * 128].rearrange("s p -> p s"))
              j0 = 0
              while j0 < FC:
                  nj = min(4, FC - j0)
                  wg = wload(wu_v[:, :, j0 * 128:(j0 + nj) * 128], KC, nj * 128)
                  wvv = wload(wu_v[:, :, DFF + j0 * 128: DFF + (j0 + nj) * 128], KC, nj * 128)
                  for jj in range(nj):
                      j = j0 + jj
                      pg_, pv_ = PSB[(2 * j) % 4], PSB[(2 * j + 1) % 4]
                      for k in range(KC):
                          P.mm(pg_[:, 0:NT], wg[:, k, jj * 128:(jj + 1) * 128], xT[:, k, 0:NT], start=(k == 0), stop=(k == KC - 1))
                      for k in range(KC):
                          P.mm(pv_[:, 0:NT], wvv[:, k, jj * 128:(jj + 1) * 128], xT[:, k, 0:NT], start=(k == 0), stop=(k == KC - 1))
                      gt = gts[j % 2]
                      acc = accs[j % 2]
                      P.copy("act", gt[:, :, 2:], pg_[:, 0:NT].re("p (s t) -> p s t", t=Tq))
                      if is_p:
                          P.copy("act", gt[:, 0, 0:2], cst[l][:, j, :])
                          P.copy("act", cst[l][:, j, :], gt[:, 0, Tq:Tq + 2])
                          if last_p:
                              P.dma("sp", o_conv_p[l, :, j * 128:(j + 1) * 128].rearrange("r p -> p r"), cst[l][:, j, :])
                      else:
                          P.copy("act", gt[:, :, 0:2], cpast[:, j])
                          P.copy("act", cout[:, j], gt[:, :, Tq:Tq + 2])
                      P.act(acc, gt[:, :, 2:], AF.Identity, bias=cb_t[:, l, j:j + 1], scale=cw_t[:, l, 2, j:j + 1])
                      P.stt(acc, gt[:, :, 1:Tq + 1], cw_t[:, l, 1, j:j + 1], acc, ALU.mult, ALU.add)
                      P.stt(acc, gt[:, :, 0:Tq], cw_t[:, l, 0, j:j + 1], acc, ALU.mult, ALU.add)
                      P.act(acc, acc, AF.Gelu)
                      P.tt("dve", hT[:, j].re("p (s t) -> p s t", t=Tq), acc, pv_[:, 0:NT].re("p (s t) -> p s t", t=Tq), ALU.mult)
                  j0 += nj
              if not is_p:
                  if R2 <= 32:
                      cof = cout.re("p j s r -> p j (s r)")
                      for j in range(FC):
                          bk, cc = divmod(j, 4)
                          P.mm(PSB[bk][0:R2, cc * 128:(cc + 1) * 128], cof[:, j, :], identf)
                      for bk in range((FC + 3) // 4):
                          w_ = min(512, DFF - bk * 512)
                          P.copy("act", ctok[0:R2, bk * 512:bk * 512 + w_], PSB[bk][0:R2, 0:w_])
                      P.dma("sp", o_conv_s[l].rearrange("s r w -> (s r) w"), ctok[0:R2])
                  else:
                      for j in range(FC):
                          for r_ in range(2):
                              P.dma("sp", o_conv_s[l, :, r_, j * 128:(j + 1) * 128].rearrange("s p -> p s"), cout[:, j, :, r_])

              stage(13)
              wd_v = w_down[l].rearrange("(k p) c -> p k c", p=128)
              proj_token_major(NT, lambda k: hT[:, k], FC,
                               lambda k0, nk_, hf: wd_v[:, k0:k0 + nk_, hf * 512:(hf + 1) * 512],
                               ln2_g[l], ln2_b[l])
              if l < L - 1:
                  make_xT(NT)
              P.barrier()

          ntile = (NT + 127) // 128
          for t in range(ntile):
              rows = min(128, NT - t * 128)
              dst = y_p[b * TB + t * 128: b * TB + t * 128 + rows, :] if is_p else y_s[t * 128:t * 128 + rows, :]
              P.dma("sp", dst, xres[t][0:rows])

    except _Stop:
        pass
    P.finish()
    P.emit(st)
    st.close()
    return nc, P, dbg


N_CORES = 8
_CACHE = {}


def _core_inputs(inp, i, NSQ):
    sl = slice(i * NSQ, (i + 1) * NSQ)
    L = inp["w_in"].shape[0]
    c = np.ascontiguousarray
    m = {
        "x_p": c(inp["x_prompt"][i]),
        "x_s": c(inp["x_sample"][sl].reshape(NSQ * DEC_SEQ, D)),
        "st_rwkv": c(inp["state_rwkv"][:, sl]),
        "st_shift": c(inp["state_shift"][:, sl, 0, :]),
        "st_gla": c(inp["state_gla"][:, sl]),
        "st_conv": c(inp["state_conv"][:, sl]),
        "r_k": c(inp["r_k"].reshape(L, AW)),
    }
    for k in ("w_in", "tok_mu", "w0", "w_lora_up", "a0", "a_lora_up", "g_lora_up", "k_k", "k_a", "lnx_g", "lnx_b",
              "vres_bias", "vres_down", "vres_up", "gla_f_up", "gla_f_bias", "gla_norm_g", "w_out", "ln1_g",
              "ln1_b", "w_up", "conv_w", "conv_b", "w_down", "ln2_g", "ln2_b"):
        m[k] = c(inp[k])
    return m


def run(inp, TB=512, debug=False, n_cores=N_CORES, trace=False, stop=99):
    inp = {k: np.asarray(v, dtype=np.float32) for k, v in inp.items()}
    L = inp["w_in"].shape[0]
    B, SEQ, _ = inp["x_prompt"].shape
    DB = inp["x_sample"].shape[0]
    assert B == n_cores
    NSQ = DB // n_cores
    TB = min(TB, SEQ)
    key = (L, SEQ, NSQ, TB, debug, stop)
    if key not in _CACHE:
        _CACHE[key] = build_program(L, SEQ, NSQ, TB, debug, stop)
    nc, P, dbg = _CACHE[key]
    in_maps = [_core_inputs(inp, i, NSQ) for i in range(n_cores)]
    res = run_bass_kernel_spmd(nc, in_maps, core_ids=list(range(n_cores)), **({"trace": True} if trace else {}))
    R = res.results
    cat = lambda name, axis: np.concatenate([r[name] for r in R], axis=axis)
    stk = lambda name: np.stack([r[name] for r in R], axis=1)
    outs = (
        np.stack([r["y_p"] for r in R], axis=0),
        cat("y_s", 0).reshape(DB, DEC_SEQ, D),
        stk("o_rwkv_p"),
        cat("o_rwkv_s", 1),
        stk("o_shift_p")[:, :, None, :],
        cat("o_shift_s", 1)[:, :, None, :],
        stk("o_gla_p"),
        cat("o_gla_s", 1),
        stk("o_conv_p"),
        cat("o_conv_s", 1),
    )
    outs = tuple(np.ascontiguousarray(o, dtype=np.float32) for o in outs)
    if debug:
        return outs, R, res
    return outs


def kernel(**inputs):
    return run(inputs)
```

```python
import math
import numpy as np
from contextlib import ExitStack
import concourse.bass as bass
import concourse.mybir as mybir
from concourse.bass_utils import run_bass_kernel_spmd

F32 = mybir.dt.float32
BF16 = mybir.dt.bfloat16
ALU = mybir.AluOpType
AF = mybir.ActivationFunctionType
ENGS = ("pe", "act", "dve", "pool", "sp")
USE_DRAIN = True


class Tile:
    __slots__ = ("ap", "keys")

    def __init__(self, ap, keys):
        self.ap = ap
        self.keys = keys if isinstance(keys, tuple) else (keys,)

    def __getitem__(self, idx):
        return Tile(self.ap[idx], self.keys)

    def v(self, ap):
        return Tile(ap, self.keys)

    def re(self, pattern_, **kw):
        return Tile(self.ap.rearrange(pattern_, **kw), self.keys)

    def bc(self, shape):
        return Tile(self.ap.broadcast_to(shape), self.keys)


class Op:
    __slots__ = ("eng", "fn", "is_dma", "deps", "inc", "ticket", "waits", "dsem", "dval", "idx", "prewait", "drain", "grp")

    def __init__(self, eng, fn, is_dma):
        self.eng = eng
        self.fn = fn
        self.is_dma = is_dma
        self.deps = set()
        self.inc = False
        self.ticket = 0
        self.waits = []
        self.dsem = None
        self.dval = 0
        self.prewait = None
        self.drain = False
        self.grp = None


def _keys(ts):
    out = []
    for t in ts:
        if isinstance(t, Tile):
            out.extend(t.keys)
    return out


def _ap(t):
    return t.ap if isinstance(t, Tile) else t


class Prog:
    skip_same = ("pe",)
    grp = None
    _gctr = 0

    def indep(self):
        prog = self

        class _G:
            def __enter__(self_):
                prog._gctr += 1
                self_.prev = prog.grp
                prog.grp = prog._gctr

            def __exit__(self_, *a):
                prog.grp = self_.prev
                return False
        return _G()

    def __init__(self, nc, n_dma_sems=10):
        self.nc = nc
        self.ops = []
        self.last_writer = {}
        self.readers = {}
        self.n_dma_sems = n_dma_sems

    def add(self, eng, fn, reads, writes, is_dma=False):
        op = Op(eng, fn, is_dma)
        op.idx = len(self.ops)
        op.grp = self.grp
        deps = op.deps
        lw, rd = self.last_writer, self.readers
        for k in reads:
            w = lw.get(k)
            if w is not None:
                deps.add(w)
        for k in writes:
            w = lw.get(k)
            if w is not None:
                deps.add(w)
            r = rd.get(k)
            if r:
                deps.update(r)
        for k in writes:
            lw[k] = op.idx
            rd[k] = []
        ws = set(writes)
        for k in reads:
            if k not in ws:
                rd.setdefault(k, []).append(op.idx)
        if self.bar_deps and eng not in self.bar_seen:
            deps.update(self.bar_deps)
            self.bar_seen.add(eng)
        deps.discard(op.idx)
        if eng in self.skip_same and not is_dma:
            for d in [d for d in deps if self.ops[d].eng == eng and not self.ops[d].is_dma]:
                deps.discard(d)
        self.ops.append(op)
        return op

    bar_deps = None
    bar_seen = None
    bar_start = 0

    def barrier(self):
        last = {}
        deps = set()
        for op in self.ops[self.bar_start:]:
            last[op.eng] = op.idx
            if op.is_dma:
                deps.add(op.idx)
        deps.update(last.values())
        if self.bar_deps and len(self.bar_seen) < len(ENGS):
            deps.update(self.bar_deps)
        self.bar_deps = deps
        self.bar_seen = set()
        self.bar_start = len(self.ops)
        self.last_writer = {}
        self.readers = {}

    def emit(self, stack):
        nc = self.nc
        ops = self.ops
        drained_upto = {e: -1 for e in ENGS}
        for op in ops:
            best = {}
            keep = set()
            same = []
            for d in op.deps:
                dop = ops[d]
                if dop.is_dma:
                    keep.add(d)
                elif dop.eng == op.eng and not op.is_dma and op.eng in ("act", "dve", "pool"):
                    same.append(d)
                elif best.get(dop.eng, -1) < d:
                    best[dop.eng] = d
            if same:
                need = [d for d in same if d >= drained_upto[op.eng]
                        and not (op.grp is not None and ops[d].grp == op.grp)]
                if need:
                    op.drain = True
                    drained_upto[op.eng] = op.idx
            keep.update(best.values())
            op.deps = keep
            for d in keep:
                ops[d].inc = True
        sems = {e: stack.enter_context(nc.semaphore("s_" + e)) for e in ENGS}
        dsems = {e: [stack.enter_context(nc.semaphore("d_%s_%d" % (e, i))) for i in range(self.n_dma_sems)]
                 for e in ENGS}
        cnt = {e: 0 for e in ENGS}
        dcnt = {e: [0] * self.n_dma_sems for e in ENGS}
        drr = {e: 0 for e in ENGS}
        for op in ops:
            if op.is_dma:
                i = drr[op.eng]
                drr[op.eng] = (i + 1) % self.n_dma_sems
                op.dsem = dsems[op.eng][i]
                if dcnt[op.eng][i] > 0:
                    op.prewait = (op.dsem, dcnt[op.eng][i])
                dcnt[op.eng][i] += 16
                op.dval = dcnt[op.eng][i]
            elif op.inc:
                cnt[op.eng] += 1
                op.ticket = cnt[op.eng]
        waited = {e: {} for e in ENGS}
        for op in ops:
            w = waited[op.eng]
            need = {}
            for d in op.deps:
                dop = ops[d]
                if dop.is_dma:
                    s, v = dop.dsem, dop.dval
                else:
                    s, v = sems[dop.eng], dop.ticket
                k = id(s)
                if w.get(k, 0) >= v:
                    continue
                if k not in need or need[k][1] < v:
                    need[k] = (s, v)
            if op.prewait is not None:
                s, v = op.prewait
                k = id(s)
                if w.get(k, 0) < v and (k not in need or need[k][1] < v):
                    need[k] = (s, v)
            for k, (s, v) in need.items():
                w[k] = v
            op.waits = list(need.values())
        per = {e: [op for op in ops if op.eng == e] for e in ENGS}
        self.stats = {e: (len(per[e]), cnt[e], sum(len(o.waits) for o in per[e])) for e in ENGS}

        def run(eng_name, eng):
            se = sems[eng_name]
            for op in per[eng_name]:
                for (s, v) in op.waits:
                    eng.wait_ge(s, v)
                if op.drain and USE_DRAIN:
                    eng.drain()
                ins = op.fn(eng)
                if op.is_dma:
                    ins.then_inc(op.dsem, 16)
                elif op.inc:
                    ins.then_inc(se, 1)

        with nc.Block() as block:
            @block.tensor
            def _(eng):
                run("pe", eng)

            @block.scalar
            def _(eng):
                run("act", eng)

            @block.vector
            def _(eng):
                run("dve", eng)

            @block.gpsimd
            def _(eng):
                run("pool", eng)

            @block.sync
            def _(eng):
                run("sp", eng)

    def mm(self, out, lhsT, rhs, start=True, stop=True):
        o, l, r = out.ap, lhsT.ap, rhs.ap
        return self.add("pe", lambda e: e.matmul(o, l, r, start=start, stop=stop),
                        _keys([lhsT, rhs]), _keys([out]))

    def transpose(self, out, in_, ident):
        o, i, d = out.ap, in_.ap, ident.ap
        return self.add("pe", lambda e: e.transpose(o, i, d), _keys([in_, ident]), _keys([out]))

    def act(self, out, in_, func, bias=None, scale=None):
        o, i = out.ap, in_.ap
        kw = {}
        if bias is not None:
            kw["bias"] = _ap(bias)
        if scale is not None:
            kw["scale"] = _ap(scale)
        return self.add("act", lambda e: e.activation(o, i, func, **kw), _keys([in_, bias, scale]), _keys([out]))

    def tt(self, eng, out, in0, in1, op):
        o, a, b = out.ap, in0.ap, in1.ap
        return self.add(eng, lambda e: e.tensor_tensor(o, a, b, op), _keys([in0, in1]), _keys([out]))

    def ts(self, eng, out, in0, s1, op0, s2=None, op1=None):
        o, a = out.ap, in0.ap
        a1, a2 = _ap(s1), _ap(s2)
        if op1 is None:
            return self.add(eng, lambda e: e.tensor_scalar(o, a, a1, None, op0), _keys([in0, s1]), _keys([out]))
        return self.add(eng, lambda e: e.tensor_scalar(o, a, a1, a2, op0, op1), _keys([in0, s1, s2]), _keys([out]))

    def stt(self, out, in0, scalar, in1, op0, op1):
        o, a, b = out.ap, in0.ap, in1.ap
        s = _ap(scalar)
        return self.add("dve", lambda e: e.scalar_tensor_tensor(o, a, s, b, op0, op1),
                        _keys([in0, in1, scalar]), _keys([out]))

    def copy(self, eng, out, in_):
        o, i = out.ap, in_.ap
        if eng == "act":
            return self.add("act", lambda e: e.copy(o, i), _keys([in_]), _keys([out]))
        return self.add(eng, lambda e: e.tensor_copy(o, i), _keys([in_]), _keys([out]))

    def recip(self, out, in_):
        o, i = out.ap, in_.ap
        return self.add("dve", lambda e: e.reciprocal(o, i), _keys([in_]), _keys([out]))

    def memset(self, eng, out, val):
        o = out.ap
        return self.add(eng, lambda e: e.memset(o, val), [], _keys([out]))

    def scan(self, out, d0, d1):
        o, a, b = out.ap, d0.ap, d1.ap
        return self.add("dve", lambda e: e.tensor_tensor_scan(o, a, b, 0.0, ALU.mult, ALU.add),
                        _keys([d0, d1]), _keys([out]))

    def dma(self, eng, out, in_, **kw):
        o, i = _ap(out), _ap(in_)
        return self.add(eng, lambda e: e.dma_start(out=o, in_=i, **kw), _keys([in_]), _keys([out]), is_dma=True)

    def finish(self, eng="sp"):
        deps = [op.idx for op in self.ops if op.is_dma]
        op = self.add(eng, lambda e: e.nop(), [], [])
        op.deps.update(deps)
        for e in ENGS:
            last = [o.idx for o in self.ops if o.eng == e and not o.is_dma and o.idx != op.idx]
            if last:
                op.deps.add(last[-1])
        return op


D = 1024
KC = 8
AW = 512
SHIFT_W = 1792
IN_W = 3344
DFF = 2816
FC = 22
ALPHA = 8 ** 0.25
GN_EPS = 64e-5
RMS_EPS = 1e-5
LN_EPS = 1e-5
KDEC = math.exp(-0.5)
DEC_SEQ = 4


class _Stop(Exception):
    pass


def _round_robin(gens, lag=1):
    live = [True] * len(gens)
    step = 0
    while any(live):
        for gi in range(len(gens)):
            if gi * lag > step or not live[gi]:
                continue
            try:
                next(gens[gi])
            except StopIteration:
                live[gi] = False
        step += 1


def build_program(L, SEQ, NSQ, TB, debug=False, stop=99):
    def stage(n):
        if n > stop:
            raise _Stop()
    nc = bass.Bass("TRN2", target_bir_lowering=False)
    P = Prog(nc)
    NTMAX = max(TB, NSQ * DEC_SEQ)
    NTILE_MAX = (NTMAX + 127) // 128
    NCHMAX = max(TB // 64, NSQ)
    NS4 = NSQ * DEC_SEQ

    def din(name, shape):
        return nc.dram_tensor(name, list(shape), F32, kind="ExternalInput").ap()

    def dout(name, shape):
        return nc.dram_tensor(name, list(shape), F32, kind="ExternalOutput").ap()

    x_p = din("x_p", [SEQ, D])
    x_s = din("x_s", [NS4, D])
    st_rwkv = din("st_rwkv", [L, NSQ, 8, 64, 64])
    st_shift = din("st_shift", [L, NSQ, SHIFT_W])
    st_gla = din("st_gla", [L, NSQ, 4, 64, 128])
    st_conv = din("st_conv", [L, NSQ, 2, DFF])
    w_in = din("w_in", [L, D, IN_W])
    tok_mu = din("tok_mu", [L, SHIFT_W])
    w0 = din("w0", [L, AW])
    w_lora_up = din("w_lora_up", [L, 64, AW])
    a0 = din("a0", [L, AW])
    a_lora_up = din("a_lora_up", [L, 64, AW])
    g_lora_up = din("g_lora_up", [L, 128, AW])
    k_k = din("k_k", [L, AW])
    k_a = din("k_a", [L, AW])
    r_k = din("r_k", [L, AW])
    lnx_g = din("lnx_g", [L, AW])
    lnx_b = din("lnx_b", [L, AW])
    LV = max(L - 1, 1)
    vres_bias = din("vres_bias", [LV, AW])
    vres_down = din("vres_down", [LV, AW, 32])
    vres_up = din("vres_up", [LV, 32, AW])
    gla_f_up = din("gla_f_up", [L, 16, 256])
    gla_f_bias = din("gla_f_bias", [L, 256])
    gla_norm_g = din("gla_norm_g", [L, 128])
    w_out = din("w_out", [L, D, D])
    ln1_g = din("ln1_g", [L, D])
    ln1_b = din("ln1_b", [L, D])
    w_up = din("w_up", [L, D, 2 * DFF])
    conv_w = din("conv_w", [L, 3, DFF])
    conv_b = din("conv_b", [L, DFF])
    w_down = din("w_down", [L, DFF, D])
    ln2_g = din("ln2_g", [L, D])
    ln2_b = din("ln2_b", [L, D])

    y_p = dout("y_p", [SEQ, D])
    y_s = dout("y_s", [NS4, D])
    o_rwkv_p = dout("o_rwkv_p", [L, 8, 64, 64])
    o_rwkv_s = dout("o_rwkv_s", [L, NSQ, 8, 64, 64])
    o_shift_p = dout("o_shift_p", [L, SHIFT_W])
    o_shift_s = dout("o_shift_s", [L, NSQ, SHIFT_W])
    o_gla_p = dout("o_gla_p", [L, 4, 64, 128])
    o_gla_s = dout("o_gla_s", [L, NSQ, 4, 64, 128])
    o_conv_p = dout("o_conv_p", [L, 2, DFF])
    o_conv_s = dout("o_conv_s", [L, NSQ, 2, DFF])
    dbg = {}

    st = ExitStack()
    st.enter_context(nc.allow_non_contiguous_dma(reason="small strided parameter/state transfers"))
    st.enter_context(nc.allow_low_precision(reason="bf16 matmul operands, fp32 accumulation"))

    def sb(name, shape, dt, nkeys=None):
        t = st.enter_context(nc.sbuf_tensor(name, list(shape), dt))
        return Tile(t.ap(), name)

    def sbk(name, shape, dt, n):
        t = st.enter_context(nc.sbuf_tensor(name, list(shape), dt))
        ap = t.ap()
        return [Tile(ap[:, i], "%s.%d" % (name, i)) for i in range(n)]

    xres = sbk("xres", [128, NTILE_MAX, D], F32, NTILE_MAX)
    xT = sb("xT", [128, KC, NTMAX], BF16)
    vfirst = sbk("vfirst", [128, 4, NTMAX], F32, 4)
    mixT = sbk("mixT", [128, KC, NTMAX], BF16, KC)
    Hm = sbk("Hm", [128, L, 4, 128], F32, L)
    Sg = sbk("Sg", [128, L, 2, 128], F32, L)
    shp = sbk("shp", [128, L, 14], F32, L)
    cst = sbk("cst", [128, L, FC, 2], F32, L)
    mu_t = sb("mu_t", [128, L, 14], F32)
    pv = {}
    for nm, src in (("w0", w0), ("a0", a0), ("kk", k_k), ("ka", k_a), ("rk", r_k), ("lg", lnx_g), ("lb", lnx_b)):
        pv[nm] = sb("pv_" + nm, [128, L, 4], F32)
        P.dma("sp", pv[nm], src.rearrange("l (c p) -> p l c", p=128))
    pv["v0"] = sb("pv_v0", [128, LV, 4], F32)
    P.dma("sp", pv["v0"], vres_bias.rearrange("l (c p) -> p l c", p=128))
    omka = sb("omka", [128, L, 4], F32)
    P.ts("dve", omka, pv["ka"], -1.0, ALU.mult, 1.0, ALU.add)
    P.dma("sp", mu_t, tok_mu.rearrange("l (c p) -> p l c", p=128))
    omu = sb("omu", [128, L, 14], F32)
    P.ts("dve", omu, mu_t, -1.0, ALU.mult, 1.0, ALU.add)
    fb_t = sb("fb_t", [128, L, 2], F32)
    P.dma("sp", fb_t, gla_f_bias.rearrange("l (c p) -> p l c", p=128))
    ng_t = sb("ng_t", [128, L], F32)
    P.dma("sp", ng_t, gla_norm_g.rearrange("l p -> p l"))
    cw_t = sb("cw_t", [128, L, 3, FC], F32)
    P.dma("sp", cw_t, conv_w.rearrange("l j (c p) -> p l j c", p=128))
    cb_t = sb("cb_t", [128, L, FC], F32)
    P.dma("sp", cb_t, conv_b.rearrange("l (c p) -> p l c", p=128))
    wa_up = sb("wa_up", [128, L, AW], BF16)
    P.dma("pool", wa_up[0:64], w_lora_up.rearrange("l k c -> k l c"))
    P.dma("pool", wa_up[64:128], a_lora_up.rearrange("l k c -> k l c"))
    g_up = sb("g_up", [128, L, AW], BF16)
    P.dma("pool", g_up, g_lora_up.rearrange("l k c -> k l c"))
    vdn = sb("vdn", [128, LV, 4, 32], BF16)
    P.dma("pool", vdn, vres_down.rearrange("l (c p) r -> p l c r", p=128))
    vup = sb("vup", [32, LV, AW], BF16)
    P.dma("pool", vup, vres_up.rearrange("l r c -> r l c"))
    fup = sb("fup", [16, L, 256], BF16)
    P.dma("pool", fup, gla_f_up.rearrange("l r c -> r l c"))

    identb = sb("identb", [128, 128], BF16)
    identf = sb("identf", [128, 128], F32)
    for t in (identb, identf):
        P.memset("dve", t, 0.0)
        tap = t.ap
        P.add("pool", (lambda a: (lambda e: e.affine_select(a, a, [[-1, 128]], ALU.not_equal, 1.0, base=0,
                                                            channel_multiplier=1)))(tap), list(t.keys), list(t.keys))
    imask = sb("imask", [64, 64], F32)
    smask = sb("smask", [64, 64], F32)
    smaskT = sb("smaskT", [64, 64], F32)
    for t, pat, cm, base in ((imask, 1, -1, 0), (smask, 1, -1, -1), (smaskT, -1, 1, -1)):
        P.memset("dve", t, 1.0)
        tap = t.ap
        P.add("pool", (lambda a, pat, cm, base: (lambda e: e.affine_select(
            a, a, [[pat, 64]], ALU.is_ge, 0.0, base=base, channel_multiplier=cm)))(tap, pat, cm, base),
            list(t.keys), list(t.keys))
    bdones = sb("bdones", [128, 128], F32)
    P.memset("dve", bdones, 0.0)
    P.memset("dve", bdones[0:64, 0:64], 1.0)
    P.memset("dve", bdones[64:128, 64:128], 1.0)
    allones = sb("allones", [128, 128], F32)
    P.memset("dve", allones, 1.0)
    rmask_p = sb("rmask_p", [128, TB], BF16)
    P.memset("dve", rmask_p, 1.0)
    P.memset("dve", rmask_p.re("p (n c) -> p n c", c=64)[:, :, 0:1], 0.0)
    rmask_s = sb("rmask_s", [128, NS4], BF16)
    P.memset("dve", rmask_s, 1.0)
    P.memset("dve", rmask_s.re("p (n c) -> p n c", c=4)[:, :, 0:1], 0.0)
    for l in range(L):
        P.memset("dve", Hm[l], 0.0)
        P.memset("dve", Sg[l], 0.0)
        P.memset("dve", shp[l], 0.0)
        P.memset("dve", cst[l], 0.0)

    NSLOT = 3
    wslot = [sb("wslot%d" % i, [128, KC, 512], BF16) for i in range(NSLOT)]
    wctr = [0]

    NGRP = 28 * L
    wscr = nc.dram_tensor("wscr", [NGRP, 128, KC * 512], BF16, kind="Internal").ap()
    gctr = [0]
    first_blk = [True]

    def wload(src_ap, nk, ncol):
        s = wslot[wctr[0] % NSLOT]
        wctr[0] += 1
        gid = gctr[0]
        gctr[0] += 1
        assert gid < NGRP
        dst = s[:, 0:nk, 0:ncol]
        scr = Tile(wscr[gid], "wscr.%d" % gid)
        flat = s.re("p k c -> p (k c)")
        if first_blk[0]:
            P.dma("pool", dst, src_ap)
            P.dma("sp", scr, flat)
        else:
            P.dma("pool", flat, scr)
        return dst

    pst = st.enter_context(nc.psum_tensor("psum", [128, 8, 512], F32))
    psap = pst.ap()
    PSB = [Tile(psap[:, i], "ps%d" % i) for i in range(8)]

    def ps_multi(b0, nb):
        return Tile(psap[:, b0:b0 + nb].rearrange("p b c -> p (b c)"), tuple("ps%d" % i for i in range(b0, b0 + nb)))

    def ps_bf(bank):
        return Tile(psap[:, bank].bitcast(BF16), "ps%d" % bank)

    ARENA_F32 = 0
    mixer_defs = []
    ffn_defs = []

    NT_ = NTMAX
    PA_W = max(TB + 1, NSQ * (DEC_SEQ + 1))
    GT_W = max(TB + 2, NSQ * (DEC_SEQ + 2))
    m_off = [0]

    def m_alloc(words):
        o = m_off[0]
        m_off[0] += (words + 7) // 8 * 8
        return o

    f_off = [0]

    def f_alloc(words):
        o = f_off[0]
        f_off[0] += (words + 7) // 8 * 8
        return o

    layout_m = {}
    layout_m["pa"] = m_alloc(14 * PA_W)
    NTMP = 9
    for i in range(NTMP):
        layout_m["t%d" % i] = m_alloc(NT_)
    layout_m["lor"] = m_alloc(NT_ // 2)
    layout_m["sgl"] = m_alloc(NT_ // 2)
    layout_m["vb"] = m_alloc(4 * NT_ // 2)
    layout_m["vdb"] = m_alloc(NT_ // 2)
    layout_m["osb"] = m_alloc(4 * NT_)
    layout_m["bon"] = m_alloc(4 * NT_ // 2)
    layout_m["gst"] = m_alloc(4 * NT_ // 2)
    layout_m["AK"] = m_alloc(4 * NT_ * 2 // 2)
    layout_m["BR"] = m_alloc(4 * NT_ * 2 // 2)
    layout_m["PC"] = m_alloc(4 * NCHMAX)
    for nm, w in (("XA", 8 * 128 // 2), ("XB", 8 * 128 // 2), ("YA", 8 * 64 // 2), ("YB", 8 * 64 // 2),
                  ("MkT", 8 * 64 // 2), ("QT", 2 * 8 * 64 // 2), ("Wsb", 8 * 128 // 2), ("Upad", 8 * 128 // 2),
                  ("Upair", 4 * 128 // 2), ("Vpad", 8 * 128 // 2), ("Vpair", 4 * 128 // 2), ("AKt", 2 * 4 * 128 // 2),
                  ("Hbd", 4 * 128 // 2), ("sttmp", 8 * 64), ("Sgb", 2 * 128 // 2), ("ATsb", 4 * 64 // 2),
                  ("Vt", 4 * 128 // 2), ("kdt", 2 * 128 // 2), ("stout", 4 * 64)):
        layout_m[nm] = m_alloc(w)
    layout_f = {}
    layout_f["zbuf"] = f_alloc(NTILE_MAX * D)
    layout_f["lng"] = f_alloc(D)
    layout_f["lnb"] = f_alloc(D)
    layout_f["hT"] = f_alloc(FC * NT_ // 2)
    layout_f["gt0"] = f_alloc(GT_W)
    layout_f["gt1"] = f_alloc(GT_W)
    layout_f["acc0"] = f_alloc(NT_)
    layout_f["acc1"] = f_alloc(NT_)
    layout_f["xb"] = f_alloc(D // 2)
    layout_f["xb1"] = f_alloc(D // 2)
    layout_f["stat"] = f_alloc(32 * NTILE_MAX)
    layout_f["cpast"] = f_alloc(FC * NSQ * 2)
    layout_f["cout"] = f_alloc(FC * NSQ * 2)
    layout_f["ctok"] = f_alloc(DFF)
    ARENA_F32 = max(m_off[0], f_off[0])
    arena_t = st.enter_context(nc.sbuf_tensor("arena", [128, ARENA_F32], F32))
    arena = arena_t.ap()
    MKEYS = tuple("m." + k for k in layout_m)

    def mtile(name, words, dt, shape_str=None, **kw):
        o = layout_m[name]
        ap = arena[:, o:o + words]
        if dt == BF16:
            ap = ap.bitcast(BF16)
        if shape_str:
            ap = ap.rearrange(shape_str, **kw)
        return Tile(ap, "m." + name)

    def ftile(name, words, dt, shape_str=None, **kw):
        o = layout_f[name]
        ap = arena[:, o:o + words]
        if dt == BF16:
            ap = ap.bitcast(BF16)
        if shape_str:
            ap = ap.rearrange(shape_str, **kw)
        return Tile(ap, "f." + name)

    def dump(name, tile, shape):
        if not debug:
            return
        d = dout("dbg_" + name, shape)
        dbg[name] = shape
        P.dma("sp", d, tile)

    def load_x_block(kind, b, NT):
        ntile = (NT + 127) // 128
        for t in range(ntile):
            rows = min(128, NT - t * 128)
            src = x_p[b * TB + t * 128: b * TB + t * 128 + rows, :] if kind == "p" else x_s[t * 128: t * 128 + rows, :]
            P.dma("sp", xres[t][0:rows], src)

    def make_xT(NT, which_ps=(6, 7)):
        ntile = (NT + 127) // 128
        for t in range(ntile):
            rows = min(128, NT - t * 128)
            xb = ftile("xb" if t % 2 == 0 else "xb1", D // 2, BF16)
            P.copy("act", xb[0:rows], xres[t][0:rows])
            pb = ps_bf(which_ps[t % 2])
            for k in range(KC):
                P.transpose(pb[:, k * 128:k * 128 + rows], xb[0:rows, k * 128:(k + 1) * 128], identb[0:rows, 0:rows])
            P.copy("dve", xT[:, :, t * 128:t * 128 + rows],
                   pb[:, 0:KC * 128].re("p (k c) -> p k c", c=128)[:, :, 0:rows])

    def layer_norm_tile(t, rows, zt, g_dram, b_dram, lng, lnb):
        st_all = ftile("stat", 32 * NTILE_MAX, F32)
        stat = Tile(st_all.ap[:, 32 * t:32 * (t + 1)], "f.stat.%d" % t)
        s6 = stat[:, 0:12].re("p (a b) -> p a b", b=6)
        for hf in range(2):
            za, sa = zt[0:rows, hf * 512:(hf + 1) * 512].ap, s6[0:rows, hf].ap
            P.add("dve", (lambda o, i: (lambda e: e.bn_stats(o, i)))(sa, za), _keys([zt]), _keys([stat]))
        yield
        mv = stat[:, 12:14]
        P.add("dve", (lambda o, i: (lambda e: e.bn_aggr(o, i)))(mv[0:rows].ap, stat[0:rows, 0:12].ap),
              _keys([stat]), _keys([stat]))
        yield
        rs = stat[:, 14:15]
        P.act(rs[0:rows], mv[0:rows, 1:2], AF.Sqrt, bias=LN_EPS, scale=1.0)
        yield
        P.recip(rs[0:rows], rs[0:rows])
        yield
        nb = stat[:, 15:16]
        P.ts("dve", nb[0:rows], mv[0:rows, 0:1], rs[0:rows], ALU.mult, -1.0, ALU.mult)
        yield
        P.act(zt[0:rows], zt[0:rows], AF.Identity, bias=nb[0:rows], scale=rs[0:rows])
        yield
        P.tt("dve", zt[0:rows], zt[0:rows], lng[0:rows], ALU.mult)
        yield
        P.tt("dve", xres[t][0:rows], zt[0:rows], lnb[0:rows], ALU.add)
        yield

    def proj_token_major(NT, lhs_chunks, nk, w_dram_kview, g_dram, b_dram):
        ntile = (NT + 127) // 128
        zb_all = ftile("zbuf", NTILE_MAX * D, F32, "p (t d) -> p t d", d=D)
        zb = [Tile(zb_all.ap[:, t], "f.zbuf.%d" % t) for t in range(NTILE_MAX)]
        lng = ftile("lng", D, F32)
        lnb = ftile("lnb", D, F32)
        P.dma("sp", lng, g_dram.partition_broadcast(128))
        P.dma("sp", lnb, b_dram.partition_broadcast(128))
        for hf in range(2):
            k0 = 0
            while k0 < nk:
                nk_ = min(KC, nk - k0)
                ws = wload(w_dram_kview(k0, nk_, hf), nk_, 512)
                for kk in range(nk_):
                    k = k0 + kk
                    for t in range(ntile):
                        rows = min(128, NT - t * 128)
                        P.mm(PSB[t][0:rows], lhs_chunks(k)[:, t * 128:t * 128 + rows], ws[:, kk, :],
                             start=(k == 0), stop=(k == nk - 1))
                k0 += nk_
            for t in range(ntile):
                rows = min(128, NT - t * 128)
                P.stt(zb[t][0:rows, hf * 512:(hf + 1) * 512], xres[t][0:rows, hf * 512:(hf + 1) * 512], ALPHA,
                      PSB[t][0:rows], ALU.mult, ALU.add)
        _round_robin([layer_norm_tile(t, min(128, NT - t * 128), zb[t], g_dram, b_dram, lng, lnb)
                      for t in range(ntile)], lag=1)

    pctr = [0]

    def proj_feature_major(NT, w_dram, col0, ncols, consume):
        wv = w_dram.rearrange("(k p) c -> p k c", p=128)
        c = 0
        j = 0
        while c < ncols:
            wcols = min(512, ncols - c)
            ws = wload(wv[:, :, col0 + c: col0 + c + wcols], KC, wcols)
            cc = 0
            while cc < wcols:
                m = min(128, wcols - cc)
                pt = PSB[pctr[0] % 4]
                pctr[0] += 1
                for k in range(KC):
                    P.mm(pt[0:m, 0:NT], ws[:, k, cc:cc + m], xT[:, k, 0:NT], start=(k == 0), stop=(k == KC - 1))
                consume(j, pt, m)
                j += 1
                cc += m
            c += wcols

    blocks = [("p", b) for b in range(SEQ // TB)] + [("s", 0)]
    try:
      stage(1)
      for (kind, b) in blocks:
          is_p = kind == "p"
          NT = TB if is_p else NS4
          C = 64 if is_p else DEC_SEQ
          nch = NT // C
          nseq = 1 if is_p else NSQ
          Tq = TB if is_p else DEC_SEQ
          nlev = int(round(math.log2(C))) - 1
          last_p = is_p and b == SEQ // TB - 1
          rmask = rmask_p if is_p else rmask_s

          gctr[0] = 0
          first_blk[0] = (kind, b) == blocks[0]
          load_x_block(kind, b, NT)
          stage(2)
          make_xT(NT)
          stage(3)
          P.barrier()

          for l in range(L):
              pa_ap = mtile("pa", 14 * PA_W, F32, "p (j w) -> p j w", w=PA_W).ap
              PAKEYS = tuple("m.pa.%d" % j for j in range(14))

              class _PA:
                  def __getitem__(self, idx):
                      j = idx[1]
                      if isinstance(j, int):
                          return Tile(pa_ap[idx], PAKEYS[j])
                      return Tile(pa_ap[idx], PAKEYS)
              pa = _PA()

              def pa_cur(j):
                  return pa[:, j, 0:nseq * (Tq + 1)].re("p (s t) -> p s t", t=Tq + 1)[:, :, 1:]

              def pa_prev(j):
                  return pa[:, j, 0:nseq * (Tq + 1)].re("p (s t) -> p s t", t=Tq + 1)[:, :, 0:Tq]

              def pa_cv(j):
                  if is_p:
                      return pa[:, j, 1:1 + NT].re("p (n c) -> p n c", c=C)
                  return pa_cur(j)

              def pa_flat(j):
                  return pa[:, j, 0:NT]

              tmp = [mtile("t%d" % i, NT_, F32)[:, 0:NT] for i in range(NTMP)]

              def cv(t):
                  return t.re("p (n c) -> p n c", c=C)

              def sv(t):
                  return t.re("p (s t) -> p s t", t=Tq)

              lor = mtile("lor", NT_ // 2, BF16)[:, 0:NT]
              sgl = mtile("sgl", NT_ // 2, BF16)[:, 0:NT]
              vb = mtile("vb", 4 * NT_ // 2, BF16, "p (c n) -> p c n", c=4)[:, :, 0:NT]
              vdb = mtile("vdb", NT_ // 2, BF16)[:, 0:NT]
              osb = mtile("osb", 4 * NT_, F32, "p (c n) -> p c n", c=4)[:, :, 0:NT]
              bon = mtile("bon", 4 * NT_ // 2, BF16, "p (c n) -> p c n", c=4)[:, :, 0:NT]
              gst = mtile("gst", 4 * NT_ // 2, BF16, "p (c n) -> p c n", c=4)[:, :, 0:NT]
              AK = mtile("AK", 4 * NT_, BF16, "p (c n) -> p c n", c=4)[:, :, 0:2 * NT].re("p c (n k s) -> p c n k s", k=2, s=C)
              BR = mtile("BR", 4 * NT_, BF16, "p (c n) -> p c n", c=4)[:, :, 0:2 * NT].re("p c (n k s) -> p c n k s", k=2, s=C)
              PC = mtile("PC", 4 * NCHMAX, F32, "p (c n) -> p c n", c=4)[:, :, 0:nch]

              def consume_pa(j, pt, m):
                  P.copy("act", pa_cur(j), pt[:, 0:NT].re("p (s t) -> p s t", t=Tq))
              proj_feature_major(NT, w_in[l], 0, SHIFT_W, consume_pa)
              stage(4)
              if is_p:
                  P.copy("dve", pa[:, :, 0], shp[l])
                  P.copy("dve", shp[l], pa[:, :, Tq])
                  if last_p:
                      P.dma("sp", o_shift_p[l].rearrange("(j p) -> p j", p=128), shp[l])
              else:
                  shs = mtile("sttmp", 8 * 64, F32)[:, 0:14 * NSQ].re("p (j s) -> p j s", s=NSQ)
                  fast_sh = NT_ >= 448 and NSQ <= 32
                  if fast_sh:
                      o2 = layout_m["t2"]
                      stok = Tile(arena[:, o2:o2 + SHIFT_W], ("m.t2", "m.t3", "m.t4", "m.t5"))
                      P.dma("sp", stok[0:NSQ], st_shift[l])
                      pin = PSB[6]
                      for j in range(14):
                          P.mm(pin[:, j * NSQ:(j + 1) * NSQ], stok[0:NSQ, j * 128:(j + 1) * 128], identf[0:NSQ, 0:NSQ])
                      P.copy("act", pa[:, :, 0:nseq * (Tq + 1)].re("p j (s t) -> p j s t", t=Tq + 1)[:, :, :, 0],
                             pin[:, 0:14 * NSQ].re("p (j s) -> p j s", s=NSQ))
                  else:
                      for j in range(14):
                          P.dma("sp", pa[:, j, 0:nseq * (Tq + 1)].re("p (s t) -> p s t", t=Tq + 1)[:, :, 0],
                                st_shift[l, :, j * 128:(j + 1) * 128].rearrange("s p -> p s"))
                  for j in range(14):
                      P.copy("dve", shs[:, j], pa[:, j, 0:nseq * (Tq + 1)].re("p (s t) -> p s t", t=Tq + 1)[:, :, Tq])
                  if fast_sh:
                      for j in range(14):
                          bk, cc = divmod(j, 4)
                          P.mm(PSB[4 + bk][0:NSQ, cc * 128:(cc + 1) * 128], shs[:, j, :], identf)
                      for bk in range(4):
                          w_ = min(512, SHIFT_W - bk * 512)
                          P.copy("act", stok[0:NSQ, bk * 512:bk * 512 + w_], PSB[4 + bk][0:NSQ, 0:w_])
                      P.dma("sp", o_shift_s[l], stok[0:NSQ])
                  else:
                      for j in range(14):
                          P.dma("sp", o_shift_s[l, :, j * 128:(j + 1) * 128].rearrange("s p -> p s"), shs[:, j])
              stage(5)
              for j in range(14):
                  tw_ = sv(tmp[j % 2])
                  P.act(tw_, pa_prev(j), AF.Copy, scale=mu_t[:, l, j:j + 1])
                  P.stt(pa_cur(j), pa_cur(j), omu[:, l, j:j + 1], tw_, ALU.mult, ALU.add)
              stage(6)
              P.act(cv(lor)[0:64], pa_cv(12)[0:64], AF.Tanh)
              P.copy("act", cv(lor)[64:128], pa_cv(12)[64:128])
              P.act(cv(sgl), pa_cv(13), AF.Sigmoid)
              if l == 0:
                  for c in range(4):
                      P.copy("act", cv(vfirst[c][:, 0:NT]), pa_cv(8 + c))
              else:
                  for c in range(4):
                      P.copy("act", cv(vb[:, c]), pa_cv(8 + c))
                  pt = PSB[4]
                  for c in range(4):
                      P.mm(pt[0:32, 0:NT], vdn[:, l - 1, c, :], vb[:, c], start=(c == 0), stop=(c == 3))
                  P.copy("act", vdb[0:32], pt[0:32, 0:NT])

              stage(7)
              T_s1, T_cs, T_eP, T_eM, T_a, T_kk, T_km, T_1, T_2 = tmp
              T_csx = T_eX = T_s1
              T_kn = T_kk
              two_chain = NT_ >= 512

              def host(names):
                  o_ = layout_m[names[0]]
                  return Tile(arena[:, o_:o_ + NT_], tuple("m." + n_ for n_ in names))[:, 0:NT]
              if two_chain:
                  tmpB = [host(["osb"]), host(["XA"]), host(["XB"]), host(["Wsb"]), host(["QT"]), host(["AKt"]),
                          host(["Upad"]), host(["Vpad"]), host(["YA", "YB"])]
              else:
                  tmpB = tmp

              def pair_gen(c, TT, BK):
                  t_s1, t_cs, t_eP, t_eM, t_a, t_kk, t_km, t_1, t_2 = TT
                  t_csx = t_eX = t_s1
                  t_kn = t_kk
                  cs_ = slice(c * 128, (c + 1) * 128)
                  pw, pa_, pg, pvv = PSB[BK], PSB[BK + 1], PSB[BK + 2], PSB[BK + 3]
                  P.mm(pw[:, 0:NT], wa_up[0:64, l, cs_], lor[0:64])
                  P.mm(pa_[:, 0:NT], wa_up[64:128, l, cs_], lor[64:128])
                  P.mm(pg[:, 0:NT], g_up[:, l, cs_], sgl)
                  if l > 0:
                      P.mm(pvv[:, 0:NT], vup[0:32, l - 1, cs_], vdb[0:32])
                  yield
                  P.act(t_s1, pw[:, 0:NT], AF.Sigmoid, bias=pv["w0"][:, l, c:c + 1], scale=1.0)
                  P.act(t_a, pa_[:, 0:NT], AF.Sigmoid, bias=pv["a0"][:, l, c:c + 1], scale=1.0)
                  if l > 0:
                      P.act(t_1, pvv[:, 0:NT], AF.Sigmoid, bias=pv["v0"][:, l - 1, c:c + 1], scale=1.0)
                  P.copy("act", gst[:, c], pg[:, 0:NT])
                  yield
                  P.scan(t_cs, rmask[:, 0:NT], t_s1)
                  yield
                  P.tt("dve", t_csx, t_cs, t_s1, ALU.subtract)
                  P.act(t_eP, t_cs, AF.Exp, scale=-KDEC)
                  P.act(t_eM, t_cs, AF.Exp, scale=KDEC)
                  yield
                  P.act(t_eX, t_csx, AF.Exp, scale=-KDEC)
                  P.copy("act", PC[:, c], cv(t_eP)[:, :, C - 1])
                  xr, xk, xv = pa_cv(c), pa_cv(4 + c), pa_cv(8 + c)
                  if l > 0:
                      P.tt("dve", cv(t_2), cv(vfirst[c][:, 0:NT]), xv, ALU.subtract)
                      yield
                      P.tt("dve", t_2, t_2, t_1, ALU.mult)
                      yield
                      P.tt("dve", xv, xv, cv(t_2), ALU.add)
                      yield
                  P.copy("act", cv(vb[:, c]), xv)
                  P.act(cv(t_kk), xk, AF.Copy, scale=pv["kk"][:, l, c:c + 1])
                  yield
                  P.act(t_1, t_kk, AF.Square)
                  P.mm(pw[:, 0:NT], bdones, t_1)
                  yield
                  P.ts("dve", t_1, pw[:, 0:NT], 1e-24, ALU.max)
                  P.act(t_1, t_1, AF.Sqrt)
                  yield
                  P.recip(t_1, t_1)
                  yield
                  P.tt("dve", t_kn, t_kk, t_1, ALU.mult)
                  yield
                  P.act(t_2, t_a, AF.Identity, bias=omka[:, l, c:c + 1], scale=pv["ka"][:, l, c:c + 1])
                  yield
                  P.tt("dve", cv(t_km), xk, cv(t_2), ALU.mult)
                  yield
                  P.tt("dve", t_1, t_a, t_kn, ALU.mult)
                  yield
                  P.tt("dve", AK[:, c, :, 0, :], cv(t_1), cv(t_eM), ALU.mult)
                  yield
                  P.tt("dve", AK[:, c, :, 1, :], cv(t_km), cv(t_eM), ALU.mult)
                  yield
                  P.stt(BR[:, c, :, 0, :], cv(t_kn), -1.0, cv(t_eX), ALU.mult, ALU.mult)
                  yield
                  P.tt("dve", BR[:, c, :, 1, :], xr, cv(t_eP), ALU.mult)
                  yield
                  P.tt("dve", cv(t_2), xr, cv(t_km), ALU.mult)
                  yield
                  P.act(t_2, t_2, AF.Copy, scale=pv["rk"][:, l, c:c + 1])
                  P.mm(pa_[:, 0:NT], bdones, t_2)
                  yield
                  P.tt("dve", cv(bon[:, c]), cv(pa_[:, 0:NT]), xv, ALU.mult)
                  yield

              for c0 in (0, 2):
                  gens = [pair_gen(c0, tmp, 0), pair_gen(c0 + 1, tmpB, 4)]
                  live = [True, True]
                  started = 0
                  step = 0
                  while any(live):
                      for gi in range(2):
                          if gi == 1 and step < 1:
                              continue
                          if live[gi]:
                              try:
                                  next(gens[gi])
                              except StopIteration:
                                  live[gi] = False
                      step += 1

              stage(8)
              XAB = [mtile(n, 8 * 128 // 2, BF16, "p (h k s) -> p h k s", h=8, k=2) for n in ("XA", "XB")]
              YAB = [mtile(n, 8 * 64 // 2, BF16, "p (h s) -> p h s", h=8) for n in ("YA", "YB")]
              MkT = mtile("MkT", 8 * 64 // 2, BF16, "p (h s) -> p h s", h=8)
              QT = mtile("QT", 2 * 8 * 64 // 2, BF16, "p (k h s) -> p k h s", k=2, h=8)
              Wsb = mtile("Wsb", 8 * 128 // 2, BF16, "p (h s) -> p h s", h=8)
              Upad = mtile("Upad", 8 * 128 // 2, BF16, "p (h s) -> p h s", h=8)
              Upair = mtile("Upair", 4 * 128 // 2, BF16, "p (h s) -> p h s", h=4)
              Vpad = mtile("Vpad", 8 * 128 // 2, BF16, "p (h s) -> p h s", h=8)
              Vpair = mtile("Vpair", 4 * 128 // 2, BF16, "p (h s) -> p h s", h=4)
              AKt = mtile("AKt", 2 * 4 * 128 // 2, BF16, "p (k c s) -> p k c s", k=2, c=4)
              Hbd = mtile("Hbd", 4 * 128 // 2, BF16, "p (c s) -> p c s", c=4)
              P.memset("dve", Vpad, 0.0)
              P.memset("dve", Upad, 0.0)
              if is_p:
                  Hcur = Hm[l]
              else:
                  Hs = mtile("sttmp", 8 * 64, F32, "p (c s) -> p c s", c=4)
                  Hcur = Hs
                  P.memset("dve", Hs, 0.0)
              stout = mtile("stout", 4 * 64, F32, "p (c s) -> p c s", c=4)

              def padview(t8):
                  a = t8.ap
                  parts = a.ap
                  return Tile(bass.AP(tensor=a.tensor, offset=a.offset, ap=[list(parts[0]), [256, 4], [192, 2], [1, 64]]),
                              t8.keys)

              dbl = (not is_p) and NT_ >= 320 and NT <= 64
              if dbl:
                  def _tail(i_, key):
                      o_ = layout_m["t%d" % i_] + 64
                      return Tile(arena[:, o_:o_ + 256].rearrange("p (c s) -> p c s", c=4), key)
                  sin_sets = [[_tail(6, "m.sinA0"), _tail(7, "m.sinA1")], [_tail(8, "m.sinB0"), _tail(0, "m.sinB1")]]
                  stouts = [stout, _tail(1, "m.stoutB")]

                  def load_sin(n_):
                      for hf in range(2):
                          P.dma("sp", sin_sets[n_ % 2][hf][0:64], st_rwkv[l, n_, 4 * hf:4 * hf + 4].rearrange("h v k -> v h k"))
                  load_sin(0)

              def store_state(Hc, dst_h, stout=stout):
                  pt = PSB[7]
                  for c in range(4):
                      P.transpose(pt[:, c * 128:(c + 1) * 128], Hc[:, c, :], identf)
                  ptv = pt.re("p (c s) -> p c s", c=4)
                  for p_ in range(2):
                      ph = slice(p_ * 64, p_ * 64 + 64)
                      P.copy("dve", stout[ph], ptv[ph, :, p_ * 64:p_ * 64 + 64])
                      P.dma("sp", dst_h.rearrange("(c p) v k -> p v c k", p=2)[p_], stout[ph])

              pipe = PA_W >= 512
              X0, X1 = XAB
              Y0, Y1 = YAB
              smk = smask[0:C, 0:C]
              imk = imask[0:C, 0:C]
              smt = smaskT[0:C, 0:C]
              idn = identb[0:C, 0:C]
              bc4 = lambda m_: Tile(m_.ap.unsqueeze(1).broadcast_to([C, 4, C]), m_.keys)

              def parow(j_, words, pattern, **kw):
                  return Tile(pa_ap[:, j_, 0:words].bitcast(BF16).rearrange(pattern, **kw), PAKEYS[j_])
              if pipe:
                  sets = [dict(Vpair=Vpair, Vpad=Vpad, AKt=AKt, MkT=MkT, QT=QT,
                               Yfin=parow(4, 256, "p (h s) -> p h s", h=8)),
                          dict(Vpair=parow(3, 256, "p (h s) -> p h s", h=4),
                               Vpad=parow(0, 512, "p (h s) -> p h s", h=8),
                               AKt=parow(1, 512, "p (k c s) -> p k c s", k=2, c=4),
                               MkT=parow(5, 256, "p (h s) -> p h s", h=8),
                               QT=parow(2, 512, "p (k h s) -> p k h s", k=2, h=8),
                               Yfin=parow(6, 256, "p (h s) -> p h s", h=8))]
                  P.memset("dve", sets[1]["Vpad"], 0.0)
              else:
                  sets = [dict(Vpair=Vpair, Vpad=Vpad, AKt=AKt, MkT=MkT, QT=QT, Yfin=None)] * 2

              def P1_stages(n, S):
                  Vpair_, Vpad_, AKt_, MkT_, QT_ = S["Vpair"], S["Vpad"], S["AKt"], S["MkT"], S["QT"]

                  def stA():
                      pvt = PSB[3]
                      for c in range(4):
                          P.mm(pvt[0:C, c * 128:(c + 1) * 128], vb[:, c, n * C:(n + 1) * C], identb)
                      pvv4 = pvt[0:C, 0:512].re("p (c s) -> p c s", c=4)
                      with P.indep():
                          P.copy("act", Vpair_[0:C], pvv4)
                          for q_ in range(2):
                              P.copy("act", Vpad_[0:C].re("p (c q) s -> p c q s", q=2)[:, :, q_, q_ * 64:(q_ + 1) * 64],
                                     pvv4[:, :, q_ * 64:(q_ + 1) * 64])
                      for k_ in range(2):
                          for c in range(4):
                              P.mm(PSB[4 + k_][0:C, c * 128:(c + 1) * 128], AK[:, c, n, k_, :], identb)
                      for k_ in range(2):
                          P.copy("act" if k_ == 0 else "dve", AKt_[0:C, k_],
                                 PSB[4 + k_][0:C, 0:512].re("p (c s) -> p c s", c=4))

                  def stB():
                      Gq = [[PSB[2 * q_ + k_][0:C, 0:4 * 2 * C].re("p (c s) -> p c s", c=4) for k_ in range(2)]
                            for q_ in range(2)]
                      PXq = [PSB[4 + q_][0:C, 0:4 * C].re("p (c s) -> p c s", c=4) for q_ in range(2)]
                      for h in range(8):
                          c, p_ = h // 2, h % 2
                          ph = slice(p_ * 64, p_ * 64 + 64)
                          for k_ in range(2):
                              P.mm(Gq[p_][k_][:, c, :], AK[ph, c, n, k_, :], BR[ph, c, n, :, :].re("p k s -> p (k s)"))
                          P.mm(PXq[p_][:, c, :], BR[ph, c, n, 0, :], AK[ph, c, n, 0, :])
                      X0q = X0[0:C].re("p (c q) k s -> p c q k s", q=2)
                      MkTq = MkT_[0:C].re("p (c q) s -> p c q s", q=2)
                      with P.indep():
                          for q_ in range(2):
                              P.tt("dve", X0q[:, :, q_, 1, 0:C], Gq[q_][0][:, :, 0:C], bc4(smk), ALU.mult)
                              P.tt("dve", MkTq[:, :, q_, 0:C], Gq[q_][1][:, :, 0:C], bc4(smk), ALU.mult)
                              for k_ in range(2):
                                  P.tt("dve", QT_[0:C, k_].re("p (c q) s -> p c q s", q=2)[:, :, q_, 0:C],
                                       Gq[q_][k_][:, :, C:2 * C], bc4(imk), ALU.mult)
                              P.tt("dve", X0q[:, :, q_, 0, 0:C], PXq[q_], bc4(smt), ALU.mult)
                      P.tt("dve", Y0[0:C, :, 0:C], X0[0:C, :, 1, 0:C],
                           Tile(idn.ap.unsqueeze(1).broadcast_to([C, 8, C]), idn.keys), ALU.add)

                  def mk_level(lev):
                      def stC():
                          lastlev = lev == nlev - 1
                          Xc, Xn = (X0, X1) if lev % 2 == 0 else (X1, X0)
                          Yc, Yn = (Y0, Y1) if lev % 2 == 0 else (Y1, Y0)
                          if lastlev and S["Yfin"] is not None:
                              Yn = S["Yfin"]
                          PL = ps_multi(0, 2)[0:C, 0:8 * 2 * C].re("p (h k s) -> p h k s", h=8, k=2)
                          for h in range(8):
                              P.mm(PL[:, h, 0, :], Xc[0:C, h, 1, 0:C], Xc[0:C, h, 0, 0:C])
                              if not lastlev:
                                  P.mm(PL[:, h, 1, :], Xc[0:C, h, 0, 0:C], Xc[0:C, h, 1, 0:C])
                          for hh in range(2):
                              hs = slice(4 * hh, 4 * hh + 4)
                              ce = "act" if hh == 0 else "dve"
                              if lastlev:
                                  P.copy(ce, Xn[0:C, hs, 0, 0:C], PL[:, hs, 0, :])
                              else:
                                  P.copy(ce, Xn[0:C, hs, :, 0:C], PL[:, hs])
                          PY = PSB[2][0:C, 0:8 * C].re("p (h s) -> p h s", h=8)
                          for h in range(8):
                              P.mm(PY[:, h, :], Xn[0:C, h, 0, 0:C], Yc[0:C, h, 0:C])
                          P.tt("dve", Yn[0:C, :, 0:C], PY, Yc[0:C, :, 0:C], ALU.add)
                      return stC
                  return [stA, stB] + [mk_level(lev) for lev in range(nlev)]

              def P2_stages(n, S):
                  Vpair_, Vpad_, AKt_, MkT_, QT_ = S["Vpair"], S["Vpad"], S["AKt"], S["MkT"], S["QT"]
                  Yf = S["Yfin"] if S["Yfin"] is not None else ((Y0, Y1)[nlev % 2])

                  def stD0():
                      if not is_p:
                          pt = PSB[7]
                          if dbl:
                              if n + 1 < nch:
                                  load_sin(n + 1)
                              for c in range(4):
                                  sh_ = sin_sets[n % 2][c // 2]
                                  P.mm(pt[:, c * 64:(c + 1) * 64],
                                       sh_[0:64, 2 * (c % 2):2 * (c % 2) + 2, :].re("p h k -> p (h k)"), identf[0:64, 0:64])
                          else:
                              sin = mtile("Wsb", 8 * 128 // 2, F32)[0:64, 0:512].re("p (h k) -> p h k", h=8)
                              P.dma("sp", sin, st_rwkv[l, n].rearrange("h v k -> v h k"))
                              for c in range(4):
                                  P.mm(pt[:, c * 64:(c + 1) * 64], sin[:, 2 * c:2 * c + 2, :].re("p h k -> p (h k)"),
                                       identf[0:64, 0:64])
                          ptv = pt[:, 0:256].re("p (c s) -> p c s", c=4)
                          for p_ in range(2):
                              ph = slice(p_ * 64, p_ * 64 + 64)
                              P.copy("dve", Hs[ph, :, p_ * 64:p_ * 64 + 64], ptv[ph])
                      P.copy("act", Hbd, Hcur)

                  def stD1():
                      PW = PSB[6][0:C, 0:512].re("p (c s) -> p c s", c=4)
                      for c in range(4):
                          P.mm(PW[:, c, :], BR[:, c, n, 0, :], Hbd[:, c, :], start=True, stop=False)
                          for p_ in range(2):
                              h = 2 * c + p_
                              P.mm(PW[:, c, :], MkT_[0:C, h, 0:C], Vpad_[0:C, h, :], start=False, stop=(p_ == 1))
                      P.copy("act", Wsb[0:C, 0:4], PW)

                  def stE():
                      PU = PSB[7][0:C, 0:512].re("p (c s) -> p c s", c=4)
                      for h in range(8):
                          c, p_ = h // 2, h % 2
                          vs = slice(p_ * 64, p_ * 64 + 64)
                          P.mm(PU[:, c, vs], Yf[0:C, h, 0:C], Wsb[0:C, c, vs])
                      with P.indep():
                          P.copy("act", Upair[0:C], PU)
                          for q_ in range(2):
                              P.copy("act", Upad[0:C].re("p (c q) s -> p c q s", q=2)[:, :, q_, q_ * 64:(q_ + 1) * 64],
                                     PU[:, :, q_ * 64:(q_ + 1) * 64])

                  def stF():
                      PO = PSB[6][:, 0:4 * C].re("p (c s) -> p c s", c=4)
                      for c in range(4):
                          P.mm(PO[:, c, :], Hbd[:, c, :], BR[:, c, n, 1, :], start=True, stop=False)
                          for p_ in range(2):
                              h = 2 * c + p_
                              P.mm(PO[:, c, :], Upad[0:C, h, :], QT_[0:C, 0, h, 0:C], start=False, stop=False)
                              P.mm(PO[:, c, :], Vpad_[0:C, h, :], QT_[0:C, 1, h, 0:C], start=False, stop=(p_ == 1))
                      P.copy("act", osb[:, :, n * C:(n + 1) * C], PO)

                  def stG():
                      PH = PSB[7].re("p (c s) -> p c s", c=4)
                      for c in range(4):
                          P.mm(PH[:, c, :], AKt_[0:C, 0, c, :], Upair[0:C, c, :], start=True, stop=False)
                          P.mm(PH[:, c, :], AKt_[0:C, 1, c, :], Vpair_[0:C, c, :], start=False, stop=True)
                      with P.indep():
                          for p_ in range(2):
                              ph = slice(p_ * 64, p_ * 64 + 64)
                              P.tt("dve", Hcur[ph, :, ph], Hcur[ph, :, ph], PH[ph, :, ph], ALU.add)
                      with P.indep():
                          for p_ in range(2):
                              ph = slice(p_ * 64, p_ * 64 + 64)
                              pcb = PC[ph, :, n:n + 1]
                              P.tt("dve", Hcur[ph, :, ph], Hcur[ph, :, ph],
                                   Tile(pcb.ap.broadcast_to([64, 4, 64]), pcb.keys), ALU.mult)
                      if not is_p:
                          store_state(Hs, o_rwkv_s[l, n], stouts[n % 2] if dbl else stout)
                  return [stD0, stD1, stE, stF, stG]

              for st_ in P1_stages(0, sets[0]):
                  st_()
              for n in range(nch):
                  p2 = P2_stages(n, sets[n % 2])
                  p1 = P1_stages(n + 1, sets[(n + 1) % 2]) if n + 1 < nch else []
                  seq = []
                  i1_, i2_ = 0, 0
                  while i1_ < len(p1) or i2_ < len(p2):
                      if i2_ < len(p2):
                          seq.append(p2[i2_]); i2_ += 1
                      if i1_ < len(p1):
                          seq.append(p1[i1_]); i1_ += 1
                  for st_ in seq:
                      st_()
              if last_p:
                  store_state(Hm[l], o_rwkv_p[l])

              stage(9)
              def run2(gens, lag=1):
                  live = [True] * len(gens)
                  step = 0
                  while any(live):
                      for gi in range(len(gens)):
                          if gi * lag > step or not live[gi]:
                              continue
                          try:
                              next(gens[gi])
                          except StopIteration:
                              live[gi] = False
                      step += 1

              def rpost_gen(c, t1, t2, ta, b0):
                  oc = osb[:, c]
                  p1, p2 = PSB[b0], PSB[b0 + 1]
                  P.act(t1, oc, AF.Square)
                  P.mm(p1[:, 0:NT], bdones, oc)
                  P.mm(p2[:, 0:NT], bdones, t1)
                  yield
                  P.act(t2, p1[:, 0:NT], AF.Copy, scale=1.0 / 64)
                  yield
                  P.act(ta, t2, AF.Square)
                  yield
                  P.stt(ta, p2[:, 0:NT], 1.0 / 64, ta, ALU.mult, ALU.subtract)
                  yield
                  P.act(ta, ta, AF.Sqrt, bias=GN_EPS, scale=1.0)
                  yield
                  P.recip(ta, ta)
                  yield
                  P.tt("dve", t2, oc, t2, ALU.subtract)
                  yield
                  P.tt("dve", t2, t2, ta, ALU.mult)
                  yield
                  P.act(t2, t2, AF.Identity, bias=pv["lb"][:, l, c:c + 1], scale=pv["lg"][:, l, c:c + 1])
                  yield
                  P.tt("dve", t2, t2, bon[:, c], ALU.add)
                  yield
                  P.tt("dve", mixT[c][:, 0:NT], t2, gst[:, c], ALU.mult)
                  yield
              for c0 in (0, 2):
                  run2([rpost_gen(c0, T_1, T_2, T_a, 0), rpost_gen(c0 + 1, T_kk, T_km, T_eM, 2)])

              stage(10)
              def consume_gla(j, pt, m):
                  P.copy("act", pa_flat(j)[0:m], pt[0:m, 0:NT])
              proj_feature_major(NT, w_in[l], SHIFT_W, IN_W - SHIFT_W, consume_gla)
              flo = lor
              P.copy("act", flo[0:16], pa_flat(12)[0:16])
              AKflat = mtile("AK", 4 * NT_, BF16)
              qtz = AKflat[:, 0:4 * NT_].re("p (c q n) -> p c q n", c=2, q=2)[:, :, :, 0:NT]
              ktb = AKflat[:, 4 * NT_:6 * NT_].re("p (c n) -> p c n", c=2)[:, :, 0:NT]
              P.memset("dve", AKflat[:, 0:4 * NT_], 0.0)
              kdb = mtile("BR", 4 * NT_, BF16, "p (c n) -> p c n", c=4)[:, 0:2, 0:NT]
              vgb = mtile("vb", 4 * NT_ // 2, BF16, "p (c n) -> p c n", c=4)[:, :, 0:NT]
              ogs = osb
              PCg = PC
              for c in range(2):
                  pz = PSB[c]
                  P.mm(pz[:, 0:NT], fup[0:16, l, c * 128:(c + 1) * 128], flo[0:16])
                  P.act(T_s1, pz[:, 0:NT], AF.Sigmoid, bias=fb_t[:, l, c:c + 1], scale=1.0)
                  P.act(T_s1, T_s1, AF.Ln)
                  P.scan(T_cs, rmask[:, 0:NT], T_s1)
                  P.act(T_eP, T_cs, AF.Exp, scale=1.0 / 16)
                  P.act(T_eM, T_cs, AF.Exp, scale=-1.0 / 16)
                  P.copy("act", PCg[:, c], cv(T_eP)[:, :, C - 1])
                  for q_ in range(2):
                      ph = slice(q_ * 64, q_ * 64 + 64)
                      P.stt(qtz[ph, c, q_], pa_flat(c)[ph], 0.125, T_eP[ph], ALU.mult, ALU.mult)
                  P.tt("dve", T_1, pa_flat(2 + c), T_eM, ALU.mult)
                  P.copy("act", ktb[:, c], T_1)
                  pcb = PCg[:, c, :]
                  P.tt("dve", cv(kdb[:, c]), cv(T_1), Tile(pcb.ap.unsqueeze(2).broadcast_to([128, nch, C]), pcb.keys), ALU.mult)
              for h in range(4):
                  P.copy("act", vgb[:, h], pa_flat(4 + h))
              Sgb = mtile("Sgb", 2 * 128 // 2, BF16, "p (c s) -> p c s", c=2)
              ATsb = mtile("ATsb", 4 * 64 // 2, BF16, "p (h s) -> p h s", h=4)
              Vt = mtile("Vt", 4 * 128 // 2, BF16, "p (h s) -> p h s", h=4)
              kdt = mtile("kdt", 2 * 128 // 2, BF16, "p (c s) -> p c s", c=2)
              pre_s = (not is_p) and PA_W >= 336 and NT_ >= 320 and NT <= 64 and nch <= 16
              if is_p:
                  Scur = Sg[l]
              elif pre_s:
                  osb_full = mtile("osb", 4 * NT_, F32, "p (c n) -> p c n", c=4).ap
                  Sbufs = []
                  for i_ in range(nch):
                      apb = pa_ap[:, i_, 80:336] if i_ < 14 else osb_full[:, i_ - 14, 64:320]
                      Sbufs.append(Tile(apb.rearrange("p (c s) -> p c s", c=2), "m.Sb%d" % i_))
                  for n in range(nch):
                      for c in range(2):
                          for p_ in range(2):
                              P.dma("sp", Sbufs[n][p_ * 64:(p_ + 1) * 64, c, :], st_gla[l, n, 2 * c + p_])
              else:
                  Scur = mtile("sttmp", 8 * 64, F32)[:, 0:256].re("p (c s) -> p c s", c=2)
              for n in range(nch):
                  if pre_s:
                      Scur = Sbufs[n]
                  elif not is_p:
                      for c in range(2):
                          for p_ in range(2):
                              P.dma("sp", Scur[p_ * 64:(p_ + 1) * 64, c, :], st_gla[l, n, 2 * c + p_])
                  P.copy("act", Sgb, Scur)
                  pvt = PSB[6]
                  for h in range(4):
                      P.mm(pvt[0:C, h * 128:(h + 1) * 128], vgb[:, h, n * C:(n + 1) * C], identb)
                  P.copy("act", Vt[0:C], pvt[0:C, 0:512].re("p (h s) -> p h s", h=4))
                  pkt = PSB[5]
                  for c in range(2):
                      P.mm(pkt[0:C, c * 128:(c + 1) * 128], kdb[:, c, n * C:(n + 1) * C], identb)
                  P.copy("act", kdt[0:C], pkt[0:C, 0:256].re("p (c s) -> p c s", c=2))
                  PA_ = PSB[0][0:C, 0:4 * C].re("p (h s) -> p h s", h=4)
                  for h in range(4):
                      c, p_ = h // 2, h % 2
                      ph = slice(p_ * 64, p_ * 64 + 64)
                      P.mm(PA_[:, h, :], ktb[:, c, n * C:(n + 1) * C], qtz[:, c, p_, n * C:(n + 1) * C])
                  imk = imask[0:C, 0:C]
                  P.tt("dve", ATsb[0:C, :, 0:C], PA_, Tile(imk.ap.unsqueeze(1).broadcast_to([C, 4, C]), imk.keys), ALU.mult)
                  PO = PSB[1][:, 0:4 * C].re("p (h s) -> p h s", h=4)
                  for h in range(4):
                      c, p_ = h // 2, h % 2
                      ph = slice(p_ * 64, p_ * 64 + 64)
                      P.mm(PO[:, h, :], Vt[0:C, h, :], ATsb[0:C, h, 0:C], start=True, stop=False)
                      P.mm(PO[:, h, :], Sgb[:, c, :], qtz[:, c, p_, n * C:(n + 1) * C], start=False, stop=True)
                  P.copy("act", ogs[:, :, n * C:(n + 1) * C], PO)
                  PS_ = PSB[2].re("p (h s) -> p h s", h=4)
                  for h in range(4):
                      c = h // 2
                      P.mm(PS_[:, h, :], kdt[0:C, c, :], Vt[0:C, h, :])
                  with P.indep():
                      for h in range(4):
                          c, p_ = h // 2, h % 2
                          ph = slice(p_ * 64, p_ * 64 + 64)
                          P.stt(Scur[ph, c, :], Scur[ph, c, :], PCg[ph, c, n:n + 1], PS_[ph, h, :], ALU.mult, ALU.add)
                  if not is_p:
                      for c in range(2):
                          for p_ in range(2):
                              P.dma("sp", o_gla_s[l, n, 2 * c + p_], Scur[p_ * 64:(p_ + 1) * 64, c, :])
              if last_p:
                  for c in range(2):
                      for p_ in range(2):
                          P.dma("sp", o_gla_p[l, 2 * c + p_], Sg[l][p_ * 64:(p_ + 1) * 64, c, :])
              def gpost_gen(h, t1, t2, ta, b0):
                  oc = ogs[:, h]
                  p1 = PSB[b0]
                  P.act(t1, oc, AF.Square)
                  P.mm(p1[:, 0:NT], allones, t1)
                  yield
                  P.act(t2, p1[:, 0:NT], AF.Sqrt, bias=RMS_EPS, scale=1.0 / 128)
                  yield
                  P.recip(t2, t2)
                  yield
                  P.tt("dve", t2, oc, t2, ALU.mult)
                  P.act(ta, pa_flat(8 + h), AF.Silu)
                  yield
                  P.stt(mixT[4 + h][:, 0:NT], t2, ng_t[:, l:l + 1], ta, ALU.mult, ALU.mult)
                  yield
              for h0 in (0, 2):
                  run2([gpost_gen(h0, T_1, T_2, T_a, 0), gpost_gen(h0 + 1, T_kk, T_km, T_eM, 1)])

              stage(11)
              P.barrier()
              wo_v = w_out[l].rearrange("(k p) c -> p k c", p=128)
              proj_token_major(NT, lambda k: mixT[k], KC,
                               lambda k0, nk_, hf: wo_v[:, k0:k0 + nk_, hf * 512:(hf + 1) * 512],
                               ln1_g[l], ln1_b[l])
              make_xT(NT)

              stage(12)
              hT = ftile("hT", FC * NT_ // 2, BF16, "p (j n) -> p j n", j=FC)[:, :, 0:NT]
              gts = [ftile("gt%d" % i, GT_W, F32)[:, 0:nseq * (Tq + 2)].re("p (s t) -> p s t", t=Tq + 2) for i in range(2)]
              accs = [ftile("acc%d" % i, NT_, F32)[:, 0:NT].re("p (s t) -> p s t", t=Tq) for i in range(2)]
              wu_v = w_up[l].rearrange("(k p) c -> p k c", p=128)
              gate_ps = {}
              if not is_p:
                  cpast = ftile("cpast", FC * NSQ * 2, F32, "p (j s r) -> p j s r", j=FC, r=2)
                  cout = ftile("cout", FC * NSQ * 2, F32, "p (j s r) -> p j s r", j=FC, r=2)
                  R2 = NSQ * 2
                  ctok = ftile("ctok", DFF, F32)
                  if R2 <= 32:
                      P.dma("sp", ctok[0:R2], st_conv[l].rearrange("s r w -> (s r) w"))
                      cpf = cpast.re("p j s r -> p (j s r)")
                      for j in range(FC):
                          bk, cc = divmod(j, 16)
                          P.mm(PSB[4 + bk][:, cc * R2:(cc + 1) * R2], ctok[0:R2, j * 128:(j + 1) * 128], identf[0:R2, 0:R2])
                      P.copy("act", cpf[:, 0:16 * R2], PSB[4][:, 0:16 * R2])
                      P.copy("act", cpf[:, 16 * R2:FC * R2], PSB[5][:, 0:(FC - 16) * R2])
                  else:
                      for j in range(FC):
                          for r_ in range(2):
                              P.dma("sp", cpast[:, j, :, r_], st_conv[l, :, r_, j * 128:(j + 1) * 128].rearrange("s p -> p s"))
              j0 = 0
              while j0 < FC:
                  nj = min(4, FC - j0)
                  wg = wload(wu_v[:, :, j0 * 128:(j0 + nj) * 128], KC, nj * 128)
                  wvv = wload(wu_v[:, :, DFF + j0 * 128: DFF + (j0 + nj) * 128], KC, nj * 128)
                  for jj in range(nj):
                      j = j0 + jj
                      pg_, pv_ = PSB[(2 * j) % 4], PSB[(2 * j + 1) % 4]
                      for k in range(KC):
                          P.mm(pg_[:, 0:NT], wg[:, k, jj * 128:(jj + 1) * 128], xT[:, k, 0:NT], start=(k == 0), stop=(k == KC - 1))
                      for k in range(KC):
                          P.mm(pv_[:, 0:NT], wvv[:, k, jj * 128:(jj + 1) * 128], xT[:, k, 0:NT], start=(k == 0), stop=(k == KC - 1))
                      gt = gts[j % 2]
                      acc = accs[j % 2]
                      P.copy("act", gt[:, :, 2:], pg_[:, 0:NT].re("p (s t) -> p s t", t=Tq))
                      if is_p:
                          P.copy("act", gt[:, 0, 0:2], cst[l][:, j, :])
                          P.copy("act", cst[l][:, j, :], gt[:, 0, Tq:Tq + 2])
                          if last_p:
                              P.dma("sp", o_conv_p[l, :, j * 128:(j + 1) * 128].rearrange("r p -> p r"), cst[l][:, j, :])
                      else:
                          P.copy("act", gt[:, :, 0:2], cpast[:, j])
                          P.copy("act", cout[:, j], gt[:, :, Tq:Tq + 2])
                      P.act(acc, gt[:, :, 2:], AF.Identity, bias=cb_t[:, l, j:j + 1], scale=cw_t[:, l, 2, j:j + 1])
                      P.stt(acc, gt[:, :, 1:Tq + 1], cw_t[:, l, 1, j:j + 1], acc, ALU.mult, ALU.add)
                      P.stt(acc, gt[:, :, 0:Tq], cw_t[:, l, 0, j:j + 1], acc, ALU.mult, ALU.add)
                      P.act(acc, acc, AF.Gelu)
                      P.tt("dve", hT[:, j].re("p (s t) -> p s t", t=Tq), acc, pv_[:, 0:NT].re("p (s t) -> p s t", t=Tq), ALU.mult)
                  j0 += nj
              if not is_p:
                  if R2 <= 32:
                      cof = cout.re("p j s r -> p j (s r)")
                      for j in range(FC):
                          bk, cc = divmod(j, 4)
                          P.mm(PSB[bk][0:R2, cc * 128:(cc + 1) * 128], cof[:, j, :], identf)
                      for bk in range((FC + 3) // 4):
                          w_ = min(512, DFF - bk * 512)
                          P.copy("act", ctok[0:R2, bk * 512:bk * 512 + w_], PSB[bk][0:R2, 0:w_])
                      P.dma("sp", o_conv_s[l].rearrange("s r w -> (s r) w"), ctok[0:R2])
                  else:
                      for j in range(FC):
                          for r_ in range(2):
                              P.dma("sp", o_conv_s[l, :, r_, j * 128:(j + 1) * 128].rearrange("s p -> p s"), cout[:, j, :, r_])

              stage(13)
              wd_v = w_down[l].rearrange("(k p) c -> p k c", p=128)
              proj_token_major(NT, lambda k: hT[:, k], FC,
                               lambda k0, nk_, hf: wd_v[:, k0:k0 + nk_, hf * 512:(hf + 1) * 512],
                               ln2_g[l], ln2_b[l])
              if l < L - 1:
                  make_xT(NT)
              P.barrier()

          ntile = (NT + 127) // 128
          for t in range(ntile):
              rows = min(128, NT - t * 128)
              dst = y_p[b * TB + t * 128: b * TB + t * 128 + rows, :] if is_p else y_s[t * 128:t * 128 + rows, :]
              P.dma("sp", dst, xres[t][0:rows])

    except _Stop:
        pass
    P.finish()
    P.emit(st)
    st.close()
    return nc, P, dbg


N_CORES = 8
_CACHE = {}


def _core_inputs(inp, i, NSQ):
    sl = slice(i * NSQ, (i + 1) * NSQ)
    L = inp["w_in"].shape[0]
    c = np.ascontiguousarray
    m = {
        "x_p": c(inp["x_prompt"][i]),
        "x_s": c(inp["x_sample"][sl].reshape(NSQ * DEC_SEQ, D)),
        "st_rwkv": c(inp["state_rwkv"][:, sl]),
        "st_shift": c(inp["state_shift"][:, sl, 0, :]),
        "st_gla": c(inp["state_gla"][:, sl]),
        "st_conv": c(inp["state_conv"][:, sl]),
        "r_k": c(inp["r_k"].reshape(L, AW)),
    }
    for k in ("w_in", "tok_mu", "w0", "w_lora_up", "a0", "a_lora_up", "g_lora_up", "k_k", "k_a", "lnx_g", "lnx_b",
              "vres_bias", "vres_down", "vres_up", "gla_f_up", "gla_f_bias", "gla_norm_g", "w_out", "ln1_g",
              "ln1_b", "w_up", "conv_w", "conv_b", "w_down", "ln2_g", "ln2_b"):
        m[k] = c(inp[k])
    return m


def run(inp, TB=512, debug=False, n_cores=N_CORES, trace=False, stop=99):
    inp = {k: np.asarray(v, dtype=np.float32) for k, v in inp.items()}
    L = inp["w_in"].shape[0]
    B, SEQ, _ = inp["x_prompt"].shape
    DB = inp["x_sample"].shape[0]
    assert B == n_cores
    NSQ = DB // n_cores
    TB = min(TB, SEQ)
    key = (L, SEQ, NSQ, TB, debug, stop)
    if key not in _CACHE:
        _CACHE[key] = build_program(L, SEQ, NSQ, TB, debug, stop)
    nc, P, dbg = _CACHE[key]
    in_maps = [_core_inputs(inp, i, NSQ) for i in range(n_cores)]
    res = run_bass_kernel_spmd(nc, in_maps, core_ids=list(range(n_cores)), **({"trace": True} if trace else {}))
    R = res.results
    cat = lambda name, axis: np.concatenate([r[name] for r in R], axis=axis)
    stk = lambda name: np.stack([r[name] for r in R], axis=1)
    outs = (
        np.stack([r["y_p"] for r in R], axis=0),
        cat("y_s", 0).reshape(DB, DEC_SEQ, D),
        stk("o_rwkv_p"),
        cat("o_rwkv_s", 1),
        stk("o_shift_p")[:, :, None, :],
        cat("o_shift_s", 1)[:, :, None, :],
        stk("o_gla_p"),
        cat("o_gla_s", 1),
        stk("o_conv_p"),
        cat("o_conv_s", 1),
    )
    outs = tuple(np.ascontiguousarray(o, dtype=np.float32) for o in outs)
    if debug:
        return outs, R, res
    return outs


def kernel(**inputs):
    return run(inputs)
```
